# Optimizing a Trainium2 kernel written in Bass

```python
import functools
import jax, jax.numpy as jnp
from jax import lax
import numpy as np

D_MODEL = 1024
BATCH = 4
SEQ = 4096
DEPTH = 2
DEC_BATCH = 32
DEC_SEQ = 1
PAST_LEN = 8192
PAGE_SIZE = 128

HEAD_DIM = 128
N_ATT_GROUPS = 3
HEADS_PER_GROUP = 4
WINDOWS = (128, 512, 2048)
DILATIONS = (1, 4, 16)
N_STEPS = 128
N_ATT_HEADS = N_ATT_GROUPS * HEADS_PER_GROUP
ATT_QKV = N_ATT_HEADS * HEAD_DIM
ATT_OUT = HEADS_PER_GROUP * HEAD_DIM
CHUNK = 128
SGU_WIDTH = 512
SGU_GROUPS = 8
SGU_GROUP_DIM = SGU_WIDTH // SGU_GROUPS
D_FF = -(-8 * D_MODEL // (3 * 256)) * 256
SPLITS = (ATT_QKV, 2 * ATT_QKV, 3 * ATT_QKV, 3 * ATT_QKV + SGU_WIDTH,
          3 * ATT_QKV + 2 * SGU_WIDTH, 3 * ATT_QKV + 2 * SGU_WIDTH + D_MODEL)
D_IN = 3 * ATT_QKV + 2 * SGU_WIDTH + 2 * D_MODEL
N_ADA = 6
EPS = 1e-6
NEG_INF = -1e30

kernel_name = "dilated_attn_sgu_gated_hybrid_step"


def rms_norm(x, w):
    xf = x.astype(jnp.float32)
    y = xf * lax.rsqrt(jnp.mean(xf * xf, axis=-1, keepdims=True) + EPS)
    return (y * w.astype(jnp.float32)).astype(x.dtype)


def layer_norm(x, w, b):
    xf = x.astype(jnp.float32)
    mu = jnp.mean(xf, axis=-1, keepdims=True)
    var = jnp.mean(jnp.square(xf - mu), axis=-1, keepdims=True)
    y = (xf - mu) * lax.rsqrt(var + EPS)
    return (y * w.astype(jnp.float32) + b.astype(jnp.float32)).astype(x.dtype)


def alibi_slopes():
    h = np.arange(1, N_ATT_HEADS + 1, dtype=np.float32)
    s = np.power(np.float32(2.0), -8.0 * h / N_ATT_HEADS).astype(np.float32)
    return jnp.asarray(s.reshape(N_ATT_GROUPS, HEADS_PER_GROUP))


def dilated_band_attention(q, k, v, slopes, dil):
    B, S, H, Dh = q.shape
    L = S // dil
    N = B * dil
    nb = -(-L // N_STEPS)
    Lp = nb * N_STEPS

    def to_blocks(t):
        t = t.reshape(B, L, dil, H, Dh).transpose(0, 2, 1, 3, 4).reshape(N, L, H, Dh)
        t = jnp.pad(t, ((0, 0), (0, Lp - L), (0, 0), (0, 0)))
        return t.reshape(N, nb, N_STEPS, H, Dh)

    def with_prev(t):
        prev = jnp.pad(t, ((0, 0), (1, 0), (0, 0), (0, 0), (0, 0)))[:, :-1]
        return jnp.concatenate([prev, t], axis=2)

    qb = to_blocks(q)
    kk = with_prev(to_blocks(k))
    vv = with_prev(to_blocks(v))
    s = jnp.einsum('nbqhd,nbkhd->nbhqk', qb, kk).astype(jnp.float32) * (HEAD_DIM ** -0.5)
    qi = jnp.arange(N_STEPS)[:, None]
    ki = jnp.arange(2 * N_STEPS)[None, :]
    dist = N_STEPS + qi - ki
    key_step = (jnp.arange(nb)[:, None, None] - 1) * N_STEPS + ki[None]
    valid = (dist >= 0) & (dist <= N_STEPS) & (key_step >= 0)
    bias = -slopes[:, None, None] * (dist * dil).astype(jnp.float32)
    s = jnp.where(valid[None, :, None], s + bias[None, None], NEG_INF)
    lse = jax.nn.logsumexp(s, axis=-1)
    p = jnp.exp(s - lse[..., None])
    o = jnp.einsum('nbhqk,nbkhd->nbqhd', p.astype(v.dtype), vv)

    def from_blocks(t):
        t = t.reshape(N, Lp, *t.shape[3:])[:, :L]
        t = t.reshape(B, dil, L, *t.shape[2:])
        t = jnp.swapaxes(t, 1, 2)
        return t.reshape(B, S, *t.shape[3:])

    return from_blocks(o), from_blocks(jnp.swapaxes(lse, 2, 3))


def dilated_cached_attention(q, k_new, v_new, kv_cache, slopes, dil):
    Lc = kv_cache.shape[1]
    T = q.shape[1]
    k_all = jnp.concatenate([kv_cache[:, :, 0].astype(k_new.dtype), k_new], axis=1)
    v_all = jnp.concatenate([kv_cache[:, :, 1].astype(v_new.dtype), v_new], axis=1)
    steps = jnp.arange(N_STEPS + 1)
    idx = Lc + jnp.arange(T)[:, None] - dil * steps[None, :]
    valid = idx >= 0
    idx = jnp.maximum(idx, 0)
    kg = k_all[:, idx]
    vg = v_all[:, idx]
    s = jnp.einsum('bthd,btjhd->bthj', q, kg).astype(jnp.float32) * (HEAD_DIM ** -0.5)
    s = s - slopes[:, None] * (dil * steps).astype(jnp.float32)
    s = jnp.where(valid[:, None, :], s, NEG_INF)
    lse = jax.nn.logsumexp(s, axis=-1)
    p = jnp.exp(s - lse[..., None])
    o = jnp.einsum('bthj,btjhd->bthd', p.astype(vg.dtype), vg)
    return o, lse


def merge_by_denominator(outs, lses):
    alpha = jax.nn.softmax(jnp.stack(lses, axis=0), axis=0)
    return jnp.einsum('gnth,gnthd->nthd', alpha.astype(outs[0].dtype), jnp.stack(outs, axis=0))


def attend_prompt(q, k, v):
    slopes = alibi_slopes()
    S = q.shape[1]
    outs, lses, states = [], [], []
    for g in range(N_ATT_GROUPS):
        o, lse = dilated_band_attention(q[:, :, g], k[:, :, g], v[:, :, g], slopes[g], DILATIONS[g])
        outs.append(o)
        lses.append(lse)
        lw = min(WINDOWS[g], S)
        states.append(jnp.stack([k[:, S - lw:, g], v[:, S - lw:, g]], axis=2))
    return merge_by_denominator(outs, lses), states


def attend_sample(q, k, v, cache1, cache2, cache3):
    slopes = alibi_slopes()
    caches = (cache1, cache2, cache3)
    outs, lses, states = [], [], []
    for g in range(N_ATT_GROUPS):
        o, lse = dilated_cached_attention(q[:, :, g], k[:, :, g], v[:, :, g], caches[g], slopes[g], DILATIONS[g])
        outs.append(o)
        lses.append(lse)
        states.append(jnp.stack([k[:, :, g], v[:, :, g]], axis=2))
    return merge_by_denominator(outs, lses), states


def sgu_prompt(vs, w_s, b_s):
    B, S, _ = vs.shape
    vr = vs.reshape(B, S // CHUNK, CHUNK, SGU_GROUPS, SGU_GROUP_DIM)
    wm = w_s * jnp.tril(jnp.ones((CHUNK, CHUNK), w_s.dtype))
    mixed = jnp.einsum('gts,bnsgc->bntgc', wm, vr) + b_s.T[:, :, None]
    return mixed.reshape(B, S, SGU_WIDTH)


def sgu_sample(vs, w_s, b_s):
    Bd, T, _ = vs.shape
    vr = vs.reshape(Bd, T, SGU_GROUPS, SGU_GROUP_DIM)
    wm = (w_s * jnp.tril(jnp.ones((CHUNK, CHUNK), w_s.dtype)))[:, :T, :T]
    mixed = jnp.einsum('gts,bsgc->btgc', wm, vr) + b_s[:, :T].T[:, :, None]
    return mixed.reshape(Bd, T, SGU_WIDTH)


def trunk_layer(x, c, attend, sgu, w_ada, b_ada, norm1_w, w_in, q_norm_w, k_norm_w,
                sgu_ln_w, sgu_ln_b, w_proj_att, w_proj_sgu, w_out, norm2_w, w_ffn_in, w_ffn_out):
    n, t = x.shape[0], x.shape[1]
    mod = (jax.nn.silu(c) @ w_ada + b_ada)[:, None, :]
    sh1, sc1, g1, sh2, sc2, g2 = jnp.split(mod, N_ADA, axis=-1)
    h = rms_norm(x, norm1_w) * (1 + sc1) + sh1
    z = h @ w_in
    q, k, v, u, vs, ga, gb = jnp.split(z, SPLITS, axis=-1)
    hs = (n, t, N_ATT_GROUPS, HEADS_PER_GROUP, HEAD_DIM)
    q = rms_norm(q.reshape(hs), q_norm_w[:, None, :])
    k = rms_norm(k.reshape(hs), k_norm_w[:, None, :])
    v = v.reshape(hs)
    y_att, att_state = attend(q, k, v)
    u = jax.nn.gelu(u, approximate=False)
    vs = layer_norm(jax.nn.gelu(vs, approximate=False), sgu_ln_w, sgu_ln_b)
    y_sgu = u * sgu(vs)
    merged = (jax.nn.sigmoid(ga) * (y_att.reshape(n, t, ATT_OUT) @ w_proj_att)
              + jax.nn.sigmoid(gb) * (y_sgu @ w_proj_sgu))
    x = x + g1 * (merged @ w_out)
    h2 = rms_norm(x, norm2_w) * (1 + sc2) + sh2
    a, b = jnp.split(h2 @ w_ffn_in, 2, axis=-1)
    x = x + g2 * ((jax.nn.silu(a) * b) @ w_ffn_out)
    return x, att_state, vs


def setup_inputs(seed: int = 0) -> dict:
    key = jax.random.key(seed)
    ks = jax.random.split(key, 24)
    f32 = jnp.float32

    def nrm(k, shape, scale):
        return jax.random.normal(k, shape, f32) * scale

    def cache(k, w):
        return nrm(k, (DEPTH, DEC_BATCH, min(w, PAST_LEN), 2, HEADS_PER_GROUP, HEAD_DIM), 1.0)

    return {
        "x_prompt": nrm(ks[0], (BATCH, SEQ, D_MODEL), 1.0),
        "x_sample": nrm(ks[1], (DEC_BATCH, DEC_SEQ, D_MODEL), 1.0),
        "cache_kv_w128": cache(ks[2], WINDOWS[0]),
        "cache_kv_w512": cache(ks[3], WINDOWS[1]),
        "cache_kv_w2048": cache(ks[4], WINDOWS[2]),
        "c_prompt": nrm(ks[5], (BATCH, D_MODEL), 1.0),
        "c_sample": nrm(ks[6], (DEC_BATCH, D_MODEL), 1.0),
        "w_ada": nrm(ks[7], (DEPTH, D_MODEL, N_ADA * D_MODEL), 0.5 * D_MODEL ** -0.5),
        "b_ada": nrm(ks[8], (DEPTH, N_ADA * D_MODEL), 0.01),
        "norm1_w": 1.0 + nrm(ks[9], (DEPTH, D_MODEL), 0.02),
        "w_in": nrm(ks[10], (DEPTH, D_MODEL, D_IN), D_MODEL ** -0.5),
        "q_norm_w": 1.0 + nrm(ks[11], (DEPTH, N_ATT_GROUPS, HEAD_DIM), 0.02),
        "k_norm_w": 1.0 + nrm(ks[12], (DEPTH, N_ATT_GROUPS, HEAD_DIM), 0.02),
        "sgu_ln_w": 1.0 + nrm(ks[13], (DEPTH, SGU_WIDTH), 0.02),
        "sgu_ln_b": nrm(ks[14], (DEPTH, SGU_WIDTH), 0.01),
        "w_spatial": nrm(ks[15], (DEPTH, SGU_GROUPS, CHUNK, CHUNK), CHUNK ** -0.5),
        "b_spatial": 1.0 + nrm(ks[16], (DEPTH, SGU_GROUPS, CHUNK), 0.1),
        "w_proj_att": nrm(ks[17], (DEPTH, ATT_OUT, D_MODEL), ATT_OUT ** -0.5),
        "w_proj_sgu": nrm(ks[18], (DEPTH, SGU_WIDTH, D_MODEL), SGU_WIDTH ** -0.5),
        "w_out": nrm(ks[19], (DEPTH, D_MODEL, D_MODEL), D_MODEL ** -0.5),
        "norm2_w": 1.0 + nrm(ks[20], (DEPTH, D_MODEL), 0.02),
        "w_ffn_in": nrm(ks[21], (DEPTH, D_MODEL, 2 * D_FF), D_MODEL ** -0.5),
        "w_ffn_out": nrm(ks[22], (DEPTH, D_FF, D_MODEL), D_FF ** -0.5),
    }


def reference(x_prompt, x_sample, cache_kv_w128, cache_kv_w512, cache_kv_w2048, c_prompt, c_sample,
              w_ada, b_ada, norm1_w, w_in, q_norm_w, k_norm_w, sgu_ln_w, sgu_ln_b, w_spatial, b_spatial,
              w_proj_att, w_proj_sgu, w_out, norm2_w, w_ffn_in, w_ffn_out):
    def layer_weights(l):
        return (w_ada[l], b_ada[l], norm1_w[l], w_in[l], q_norm_w[l], k_norm_w[l], sgu_ln_w[l], sgu_ln_b[l],
                w_proj_att[l], w_proj_sgu[l], w_out[l], norm2_w[l], w_ffn_in[l], w_ffn_out[l])

    xp = x_prompt
    p_kv1, p_kv2, p_kv3 = [], [], []
    for l in range(DEPTH):
        sgu = functools.partial(sgu_prompt, w_s=w_spatial[l], b_s=b_spatial[l])
        xp, st, _ = trunk_layer(xp, c_prompt, attend_prompt, sgu, *layer_weights(l))
        p_kv1.append(st[0])
        p_kv2.append(st[1])
        p_kv3.append(st[2])

    xs = x_sample
    s_kv1, s_kv2, s_kv3, s_v = [], [], [], []
    for l in range(DEPTH):
        attend = functools.partial(attend_sample, cache1=cache_kv_w128[l], cache2=cache_kv_w512[l],
                                   cache3=cache_kv_w2048[l])
        sgu = functools.partial(sgu_sample, w_s=w_spatial[l], b_s=b_spatial[l])
        xs, st, vrows = trunk_layer(xs, c_sample, attend, sgu, *layer_weights(l))
        s_kv1.append(st[0])
        s_kv2.append(st[1])
        s_kv3.append(st[2])
        s_v.append(vrows)

    return (xp, xs,
            jnp.stack(p_kv1), jnp.stack(p_kv2), jnp.stack(p_kv3),
            jnp.stack(s_kv1), jnp.stack(s_kv2), jnp.stack(s_kv3),
            jnp.stack(s_v))
```

```python
import contextlib
import numpy as np
import ml_dtypes
import concourse.bass as bass
import concourse.mybir as mybir
from concourse.bass_utils import run_bass_kernel_spmd

F32 = mybir.dt.float32
BF16 = mybir.dt.bfloat16
AF = mybir.ActivationFunctionType
ALU = mybir.AluOpType
AX = mybir.AxisListType
EPOCH = 12000
SAME_ENGINE_WAITS = True


class Buf:
    __slots__ = ("name", "w", "r")

    def __init__(self, name):
        self.name = name
        self.w = None
        self.r = {}


class Eng:
    def __init__(self, k, name):
        self.k = k
        self.name = name
        self.prog = []
        self.sem = None
        self.cnt = 0
        self.waited = {}
        self.own = set()

    def wait(self, ev):
        if ev is None:
            return
        sem, val = ev
        if sem.num in self.own and (self.name == "tensor" or not SAME_ENGINE_WAITS):
            return
        if self.waited.get(sem.num, 0) >= val:
            return
        self.waited[sem.num] = val
        self.prog.append(lambda e, sem=sem, val=val: e.wait_ge(sem, val))

    def emit(self, fn, inc=True):
        if not inc:
            self.prog.append(fn)
            return None
        if self.sem is None or self.cnt >= EPOCH:
            self.sem = self.k.new_sem()
            self.own.add(self.sem.num)
            self.cnt = 0
        self.cnt += 1
        sem = self.sem
        self.prog.append(lambda e, fn=fn, sem=sem: fn(e).then_inc(sem, 1))
        return (sem, self.cnt)


class K:
    def __init__(self):
        self.nc = bass.Bass("TRN2", target_bir_lowering=False)
        self.st = contextlib.ExitStack()
        self.e = {n: Eng(self, n) for n in ("tensor", "vector", "scalar", "gpsimd", "sync")}
        self.nsem = 0
        self.dpools = {}
        self.out_events = []
        self.psum_f32 = self.st.enter_context(self.nc.psum_tensor("psf", [128, 6, 512], F32))[:, :, :]
        self.psum_bf = self.st.enter_context(self.nc.psum_tensor("psb", [128, 2, 1024], BF16))[:, :, :]
        self.pbuf = [Buf("ps%d" % i) for i in range(6)]
        self.pbuf_bf = Buf("psb")

    def new_sem(self):
        self.nsem += 1
        return self.st.enter_context(self.nc.semaphore("s%d" % self.nsem))

    def din(self, name, shape, dt=F32):
        return self.nc.dram_tensor(name, list(shape), dt, kind="ExternalInput").ap()

    def dout(self, name, shape, dt=F32):
        return self.nc.dram_tensor(name, list(shape), dt, kind="ExternalOutput").ap()

    def dscratch(self, name, shape, dt=F32):
        return self.nc.dram_tensor(name, list(shape), dt, kind="Internal").ap()

    def sb(self, name, shape, dt):
        h = self.st.enter_context(self.nc.sbuf_tensor(name, list(shape), dt))
        return h[tuple(slice(None) for _ in shape)]

    def buf(self, name=""):
        return Buf(name)

    def bufs(self, name, n):
        return [Buf("%s%d" % (name, i)) for i in range(n)]

    def _access(self, E, reads, writes):
        for b in reads:
            E.wait(b.w)
        for b in writes:
            E.wait(b.w)
            for ev in list(b.r.values()):
                E.wait(ev)

    def _done(self, ev, reads, writes):
        sem, val = ev
        for b in reads:
            cur = b.r.get(sem.num)
            if cur is None or cur[1] < val:
                b.r[sem.num] = ev
        for b in writes:
            b.w = ev
            b.r = {}

    def op(self, eng, method, reads, writes, *args, **kw):
        E = self.e[eng]
        self._access(E, reads, writes)
        ev = E.emit(lambda e: getattr(e, method)(*args, **kw))
        self._done(ev, reads, writes)

    def mms(self, items, reads, writes):
        E = self.e["tensor"]
        self._access(E, reads, writes)
        n = len(items)
        ev = None
        for i, (o, l, r, s0, s1) in enumerate(items):
            fn = (lambda e, o=o, l=l, r=r, s0=s0, s1=s1: e.matmul(o, l, r, start=s0, stop=s1))
            ev = E.emit(fn, inc=(i == n - 1))
        self._done(ev, reads, writes)

    def mm(self, out, lhsT, rhs, start, stop, reads, writes):
        self.mms([(out, lhsT, rhs, start, stop)], reads, writes)

    def tr(self, out, in_, ident, reads, writes):
        E = self.e["tensor"]
        self._access(E, reads, writes)
        ev = E.emit(lambda e: e.transpose(out, in_, ident))
        self._done(ev, reads, writes)

    def trs(self, items, reads, writes):
        E = self.e["tensor"]
        self._access(E, reads, writes)
        n = len(items)
        ev = None
        for i, (o, a, idn) in enumerate(items):
            ev = E.emit(lambda e, o=o, a=a, idn=idn: e.transpose(o, a, idn), inc=(i == n - 1))
        self._done(ev, reads, writes)

    def act(self, out, in_, func, reads, writes, **kw):
        self.op("scalar", "activation", reads, writes, out, in_, func, **kw)

    def dve_ts(self, out, in0, s1, s2, op0, op1, reads, writes, eng="vector"):
        if op1 is None:
            self.op(eng, "tensor_scalar", reads, writes, out, in0, s1, None, op0)
        else:
            self.op(eng, "tensor_scalar", reads, writes, out, in0, s1, s2, op0, op1)

    def dve_tt(self, out, in0, in1, op, reads, writes, eng="vector"):
        self.op(eng, "tensor_tensor", reads, writes, out, in0, in1, op)

    def dve_stt(self, out, in0, scalar, in1, op0, op1, reads, writes, eng="vector"):
        self.op(eng, "scalar_tensor_tensor", reads, writes, out, in0, scalar, in1, op0, op1)

    def dve_copy(self, out, in_, reads, writes, eng="vector"):
        self.op(eng, "tensor_copy", reads, writes, out, in_)

    def dma(self, q, out, in_, reads, writes, pool=None, npool=4, out_final=False, **kw):
        E = self.e[q]
        pool = pool or (q + "_d")
        if pool not in self.dpools:
            self.dpools[pool] = [[self.new_sem(), 0] for _ in range(npool)] + [0]
        P = self.dpools[pool]
        slot = P[P[-1] % (len(P) - 1)]
        P[-1] += 1
        self._access(E, reads, writes)
        sem, tot = slot
        if tot:
            E.wait((sem, tot))
        slot[1] = tot + 16
        E.emit(lambda e: e.dma_start(out=out, in_=in_, **kw).then_inc(sem, 16), inc=False)
        ev = (sem, tot + 16)
        self._done(ev, reads, writes)
        if out_final:
            self.out_events.append(ev)
        return ev

    def cc_allgather(self, in_ap, out_ap, reads, writes, groups):
        E = self.e["gpsimd"]
        self._access(E, reads, writes)
        sem = self.new_sem()
        E.emit(lambda e: e.collective_compute("AllGather", ALU.bypass, replica_groups=groups, ins=[in_ap], outs=[out_ap]).then_inc(sem), inc=False)
        self._done((sem, 1), reads, writes)

    def finish(self):
        S = self.e["sync"]
        for ev in self.out_events:
            S.wait(ev)
        for n in ("tensor", "vector", "scalar"):
            E = self.e[n]
            if E.sem is not None:
                S.wait((E.sem, E.cnt))
        with self.nc.Block() as block:
            for name in ("tensor", "vector", "scalar", "gpsimd", "sync"):
                prog = self.e[name].prog

                def run(e, prog=prog):
                    for fn in prog:
                        fn(e)
                getattr(block, name)(run)
        self.st.close()


T = 2048
TS = T + 4
DILS = (1, 4, 16)
WINS = (128, 512, 2048)
SCALE = 128.0 ** -0.5
EPS = 1e-6
NWS = 7
STQ = "sync"


def block_cols(g, b):
    if g == 0:
        return (b * 128, (b + 1) * 128, 1)
    if g == 1:
        j, r = b // 4, b % 4
        return (j * 512 + r, (j + 1) * 512, 4)
    return (b, T, 16)


SKEW_D = 2


def skew(items, d):
    d = min(d, SKEW_D)
    n = len(items)
    for t in range(n + d):
        if t < n:
            items[t][0]()
        if t - d >= 0 and items[t - d][1] is not None:
            items[t - d][1]()


def build(stop_after=None):
    k = K()
    xY = k.din("xY", [128, 8, T]); xS = k.din("xS", [128, 8, 4])
    cT = k.din("cT", [128, 8, 8]); hmask = k.din("hmask", [128, 1])
    ident_d = k.din("ident", [128, 128], BF16); ones_d = k.din("ones", [128, 128], BF16)
    E_d = k.din("E", [128, 12, 2, 128], BF16); AL_d = k.din("AL", [128, 12]); tril_d = k.din("tril", [128, 128])
    w_ada = k.din("w_ada", [2, 1024, 6144]); b_adaT = k.din("b_adaT", [128, 2, 48])
    norm1T = k.din("norm1T", [128, 2, 8]); norm2T = k.din("norm2T", [128, 2, 8])
    w_in = k.din("w_in", [2, 1024, 7680])
    qnw = k.din("q_norm_w", [2, 3, 128]); knw = k.din("k_norm_w", [2, 3, 128])
    lnw = k.din("sgu_ln_w", [2, 512]); lnb = k.din("sgu_ln_b", [2, 512])
    w_sp = k.din("w_spatial", [2, 8, 128, 128]); b_sp = k.din("b_spatial", [2, 8, 128])
    w_pa = k.din("w_proj_att", [2, 512, 1024]); w_ps = k.din("w_proj_sgu", [2, 512, 1024])
    w_o = k.din("w_out", [2, 1024, 1024]); w_fi = k.din("w_ffn_in", [2, 1024, 5632]); w_fo = k.din("w_ffn_out", [2, 2816, 1024])
    cch = [k.din("cache%d" % g, [2, 4, WINS[g], 2, 4, 128]) for g in range(3)]
    yT = k.dout("yT", [128, 8, T]); ysT = k.dout("ysT", [128, 8, 4])
    kvp = [k.dout("kvp%d" % g, [2, WINS[g], 2, 4, 128]) for g in range(3)]
    kvs = [k.dout("kvs%d" % g, [2, 4, 2, 4, 128]) for g in range(3)]
    sguv = k.dout("sguv", [2, 4, 512])
    x1Y = k.dscratch("x1Y", [128, 8, TS]); xmD = k.dscratch("xmD", [128, 8, TS])
    b_x1Y = k.bufs("x1Y", 5); b_xmD = k.bufs("xmD", 5)
    NSL = {"A": 5, "B0": 4, "B1": 4, "B2": 4, "B3": 4}
    hs = {(l, ab): k.dscratch("hs%d%s" % (l, ab), [256, NSL[ab] * 512], BF16) for l in range(2) for ab in NSL}
    hr = {(l, ab): k.dscratch("hr%d%s" % (l, ab), [512, NSL[ab] * 512], BF16) for l in range(2) for ab in NSL}
    b_hs = {(l, ab): k.bufs("hs", 2 * 2 * NSL[ab]) for l in range(2) for ab in NSL}
    b_hr = {(l, ab): k.buf("hr") for l in range(2) for ab in NSL}
    PAIRS = [[0, 1], [2, 3], [4, 5], [6, 7]]
    pending = []

    def hslot(g, b):
        if g == 0:
            return "A", 0
        if g == 1:
            return "A", 1 + b - 12
        return "B%d" % (b // 4), b % 4

    RB = 73760
    R = k.sb("R", [128, RB // 2], BF16)

    def view(off, shape, dt):
        n = int(np.prod(shape)) * (4 if dt == F32 else 2)
        v = R[:, off // 2:(off + n) // 2]
        if dt == F32:
            v = v.bitcast(F32)
        names = " ".join("d%d" % i for i in range(len(shape)))
        if len(shape) > 1:
            v = v.rearrange("p (%s) -> p %s" % (names, names), **{"d%d" % i: s for i, s in enumerate(shape)})
        return v
    yatt = view(0, [4, TS], BF16)
    acc = view(16416, [2, 2, T], F32)
    QT = view(49184, [2, 16, 128], BF16); KT = view(49184 + 8192, [2, 16, 128], BF16); Vt = view(49184 + 16384, [16, 256], BF16)
    ysgu = view(16416, [4, TS], BF16); uT = view(32832, [4, TS], BF16); mg = view(32832, [8, TS], BF16)
    hid = view(0, [22, 1028], BF16)
    hT = k.sb("hT", [128, 8, TS], BF16)
    WS = [k.sb("ws%d" % i, [128, 2048], BF16) for i in range(NWS)]
    b_w = k.bufs("w", NWS)
    xrs = [k.sb("xr%d" % i, [128, 8, 256], F32) for i in range(2)]; b_xrs = [k.bufs("xr%d_" % i, 8) for i in range(2)]
    sqt = k.sb("sqt", [128, 8, 256], BF16); b_sq = k.bufs("sq", 8)
    tmpb = sqt; b_tmpb = b_sq
    rsts = [k.sb("rst%d" % i, [128, 256], F32) for i in range(2)]; b_rsts = k.bufs("rst", 2)
    ident = k.sb("identS", [128, 128], BF16); ones = k.sb("onesS", [128, 128], BF16)
    Et = k.sb("Et", [128, 12, 2, 128], BF16); Eh = k.sb("Eh", [128, 12, 128], BF16)
    ALt = k.sb("ALt", [128, 12], F32); tril = k.sb("trilS", [128, 128], F32); hm = k.sb("hm", [128, 1], F32)
    MOD = k.sb("MOD", [128, 2, 6, 8, 8], F32); b_mod = [k.bufs("mod%d_" % l, 6) for l in range(2)]
    cst = k.sb("cst", [128, 8, 8], F32); csb = k.sb("csb", [128, 8, 8], BF16)
    bad = k.sb("bad", [128, 2, 48], F32); n1t = k.sb("n1t", [128, 2, 8], F32); n2t = k.sb("n2t", [128, 2, 8], F32)
    QW = k.sb("QW", [128, 1, 3, 128], F32); KW = k.sb("KW", [128, 1, 3, 128], F32)
    LNW = k.sb("LNW", [128, 1, 512], F32); LNB = k.sb("LNB", [128, 1, 512], F32)
    WMT = k.sb("WMT", [128, 1, 8, 128], BF16); BSP = k.sb("BSP", [1, 1, 8, 128], BF16)
    W00 = k.sb("W00", [1, 1, 512], F32); B00 = k.sb("B00", [1, 1, 512], F32)
    b_lc = k.buf("lc")
    w8 = k.sb("w8", [1, 16], F32); b_w8 = k.buf("w8")
    b_c = k.buf("consts")
    NSTG = 3
    stg = [k.sb("stg%d" % i, [128, 512], F32) for i in range(NSTG)]; b_stg = k.bufs("stg", NSTG)
    sbf = [k.sb("sbf%d" % i, [128, 512], BF16) for i in range(3)]; b_sbf = k.bufs("sbf", 3)
    hKt = [k.sb("hK%d" % i, [128, 2, 128], BF16) for i in range(3)]; hVt = [k.sb("hV%d" % i, [128, 256], BF16) for i in range(3)]
    b_hK = k.bufs("hK", 3); b_hV = k.bufs("hV", 3)
    sm = [k.sb("sm%d" % i, [128, 16], F32) for i in range(6)]; b_sm = k.bufs("sm", 6)
    xmc = [k.sb("xmc%d" % i, [128, 512], F32) for i in range(2)]; b_xmc = k.bufs("xmc", 2)
    sQb = k.sb("sQb", [1, 4, 256], BF16); sKb = k.sb("sKb", [1, 4, 256], BF16); sVb = k.sb("sVb", [1, 4, 256], BF16)
    b_sQ = k.bufs("sQ", 4); b_sK = k.bufs("sK", 4); b_sV = k.bufs("sV", 4)
    cK = [xmc[0].rearrange("p (a b d) -> p a b d", a=2, b=2)]; b_cK = [b_xmc[0]]
    cVb = k.sb("cVb", [128, 2, 128], BF16); b_cVb = k.buf("cVb")
    sacc = k.sb("sacc", [128, 4, 4], F32); b_sacc = k.bufs("sacc", 4)
    psm = k.sb("psm", [128, 4], BF16); b_psm = k.buf("psm")
    pself = k.sb("pself", [1, 4], BF16); b_pself = k.buf("pself")

    ps = k.psum_f32; pb = k.pbuf; psb = k.psum_bf
    pbq = k.bufs("psbq", 2)
    cnt = {}

    def nxt(name, n):
        i = cnt.get(name, 0) % n
        cnt[name] = cnt.get(name, 0) + 1
        return i

    def barrier():
        evs = []
        for n in ("tensor", "vector", "scalar"):
            E = k.e[n]
            if E.sem is not None:
                evs.append((E.sem, E.cnt))
        for P in k.dpools.values():
            for sem, tot in P[:-1]:
                if tot:
                    evs.append((sem, tot))
        for n in ("vector", "scalar"):
            for ev in evs:
                k.e[n].wait(ev)

    def wload(src, kc, n, ada_ok=True):
        assert kc * n <= 2048
        i = nxt("w", NWS)
        v = WS[i][:, 0:kc * n].rearrange("p (a b) -> p a b", a=kc)
        k.dma("gpsimd", v, src.rearrange("(a p) n -> p a n", p=128), [], [b_w[i]], pool="w", npool=NWS)
        if ada_ok and ada_en[0] and ada_jobs:
            cnt["wm"] = cnt.get("wm", 0) + 1
            if cnt["wm"] % 3 == 0:
                ada_run(1)
        if pending:
            pending[0] -= 1
            if pending[0] <= 0:
                for fn in pending[1:]:
                    fn()
                del pending[:]
        return v, b_w[i]

    S_ = "sync"
    for dst, src in ((ident, ident_d), (ones, ones_d), (Et, E_d), (ALt, AL_d), (tril, tril_d), (hm, hmask), (cst, cT),
                     (bad, b_adaT), (n1t, norm1T), (n2t, norm2T)):
        k.dma(S_, dst, src, [], [Buf("c")], pool="cst", npool=10)
    for ev_ in [(sem_, tot_) for sem_, tot_ in k.dpools["cst"][:-1] if tot_]:
        for n_ in ("tensor", "vector", "scalar"):
            k.e[n_].wait(ev_)
    wst = xrs[0][:, :, 0:128]; wsb = sqt[:, :, 0:128]

    def load_layer_consts(l):
        for g in range(3):
            k.dma(S_, QW[:, 0, g, :], qnw[l, g:g + 1, :].partition_broadcast(128), [], [b_lc])
            k.dma(S_, KW[:, 0, g, :], knw[l, g:g + 1, :].partition_broadcast(128), [], [b_lc])
        k.dma(S_, LNW[:, 0, :], lnw[l:l + 1, :].partition_broadcast(128), [], [b_lc])
        k.dma(S_, LNB[:, 0, :], lnb[l:l + 1, :].partition_broadcast(128), [], [b_lc])
        k.dma("gpsimd", BSP[0:1, 0, :, :], b_sp[l:l + 1, :, :], [], [b_lc], pool="w", npool=NWS)
        k.dma(S_, w8[0:1, 0:8].rearrange("p (g o) -> p g o", o=1), b_sp[l:l + 1, :, 0:1], [], [b_w8], allow_slow_non_contiguous=True)
        k.dma(S_, w8[0:1, 8:16].rearrange("p (g o) -> p g o", o=1), w_sp[l:l + 1, :, 0, 0:1], [], [b_w8], allow_slow_non_contiguous=True)
        k.dve_copy(B00[0:1, 0, :].rearrange("p (g c) -> p g c", g=8), w8[0:1, 0:8].unsqueeze(2).to_broadcast([1, 8, 64]), [b_w8], [b_lc])
        k.dve_copy(W00[0:1, 0, :].rearrange("p (g c) -> p g c", g=8), w8[0:1, 8:16].unsqueeze(2).to_broadcast([1, 8, 64]), [b_w8], [b_lc])
        k.dma(S_, wst, w_sp[l].rearrange("g t s -> t g s"), [], b_xrs[0])
        for g8 in range(8):
            k.dve_tt(wsb[:, g8, :], wst[:, g8, :], tril, ALU.mult, b_xrs[0] + [b_c], [b_sq[g8]])
        k.trs([(psb[:, 0, g8 * 128:(g8 + 1) * 128], wsb[:, g8, :], ident) for g8 in range(8)], b_sq + [b_c], [pbq[0]])
        k.dve_copy(WMT[:, 0].rearrange("p g t -> p (g t)"), psb[:, 0, :], [pbq[0]], [b_lc])

    for gh in range(12):
        k.dve_ts(Eh[:, gh, :], Et[:, gh, 0, :], hm[:, 0:1], None, ALU.mult, None, [b_c], [b_c])
    k.act(csb, cst, AF.Silu, [b_c], [b_c])
    ada_jobs = []
    ada_en = [True]

    def mk_ada(l, t):
        def job():
            wv, bw = wload(w_ada[l, :, t * 256:(t + 1) * 256], 8, 256, ada_ok=False)
            return lambda: compute(wv, bw)

        def compute(wv, bw):
            for oc in range(2):
                j = t * 2 + oc
                bk = nxt("pb", 6)
                k.mms([(ps[:, bk, 0:8], wv[:, kc, oc * 128:(oc + 1) * 128], csb[:, kc, :], kc == 0, kc == 7) for kc in range(8)],
                      [bw, b_c], [pb[bk]])
                k.dve_ts(MOD[:, l, j // 8, j % 8, :], ps[:, bk, 0:8], bad[:, l, j:j + 1], None, ALU.add, None, [pb[bk], b_c], [b_mod[l][j // 8]])
            if t in (7, 19):
                a_ = 1 if t == 7 else 4
                nt = n1t if t == 7 else n2t
                for kc in range(8):
                    k.dve_ts(MOD[:, l, a_, kc, :], MOD[:, l, a_, kc, :], 1.0, nt[:, l, kc:kc + 1], ALU.add, ALU.mult, [b_mod[l][a_], b_c], [b_mod[l][a_]])
        return job
    for l in range(2):
        for t in range(24):
            ada_jobs.append((l, mk_ada(l, t)))

    ada_pend = []

    def ada_run(n=1, upto_layer=None, drain=False):
        while ada_jobs and (n > 0 or (upto_layer is not None and ada_jobs[0][0] <= upto_layer)):
            comp = ada_jobs.pop(0)[1]()
            if ada_pend:
                ada_pend.pop(0)()
            ada_pend.append(comp)
            n -= 1
        if drain or upto_layer is not None:
            while ada_pend:
                ada_pend.pop(0)()
    ada_run(8, drain=True)

    REG_P = [(j * 512, 512, "p", j) for j in range(4)]
    REG_S = (T, 4, "s", 4)
    b_hT = [k.bufs("hT%d_" % j, 8) for j in range(5)]

    def modmul(dst, src, l, a, kc, reg, other=None, op1=None, eng="vector", reads=(), writes=()):
        c0, n, kind, ri = reg
        rd = list(reads) + [b_mod[l][a]]
        if kind == "p":
            sc = MOD[:, l, a, kc, 0:1]
            if other is None:
                k.dve_ts(dst, src, sc, None, ALU.mult, None, rd, list(writes), eng=eng)
            else:
                k.dve_stt(dst, src, sc, other, ALU.mult, op1, rd, list(writes), eng=eng)
        else:
            mt = MOD[:, l, a, kc, 1:5]
            if other is None:
                k.dve_tt(dst, src, mt, ALU.mult, rd, list(writes), eng=eng)
            else:
                i = nxt("sm", 6)
                k.dve_tt(sm[i][:, 0:4], src, mt, ALU.mult, rd, [b_sm[i]], eng=eng)
                k.dve_tt(dst, sm[i][:, 0:4], other, op1, [b_sm[i]] + list(reads), list(writes), eng=eng)

    def norm_a(i, nsub, xr, b_xr):
        rst, b_rst = rsts[i % 2], b_rsts[i % 2]
        for kc in range(8):
            k.act(sqt[:, kc, 0:nsub], xr[:, kc, 0:nsub], AF.Square, [b_xr[kc]], [b_sq[kc]])
        bk = nxt("pb", 6)
        k.mms([(ps[:, bk, 0:nsub], ones, sqt[:, kc, 0:nsub], kc == 0, kc == 7) for kc in range(8)], b_sq + [b_c], [pb[bk]])
        k.act(rst[:, 0:nsub], ps[:, bk, 0:nsub], AF.Sqrt, [pb[bk]], [b_rst], scale=1.0 / 1024, bias=EPS)

    def norm_a2(i, nsub):
        rst, b_rst = rsts[i % 2], b_rsts[i % 2]
        k.op("vector", "reciprocal", [b_rst], [b_rst], rst[:, 0:nsub], rst[:, 0:nsub])

    def norm_b(i, l, a, bidx, reg, sub, nsub, xr, b_xr):
        c0, n, kind, ri = reg
        rst, b_rst = rsts[i % 2], b_rsts[i % 2]
        for kc in range(8):
            xv = xr[:, kc, 0:nsub]
            tv = tmpb[:, kc, 0:nsub]
            k.dve_tt(tv, xv, rst[:, 0:nsub], ALU.mult, [b_xr[kc], b_rst], [b_tmpb[kc]])
        for kc in range(8):
            tv = tmpb[:, kc, 0:nsub]
            hv = hT[:, kc, c0 + sub:c0 + sub + nsub]
            if kind == "p":
                k.act(hv, tv, AF.Identity, [b_tmpb[kc], b_mod[l][a], b_mod[l][bidx]], [b_hT[ri][kc]],
                      scale=MOD[:, l, a, kc, 0:1], bias=MOD[:, l, bidx, kc, 0:1])
            else:
                k.dve_tt(tv, tv, MOD[:, l, a, kc, 1:5], ALU.mult, [b_tmpb[kc], b_mod[l][a]], [b_tmpb[kc]])
                k.dve_tt(hv, tv, MOD[:, l, bidx, kc, 1:5], ALU.add, [b_tmpb[kc], b_mod[l][bidx]], [b_hT[ri][kc]])

    def subregs(reg):
        c0, n, kind, ri = reg
        return [(0, 256), (256, 256)] if kind == "p" else [(0, 4)]

    def run_pass(l):
        isY = True
        regs = REG_P + [REG_S]
        if l == 0:
            def xsrc(reg, sub, ns):
                c0, n, kind, ri = reg
                if kind == "s":
                    return xS[:, :, 0:4], []
                return xY[:, :, c0 + sub:c0 + sub + ns], []
        else:
            def xsrc(reg, sub, ns):
                c0, n, kind, ri = reg
                return x1Y[:, :, c0 + sub:c0 + sub + ns], [b_x1Y[ri]]
        xdst, bxd = (x1Y, b_x1Y)
        final = l == 1
        barrier()
        load_layer_consts(l)
        subs = [(reg, sub, ns) for reg in regs for sub, ns in subregs(reg)]

        def xload(i):
            reg, sub, ns = subs[i]
            src, rb = xsrc(reg, sub, ns)
            k.dma(S_, xrs[i % 2][:, :, 0:ns], src, rb, b_xrs[i % 2])
        xload(0)
        xload(1)
        for t in range(len(subs) + 1):
            if t >= 1:
                reg, sub, ns = subs[t - 1]
                norm_b(t - 1, l, 1, 0, reg, sub, ns, xrs[(t - 1) % 2], b_xrs[(t - 1) % 2])
                if t + 1 < len(subs):
                    xload(t + 1)
            if t < len(subs):
                reg, sub, ns = subs[t]
                norm_a(t, ns, xrs[t % 2], b_xrs[t % 2])
                norm_a2(t, ns)
        bh_all = [b for j in range(5) for b in b_hT[j]]
        if stop_after == "N1":
            return True
        def sweep(kv_only):
            for hp in range(2):
                for g in range(3):
                    dil = DILS[g]
                    if kv_only:
                        blocks = {0: [15], 1: [12, 13, 14, 15], 2: list(range(16))}[g]
                    else:
                        blocks = list(range(16))
                    items = []
                    for part in ((1, 2) if kv_only else (0, 1, 2)):
                        col0 = part * 1536 + g * 512 + hp * 256
                        blist = [(b, 128) for b in blocks] + ([(16 + i, 1) for i in range(4)] if not kv_only else [])
                        wref = {}
                        for bi_, (b, M) in enumerate(blist):
                            def s0(part=part, b=b, M=M, col0=col0, first=(bi_ == 0), wref=wref, st={}):
                                if first:
                                    wref["w"] = wload(w_in[l, :, col0:col0 + 256], 8, 256)
                                wv, bw = wref["w"]
                                nw = (QW, KW, None)[part]
                                if M == 128:
                                    s0_, s1_, st_ = block_cols(g, b)
                                    lc = lambda kc: hT[:, kc, s0_:s1_:st_]
                                    rdh = bh_all[0:32]
                                else:
                                    i_s = b - 16
                                    lc = lambda kc: hT[:, kc, T + i_s:T + i_s + 1]
                                    rdh = b_hT[4]
                                bk = nxt("pb", 6)
                                k.mms([(ps[0:M, bk, 0:256], lc(kc), wv[:, kc, :], kc == 0, kc == 7) for kc in range(8)], [bw] + rdh, [pb[bk]])
                                pv = ps[0:M, bk, 0:256]
                                si = nxt("stg", NSTG); st4 = stg[si]; bst = b_stg[si]
                                if part < 2:
                                    mi = nxt("sm", 6); smv = sm[mi]; bsm = b_sm[mi]
                                    for h2 in range(2):
                                        k.act(st4[0:M, 256 + h2 * 128:256 + (h2 + 1) * 128], pv[:, h2 * 128:(h2 + 1) * 128], AF.Square, [pb[bk]], [bst, bsm],
                                              accum_out=smv[0:M, h2:h2 + 1])
                                    k.act(smv[0:M, 0:2], smv[0:M, 0:2], AF.Sqrt, [bsm], [bsm], scale=1.0 / 128, bias=EPS)
                                    k.op("vector", "reciprocal", [bsm], [bsm], smv[0:M, 0:2], smv[0:M, 0:2])
                                    for h2 in range(2):
                                        k.dve_stt(st4[0:M, h2 * 128:(h2 + 1) * 128], pv[:, h2 * 128:(h2 + 1) * 128], smv[0:M, h2:h2 + 1],
                                                  nw[0:M, 0, g, :], ALU.mult, ALU.mult, [pb[bk], bsm, b_lc], [bst])
                                else:
                                    k.act(st4[0:M, 0:256], pv, AF.Copy, [pb[bk]], [bst])
                                if M == 1:
                                    dstt, bd = ((sQb, b_sQ), (sKb, b_sK), (sVb, b_sV))[part]
                                    k.dve_copy(dstt[0:1, i_s, :], st4[0:1, 0:256], [bst], [bd[i_s]])
                                    if part > 0:
                                        k.dma(STQ, kvs[g][l, i_s:i_s + 1, part - 1, hp * 2:hp * 2 + 2, :],
                                              st4[0:1, 0:256].rearrange("p (h d) -> p h d", h=2), [bst], [], pool="st", npool=6, out_final=True)
                                    return
                                if part > 0 and not kv_only:
                                    rows = None
                                    if g == 0 and b == 15:
                                        rows = (0, 128, 1)
                                    elif g == 1 and b >= 12:
                                        rows = (b % 4, 512, 4)
                                    elif g == 2:
                                        rows = (b, 2048, 16)
                                    if rows is not None:
                                        k.dma(STQ, kvp[g][l, rows[0]:rows[1]:rows[2], part - 1, hp * 2:hp * 2 + 2, :],
                                              st4[:, 0:256].rearrange("p (h d) -> p h d", h=2), [bst], [], pool="st", npool=6, out_final=True)
                                if part == 2:
                                    k.dve_copy(Vt[:, b, :], st4[:, 0:256], [bst], [b_V[b]])
                                else:
                                    bi = nxt("sbf", 3)
                                    st["bi"] = bi
                                    k.dve_copy(sbf[bi][:, 0:256], st4[:, 0:256], [bst], [b_sbf[bi]])

                            def s1(part=part, b=b, M=M, st=s0.__defaults__[-1]):
                                if M == 1 or part == 2:
                                    return
                                bi = st["bi"]
                                qi = nxt("pq", 2)
                                k.trs([(psb[:, qi, h2 * 128:(h2 + 1) * 128], sbf[bi][:, h2 * 128:(h2 + 1) * 128], ident) for h2 in range(2)],
                                      [b_sbf[bi], b_c], [pbq[qi]])
                                dT_, bT_ = (QT, b_QT) if part == 0 else (KT, b_KT)
                                k.act(dT_[:, :, b, :], psb[:, qi, 0:256].rearrange("p (h q) -> p h q", h=2), AF.Copy, [pbq[qi]], [bT_[b]])
                            items.append((s0, s1))
                    skew(items, 2)
                    if stop_after == "QKV0":
                        return True
                    if kv_only:
                        hb = {0: [15], 1: [12, 13, 14, 15], 2: list(range(16))}[g]
                        for b in hb:
                            ab, sl = hslot(g, b)
                            c0_ = sl * 512 + hp * 256
                            k.dma(STQ, hs[(l, ab)][0:128, c0_:c0_ + 256].rearrange("p (h q) -> p h q", h=2), KT[:, :, b, :], [b_KT[b]],
                                  [b_hs[(l, ab)][(sl * 2 + hp) * 2]], pool="st", npool=6)
                            k.dma(STQ, hs[(l, ab)][128:256, c0_:c0_ + 256], Vt[:, b, :], [b_V[b]], [b_hs[(l, ab)][(sl * 2 + hp) * 2 + 1]], pool="st", npool=6)
                    if kv_only:
                        continue
                    items = []
                    for b in {0: list(range(1, 16)) + [0], 1: list(range(4, 16)) + [0, 1, 2, 3], 2: list(range(16))}[g]:
                        if g == 0:
                            pvb = b - 1 if b > 0 else None
                            hb_ = 15
                        elif g == 1:
                            pvb = b - 4 if b >= 4 else None
                            hb_ = 12 + b
                        else:
                            pvb = None
                            hb_ = b
                        use_halo = pvb is None
                        hasprev = use_halo or pvb is not None
                        def a0(b=b, pvb=pvb, hb_=hb_, use_halo=use_halo, hasprev=hasprev, st={}):
                            hi = None
                            if use_halo:
                                hi = nxt("h", 3)
                                ab, sl = hslot(g, hb_)
                                c0_ = sl * 512 + hp * 256
                                k.dma(S_, hKt[hi], hr[(l, ab)][0:128, c0_:c0_ + 256].rearrange("p (h q) -> p h q", h=2), [b_hr[(l, ab)]], [b_hK[hi]], pool="hl", npool=3)
                                k.dma(S_, hVt[hi], hr[(l, ab)][128:256, c0_:c0_ + 256], [b_hr[(l, ab)]], [b_hV[hi]], pool="hl2", npool=3)
                            st["hi"] = hi
                            gh0 = g * 4 + hp * 2
                            bk = nxt("pb", 6)
                            its = []; rd = [b_QT[b], b_KT[b]]
                            for h2 in range(2):
                                if hasprev:
                                    if use_halo:
                                        kprev = hKt[hi][:, h2, :]; rd.append(b_hK[hi])
                                    else:
                                        kprev = KT[:, h2, pvb, :]; rd.append(b_KT[pvb])
                                    its.append((ps[:, bk, h2 * 256:h2 * 256 + 128], kprev, QT[:, h2, b, :], True, True))
                                its.append((ps[:, bk, h2 * 256 + 128:h2 * 256 + 256], KT[:, h2, b, :], QT[:, h2, b, :], True, True))
                            k.mms(its, rd, [pb[bk]])
                            pi = nxt("sbf", 3); P_ = sbf[pi]; bP = b_sbf[pi]
                            st["pi"] = pi
                            P4 = P_[:, :].rearrange("p (h a q) -> p h a q", h=2, a=2)
                            S4 = ps[:, bk, :].rearrange("p (h a q) -> p h a q", h=2, a=2)
                            if hasprev:
                                k.act(P_[:, :], ps[:, bk, :], AF.Exp, [pb[bk]], [bP], scale=SCALE)
                            else:
                                k.act(P4[:, :, 1, :], S4[:, :, 1, :], AF.Exp, [pb[bk]], [bP], scale=SCALE)
                            if use_halo:
                                k.dve_tt(P4[:, :, 0, :], P4[:, :, 0, :], Eh[:, gh0:gh0 + 2, :], ALU.mult, [bP, b_c], [bP])
                                k.dve_tt(P4[:, :, 1, :], P4[:, :, 1, :], Et[:, gh0:gh0 + 2, 1, :], ALU.mult, [bP, b_c], [bP])
                            elif hasprev:
                                k.dve_tt(P_[:, :], P_[:, :], Et[:, gh0:gh0 + 2, :, :].rearrange("p h a q -> p (h a q)"), ALU.mult, [bP, b_c], [bP])
                            else:
                                k.dve_tt(P4[:, :, 1, :], P4[:, :, 1, :], Et[:, gh0:gh0 + 2, 1, :], ALU.mult, [bP, b_c], [bP])

                        def a1(b=b, pvb=pvb, use_halo=use_halo, hasprev=hasprev, st=a0.__defaults__[-1]):
                            hi = st["hi"]
                            pi = st["pi"]; P_ = sbf[pi]; bP = b_sbf[pi]
                            s0_, s1_, st_ = block_cols(g, b)
                            bk2 = nxt("pb", 6)
                            its = []; rd = [bP, b_V[b], b_c]
                            for h2 in range(2):
                                Pp = P_[:, h2 * 256:h2 * 256 + 128]; Po = P_[:, h2 * 256 + 128:h2 * 256 + 256]
                                oO = ps[:, bk2, h2 * 256:h2 * 256 + 128]; oD = ps[:, bk2, h2 * 256 + 128:h2 * 256 + 256]
                                if hasprev:
                                    if use_halo:
                                        vprev = hVt[hi][:, h2 * 128:(h2 + 1) * 128]; rd.append(b_hV[hi])
                                    else:
                                        vprev = Vt[:, pvb, h2 * 128:(h2 + 1) * 128]; rd.append(b_V[pvb])
                                    its.append((oO, vprev, Pp, True, False))
                                its.append((oO, Vt[:, b, h2 * 128:(h2 + 1) * 128], Po, not hasprev, True))
                                if hasprev:
                                    its.append((oD, ones, Pp, True, False))
                                its.append((oD, ones, Po, not hasprev, True))
                            k.mms(its, rd, [pb[bk2]])
                            av = acc[:, :, :, s0_:s1_:st_]
                            pv2 = ps[:, bk2, :].rearrange("p (h a q) -> p h a q", h=2, a=2)
                            if g == 0:
                                k.act(av, pv2, AF.Copy, [pb[bk2]], b_accb)
                            else:
                                k.dve_tt(av, pv2, av, ALU.add, [pb[bk2]] + b_accb, b_accb)
                        items.append((a0, a1))
                    skew(items, 2)
                    if stop_after == "ATT0":
                        return True
                    if isY:
                        for i_s in range(4):
                            ci = nxt("cK", 1)
                            k.dma(S_, cK[ci], cch[g][l, i_s, 0:WINS[g]:dil, :, hp * 2:hp * 2 + 2, :], [], [b_cK[ci]], pool="ck", npool=2)
                            bk = nxt("pb", 6)
                            k.mm(ps[:, bk, 0:256], ones[0:1, :], sQb[0:1, i_s, :], True, True, [b_sQ[i_s], b_c], [pb[bk]])
                            si = nxt("stg", NSTG); st4 = stg[si]; bst = b_stg[si]
                            k.dve_tt(st4[:, 0:256].rearrange("p (h d) -> p h d", h=2), cK[ci][:, 0, :, :], ps[:, bk, 0:256].rearrange("p (h d) -> p h d", h=2),
                                     ALU.mult, [b_cK[ci], pb[bk]], [bst])
                            mi = nxt("sm", 6); smv = sm[mi]; bsm = b_sm[mi]
                            k.op("vector", "tensor_reduce", [bst], [bsm], smv[:, 0:2], st4[:, 0:256].rearrange("p (h d) -> p h d", h=2), AX.X, ALU.add)
                            gh0 = g * 4 + hp * 2
                            k.dve_stt(smv[:, 0:2], smv[:, 0:2], SCALE, ALt[:, gh0:gh0 + 2], ALU.mult, ALU.add, [bsm, b_c], [bsm])
                            k.act(psm[:, 0:2], smv[:, 0:2], AF.Exp, [bsm], [b_psm])
                            k.dve_tt(st4[0:1, 256:512], sQb[0:1, i_s, :], sKb[0:1, i_s, :], ALU.mult, [b_sQ[i_s], b_sK[i_s]], [bst])
                            k.op("vector", "tensor_reduce", [bst], [bsm], smv[0:1, 4:6], st4[0:1, 256:512].rearrange("p (h d) -> p h d", h=2), AX.X, ALU.add)
                            k.act(pself[0:1, 0:2], smv[0:1, 4:6], AF.Exp, [bsm], [b_pself], scale=SCALE)
                            k.dve_copy(cVb, cK[ci][:, 1, :, :], [b_cK[ci]], [b_cVb])
                            bk2 = nxt("pb", 6)
                            its = []
                            for h2 in range(2):
                                its.append((ps[:, bk2, h2:h2 + 1], cVb[:, h2, :], psm[:, h2:h2 + 1], True, False))
                                its.append((ps[:, bk2, h2:h2 + 1], sVb[0:1, i_s, h2 * 128:(h2 + 1) * 128], pself[0:1, h2:h2 + 1], False, True))
                            its.append((ps[:, bk2, 2:4], ones, psm[:, 0:2], True, False))
                            its.append((ps[:, bk2, 2:4], ones[0:1, :], pself[0:1, 0:2], False, True))
                            k.mms(its, [b_cVb, b_psm, b_pself, b_sV[i_s], b_c], [pb[bk2]])
                            if g == 0:
                                k.dve_copy(sacc[:, i_s, :], ps[:, bk2, 0:4], [pb[bk2]], [b_sacc[i_s]])
                            else:
                                k.dve_tt(sacc[:, i_s, :], ps[:, bk2, 0:4], sacc[:, i_s, :], ALU.add, [pb[bk2], b_sacc[i_s]], [b_sacc[i_s]])
                if kv_only:
                    continue
                for h2 in range(2):
                    h = hp * 2 + h2
                    for q4 in range(4):
                        cs_ = slice(q4 * 512, (q4 + 1) * 512)
                        k.op("vector", "reciprocal", [b_accb[h2]], [b_accb[h2]], acc[:, h2, 1, cs_], acc[:, h2, 1, cs_])
                        k.dve_tt(yatt[:, h, cs_], acc[:, h2, 0, cs_], acc[:, h2, 1, cs_], ALU.mult, [b_accb[h2]], [b_yatt[h]])
                if isY:
                    for i_s in range(4):
                        k.op("vector", "reciprocal", [b_sacc[i_s]], [b_sacc[i_s]], sacc[:, i_s, 2:4], sacc[:, i_s, 2:4])
                        k.dve_tt(yatt[:, hp * 2:hp * 2 + 2, T + i_s], sacc[:, i_s, 0:2], sacc[:, i_s, 2:4], ALU.mult, [b_sacc[i_s]], [b_yatt[hp * 2], b_yatt[hp * 2 + 1]])

        sweep(True)
        def mk(ab):
            def fn():
                k.cc_allgather(hs[(l, ab)], hr[(l, ab)], b_hs[(l, ab)], [b_hr[(l, ab)]], PAIRS)
            return fn
        pending[:] = [2] + [mk(ab) for ab in NSL]
        sweep(False)
        if stop_after == "att":
            return True
        barrier()
        b_u = [[Buf("u") for _ in range(5)] for _ in range(4)]
        for t2 in range(2):
            wv, bw = wload(w_in[l, :, 4608 + t2 * 256:4608 + (t2 + 1) * 256], 8, 256)
            for o2 in range(2):
                oc = t2 * 2 + o2
                for reg in regs:
                    c0, n, kind, ri = reg
                    bk = nxt("pb", 6)
                    k.mms([(ps[:, bk, 0:n], wv[:, kc, o2 * 128:(o2 + 1) * 128], hT[:, kc, c0:c0 + n], kc == 0, kc == 7) for kc in range(8)],
                          [bw] + b_hT[ri], [pb[bk]])
                    k.act(uT[:, oc, c0:c0 + n], ps[:, bk, 0:n], AF.Gelu, [pb[bk]], [b_u[oc][ri]])
        wvA, bwA = wload(w_in[l, :, 5120:5376], 8, 256)
        wvB, bwB = wload(w_in[l, :, 5376:5632], 8, 256)
        b_ys = k.bufs("ys", 5)
        blist = [(b, 128) for b in range(16)] + ([(16 + i, 1) for i in range(4)] if isY else [])
        items = []
        for b, M in blist:
            def g0(b=b, M=M, st={}):
                if M == 128:
                    lc = lambda kc: hT[:, kc, b * 128:(b + 1) * 128]
                    rdh = b_hT[b // 4]
                else:
                    i_s = b - 16
                    lc = lambda kc: hT[:, kc, T + i_s:T + i_s + 1]
                    rdh = b_hT[4]
                bk = nxt("pb", 6)
                k.mms([(ps[0:M, bk, 0:256], lc(kc), wvA[:, kc, :], kc == 0, kc == 7) for kc in range(8)]
                      + [(ps[0:M, bk, 256:512], lc(kc), wvB[:, kc, :], kc == 0, kc == 7) for kc in range(8)], [bwA, bwB] + rdh, [pb[bk]])
                si = nxt("stg", NSTG); gv = stg[si]; bst = b_stg[si]
                mi = nxt("sm", 6); smv = sm[mi]; bsm = b_sm[mi]
                k.act(gv[0:M, :], ps[0:M, bk, 0:512], AF.Gelu, [pb[bk]], [bst, bsm], accum_out=smv[0:M, 0:1])
                si2 = nxt("stg", NSTG); jk = stg[si2]; bjk = b_stg[si2]
                k.act(jk[0:M, :], gv[0:M, :], AF.Square, [bst], [bjk, bsm], accum_out=smv[0:M, 1:2])
                k.dve_ts(smv[0:M, 0:2], smv[0:M, 0:2], 1.0 / 512, None, ALU.mult, None, [bsm], [bsm])
                k.dve_tt(smv[0:M, 2:3], smv[0:M, 0:1], smv[0:M, 0:1], ALU.mult, [bsm], [bsm])
                k.dve_tt(smv[0:M, 3:4], smv[0:M, 1:2], smv[0:M, 2:3], ALU.subtract, [bsm], [bsm])
                k.act(smv[0:M, 3:4], smv[0:M, 3:4], AF.Sqrt, [bsm], [bsm], scale=1.0, bias=EPS)
                k.op("vector", "reciprocal", [bsm], [bsm], smv[0:M, 3:4], smv[0:M, 3:4])
                k.dve_ts(gv[0:M, :], gv[0:M, :], smv[0:M, 0:1], smv[0:M, 3:4], ALU.subtract, ALU.mult, [bst, bsm], [bst])
                k.dve_tt(gv[0:M, :], gv[0:M, :], LNW[0:M, 0, :], ALU.mult, [bst, b_lc], [bst])
                bi = nxt("sbf", 3); vb = sbf[bi]; bvb = b_sbf[bi]
                st["bi"] = bi
                if M == 128:
                    k.dve_tt(vb[:, :], gv[:, :], LNB[:, 0, :], ALU.add, [bst, b_lc], [bvb])
                else:
                    k.dve_tt(gv[0:1, :], gv[0:1, :], LNB[0:1, 0, :], ALU.add, [bst, b_lc], [bst])
                    k.dma(STQ, sguv[l, i_s:i_s + 1, :], gv[0:1, :], [bst], [], pool="st", npool=6, out_final=True)
                    k.dve_tt(jk[0:1, :], gv[0:1, :], W00[0:1, 0, :], ALU.mult, [bst, b_lc], [bjk])
                    k.dve_tt(vb[0:1, :], jk[0:1, :], B00[0:1, 0, :], ALU.add, [bjk, b_lc], [bvb])

            def g1(b=b, M=M, st=g0.__defaults__[-1]):
                bi = st["bi"]; vb = sbf[bi]; bvb = b_sbf[bi]
                bk2 = nxt("pb", 6)
                if M == 128:
                    its = []
                    for g8 in range(8):
                        c4, gg = g8 // 2, g8 % 2
                        o_ = ps[64 * gg:64 * gg + 64, bk2, c4 * 128:(c4 + 1) * 128]
                        its.append((o_, vb[:, g8 * 64:(g8 + 1) * 64], WMT[:, 0, g8, :], True, False))
                        its.append((o_, ones[0:1, 0:64], BSP[0:1, 0, g8, :], False, True))
                    k.mms(its, [bvb, b_c, b_lc], [pb[bk2]])
                    k.dve_tt(ysgu[:, :, b * 128:(b + 1) * 128], ps[:, bk2, :].rearrange("p (c t) -> p c t", c=4), uT[:, :, b * 128:(b + 1) * 128], ALU.mult,
                             [pb[bk2]] + [b_u[oc][b // 4] for oc in range(4)], [b_ys[b // 4]])
                else:
                    i_s = b - 16
                    k.mms([(ps[:, bk2, c4:c4 + 1], vb[0:1, c4 * 128:(c4 + 1) * 128], ones[0:1, 0:1], True, True) for c4 in range(4)], [bvb, b_c], [pb[bk2]])
                    k.dve_tt(ysgu[:, :, T + i_s], ps[:, bk2, 0:4], uT[:, :, T + i_s], ALU.mult, [pb[bk2]] + [b_u[oc][4] for oc in range(4)], [b_ys[4]])
            items.append((g0, g1))
        skew(items, 2)
        if stop_after == "sgu":
            return True
        ada_run(0, upto_layer=l)
        ada_en[0] = False
        barrier()
        b_mg = [[Buf("mg") for _ in range(5)] for _ in range(8)]
        for t4 in range(4):
            wga, bga = wload(w_in[l, :, 5632 + t4 * 256:5632 + (t4 + 1) * 256], 8, 256)
            wgb, bgb = wload(w_in[l, :, 6656 + t4 * 256:6656 + (t4 + 1) * 256], 8, 256)
            wpa, bpa = wload(w_pa[l, :, t4 * 256:(t4 + 1) * 256], 4, 256)
            wps, bps = wload(w_ps[l, :, t4 * 256:(t4 + 1) * 256], 4, 256)
            for oc in range(2):
                c = t4 * 2 + oc
                for reg in regs:
                    c0, n, kind, ri = reg
                    b1, b2, b3, b4 = nxt("pb", 6), nxt("pb", 6), nxt("pb", 6), nxt("pb", 6)
                    osl = slice(oc * 128, (oc + 1) * 128)
                    k.mms([(ps[:, b1, 0:n], wga[:, kc, osl], hT[:, kc, c0:c0 + n], kc == 0, kc == 7) for kc in range(8)], [bga] + b_hT[ri], [pb[b1]])
                    k.mms([(ps[:, b2, 0:n], wgb[:, kc, osl], hT[:, kc, c0:c0 + n], kc == 0, kc == 7) for kc in range(8)], [bgb] + b_hT[ri], [pb[b2]])
                    k.mms([(ps[:, b3, 0:n], wpa[:, kc, osl], yatt[:, kc, c0:c0 + n], kc == 0, kc == 3) for kc in range(4)], [bpa] + b_yatt, [pb[b3]])
                    k.mms([(ps[:, b4, 0:n], wps[:, kc, osl], ysgu[:, kc, c0:c0 + n], kc == 0, kc == 3) for kc in range(4)], [bps, b_ys[ri]], [pb[b4]])
                    s1 = nxt("stg", NSTG); s2 = nxt("stg", NSTG)
                    k.act(stg[s1][:, 0:n], ps[:, b1, 0:n], AF.Sigmoid, [pb[b1]], [b_stg[s1]])
                    k.act(stg[s2][:, 0:n], ps[:, b2, 0:n], AF.Sigmoid, [pb[b2]], [b_stg[s2]])
                    k.dve_tt(stg[s1][:, 0:n], stg[s1][:, 0:n], ps[:, b3, 0:n], ALU.mult, [b_stg[s1], pb[b3]], [b_stg[s1]])
                    k.dve_tt(stg[s2][:, 0:n], stg[s2][:, 0:n], ps[:, b4, 0:n], ALU.mult, [b_stg[s2], pb[b4]], [b_stg[s2]])
                    k.dve_tt(mg[:, c, c0:c0 + n], stg[s1][:, 0:n], stg[s2][:, 0:n], ALU.add, [b_stg[s1], b_stg[s2]], [b_mg[c][ri]])
        if stop_after == "merge":
            return True
        wo = [wload(w_o[l, :, t4 * 256:(t4 + 1) * 256], 8, 256) for t4 in range(4)]
        xload(0)
        xload(1)
        for t in range(len(subs) + 1):
            if t >= 1:
                reg, sub, ns = subs[t - 1]
                norm_b(t - 1, l, 4, 3, reg, sub, ns, xrs[(t - 1) % 2], b_xrs[(t - 1) % 2])
                if t + 1 < len(subs):
                    xload(t + 1)
            if t < len(subs):
                reg, sub, ns = subs[t]
                c0, n, kind, ri = reg
                xr, b_xr = xrs[t % 2], b_xrs[t % 2]
                for c in range(8):
                    wv_, bw_ = wo[c // 2]
                    bk = nxt("pb", 6)
                    k.mms([(ps[:, bk, 0:ns], wv_[:, kc, (c % 2) * 128:(c % 2 + 1) * 128], mg[:, kc, c0 + sub:c0 + sub + ns], kc == 0, kc == 7) for kc in range(8)],
                          [bw_] + [b_mg[kc][ri] for kc in range(8)], [pb[bk]])
                    modmul(xr[:, c, 0:ns], ps[:, bk, 0:ns], l, 2, c, reg, other=xr[:, c, 0:ns], op1=ALU.add, reads=[pb[bk], b_xr[c]], writes=[b_xr[c]])
                k.dma(STQ, xmD[:, :, c0 + sub:c0 + sub + ns], xr[:, :, 0:ns], b_xr, [b_xmD[ri]], pool="st", npool=6)
                norm_a(t, ns, xr, b_xr)
                norm_a2(t, ns)
        if stop_after == "wout":
            return True
        ada_en[0] = True
        barrier()
        for grp in ([regs[0:2], regs[2:]]):
            hoff = grp[0][0]
            b_hd = [[Buf("hd") for _ in range(5)] for _ in range(22)]
            for f2 in range(11):
                wa, ba_ = wload(w_fi[l, :, f2 * 256:(f2 + 1) * 256], 8, 256)
                wb_, bb_ = wload(w_fi[l, :, 2816 + f2 * 256:2816 + (f2 + 1) * 256], 8, 256)
                for oc in range(2):
                    f = f2 * 2 + oc
                    for reg in grp:
                        c0, n, kind, ri = reg
                        b1, b2 = nxt("pb", 6), nxt("pb", 6)
                        osl = slice(oc * 128, (oc + 1) * 128)
                        k.mms([(ps[:, b1, 0:n], wa[:, kc, osl], hT[:, kc, c0:c0 + n], kc == 0, kc == 7) for kc in range(8)], [ba_] + b_hT[ri], [pb[b1]])
                        k.mms([(ps[:, b2, 0:n], wb_[:, kc, osl], hT[:, kc, c0:c0 + n], kc == 0, kc == 7) for kc in range(8)], [bb_] + b_hT[ri], [pb[b2]])
                        s1 = nxt("stg", NSTG)
                        k.act(stg[s1][:, 0:n], ps[:, b1, 0:n], AF.Silu, [pb[b1]], [b_stg[s1]])
                        k.dve_tt(hid[:, f, c0 - hoff:c0 - hoff + n], stg[s1][:, 0:n], ps[:, b2, 0:n], ALU.mult, [b_stg[s1], pb[b2]], [b_hd[f][ri]])
            cr = [(c, reg) for c in range(8) for reg in grp]

            def xmload(j):
                c, reg = cr[j]
                c0, n, kind, ri = reg
                k.dma(S_, xmc[j % 2][:, 0:n], xmD[:, c, c0:c0 + n], [b_xmD[ri]], [b_xmc[j % 2]], pool="xm", npool=2)
            xmload(0)
            wfo = None
            for j, (c, reg) in enumerate(cr):
                c0, n, kind, ri = reg
                if reg is grp[0]:
                    wfo = [wload(w_fo[l, hf * 1408:(hf + 1) * 1408, c * 128:(c + 1) * 128], 11, 128) for hf in range(2)]
                if j + 1 < len(cr):
                    xmload(j + 1)
                xi = j % 2
                bk = nxt("pb", 6)
                k.mms([(ps[:, bk, 0:n], wfo[kc // 11][0][:, kc % 11, :], hid[:, kc, c0 - hoff:c0 - hoff + n], kc == 0, kc == 21) for kc in range(22)],
                      [wfo[0][1], wfo[1][1]] + [b_hd[kc][ri] for kc in range(22)], [pb[bk]])
                modmul(xmc[xi][:, 0:n], ps[:, bk, 0:n], l, 5, c, reg, other=xmc[xi][:, 0:n], op1=ALU.add, reads=[pb[bk], b_xmc[xi]], writes=[b_xmc[xi]])
                if final:
                    dst = ysT[:, c, 0:4] if kind == "s" else yT[:, c, c0:c0 + n]
                    k.dma(STQ, dst, xmc[xi][:, 0:n], [b_xmc[xi]], [], pool="st", npool=6, out_final=True)
                else:
                    k.dma(STQ, xdst[:, c, c0:c0 + n], xmc[xi][:, 0:n], [b_xmc[xi]], [bxd[ri]], pool="st", npool=6)

    b_accb = k.bufs("acc", 2)
    b_QT = k.bufs("QT", 16); b_KT = k.bufs("KT", 16); b_V = k.bufs("V", 16)
    b_yatt = k.bufs("yatt", 4)
    for l_ in range(2):
        ada_run(0, upto_layer=l_ - 1)
        if l_ == 1:
            ada_run(8, drain=True)
        if run_pass(l_) or stop_after == "pass%d" % l_:
            break
    k.finish()
    return k


def _fm(a):
    n = a.shape[0]
    return np.ascontiguousarray(a.T.reshape(8, 128, n).transpose(1, 0, 2))


def _consts():
    hh = np.arange(1, 13, dtype=np.float32)
    slopes = np.power(np.float32(2.0), -8.0 * hh / 12).astype(np.float32).reshape(3, 4)
    kk = np.arange(128)[:, None].astype(np.float64)
    qq = np.arange(128)[None, :].astype(np.float64)
    E = np.zeros((128, 12, 2, 128), np.float32)
    AL = np.zeros((128, 12), np.float32)
    for g in range(3):
        for h in range(4):
            s = float(slopes[g, h]) * DILS[g]
            E[:, g * 4 + h, 0, :] = np.where(kk >= qq, np.exp(-s * (128 + qq - kk)), 0.0)
            E[:, g * 4 + h, 1, :] = np.where(kk <= qq, np.exp(-s * (qq - kk)), 0.0)
            AL[:, g * 4 + h] = -s * (128 - np.arange(128))
    tril = (np.arange(128)[None, :] <= np.arange(128)[:, None]).astype(np.float32)
    return dict(ident=np.eye(128).astype(ml_dtypes.bfloat16), ones=np.ones((128, 128), ml_dtypes.bfloat16),
                E=E.astype(ml_dtypes.bfloat16), AL=AL, tril=tril)


_CACHE = {}


def kernel(x_prompt, x_sample, cache_kv_w128, cache_kv_w512, cache_kv_w2048, c_prompt, c_sample,
           w_ada, b_ada, norm1_w, w_in, q_norm_w, k_norm_w, sgu_ln_w, sgu_ln_b, w_spatial, b_spatial,
           w_proj_att, w_proj_sgu, w_out, norm2_w, w_ffn_in, w_ffn_out):
    f = lambda a: np.ascontiguousarray(np.asarray(a, dtype=np.float32))
    x_prompt, x_sample = f(x_prompt), f(x_sample)
    caches = [f(cache_kv_w128), f(cache_kv_w512), f(cache_kv_w2048)]
    c_prompt, c_sample = f(c_prompt), f(c_sample)
    if "k" not in _CACHE:
        _CACHE["k"] = build()
    kk = _CACHE["k"]
    shared = dict(w_ada=f(w_ada), w_in=f(w_in), q_norm_w=f(q_norm_w), k_norm_w=f(k_norm_w), sgu_ln_w=f(sgu_ln_w), sgu_ln_b=f(sgu_ln_b),
                  w_spatial=f(w_spatial), b_spatial=f(b_spatial), w_proj_att=f(w_proj_att), w_proj_sgu=f(w_proj_sgu), w_out=f(w_out),
                  w_ffn_in=f(w_ffn_in), w_ffn_out=f(w_ffn_out))
    shared["b_adaT"] = np.ascontiguousarray(f(b_ada).reshape(2, 48, 128).transpose(2, 0, 1))
    shared["norm1T"] = np.ascontiguousarray(f(norm1_w).reshape(2, 8, 128).transpose(2, 0, 1))
    shared["norm2T"] = np.ascontiguousarray(f(norm2_w).reshape(2, 8, 128).transpose(2, 0, 1))
    shared.update(_consts())
    in_maps = []
    for c in range(8):
        b, r = c // 2, c % 2
        m = dict(shared)
        m["xY"] = _fm(x_prompt[b, r * T:(r + 1) * T])
        m["xS"] = _fm(x_sample[4 * c:4 * c + 4, 0, :])
        cc = np.zeros((8, 1024), np.float32)
        cc[0] = c_prompt[b]
        cc[1:5] = c_sample[4 * c:4 * c + 4]
        m["cT"] = _fm(cc)
        m["hmask"] = np.full((128, 1), float(r), np.float32)
        for g in range(3):
            m["cache%d" % g] = np.ascontiguousarray(caches[g][:, 4 * c:4 * c + 4])
        in_maps.append(m)
    res = run_bass_kernel_spmd(kk.nc, in_maps, core_ids=list(range(8)))
    R_ = res.results
    y_prompt = np.zeros((4, 4096, 1024), np.float32)
    y_sample = np.zeros((32, 1, 1024), np.float32)
    kvp = [np.zeros((2, 4, WINS[g], 2, 4, 128), np.float32) for g in range(3)]
    kvs = [np.zeros((2, 32, 1, 2, 4, 128), np.float32) for g in range(3)]
    sguv = np.zeros((2, 32, 1, 512), np.float32)
    for c in range(8):
        b, r = c // 2, c % 2
        o = R_[c]
        y_prompt[b, r * T:(r + 1) * T] = np.asarray(o["yT"]).transpose(1, 0, 2).reshape(1024, T).T
        y_sample[4 * c:4 * c + 4, 0] = np.asarray(o["ysT"]).transpose(1, 0, 2).reshape(1024, 4).T
        for g in range(3):
            if r == 1:
                kvp[g][:, b] = np.asarray(o["kvp%d" % g])
            kvs[g][:, 4 * c:4 * c + 4, 0] = np.asarray(o["kvs%d" % g])
        sguv[:, 4 * c:4 * c + 4, 0] = np.asarray(o["sguv"])
    return (y_prompt, y_sample, kvp[0], kvp[1], kvp[2], kvs[0], kvs[1], kvs[2], sguv)
```

```python
import contextlib
import numpy as np
import ml_dtypes
import concourse.bass as bass
import concourse.mybir as mybir
from concourse.bass_utils import run_bass_kernel_spmd

F32 = mybir.dt.float32
BF16 = mybir.dt.bfloat16
AF = mybir.ActivationFunctionType
ALU = mybir.AluOpType
AX = mybir.AxisListType
EPOCH = 12000
SAME_ENGINE_WAITS = True


class Buf:
    __slots__ = ("name", "w", "r")

    def __init__(self, name):
        self.name = name
        self.w = None
        self.r = {}


class Eng:
    def __init__(self, k, name):
        self.k = k
        self.name = name
        self.prog = []
        self.sem = None
        self.cnt = 0
        self.waited = {}
        self.own = set()

    def wait(self, ev):
        if ev is None:
            return
        sem, val = ev
        if sem.num in self.own and (self.name == "tensor" or not SAME_ENGINE_WAITS):
            return
        if self.waited.get(sem.num, 0) >= val:
            return
        self.waited[sem.num] = val
        self.prog.append(lambda e, sem=sem, val=val: e.wait_ge(sem, val))

    def emit(self, fn, inc=True):
        if not inc:
            self.prog.append(fn)
            return None
        if self.sem is None or self.cnt >= EPOCH:
            self.sem = self.k.new_sem()
            self.own.add(self.sem.num)
            self.cnt = 0
        self.cnt += 1
        sem = self.sem
        self.prog.append(lambda e, fn=fn, sem=sem: fn(e).then_inc(sem, 1))
        return (sem, self.cnt)


class K:
    def __init__(self):
        self.nc = bass.Bass("TRN2", target_bir_lowering=False)
        self.st = contextlib.ExitStack()
        self.e = {n: Eng(self, n) for n in ("tensor", "vector", "scalar", "gpsimd", "sync")}
        self.nsem = 0
        self.dpools = {}
        self.out_events = []
        self.psum_f32 = self.st.enter_context(self.nc.psum_tensor("psf", [128, 6, 512], F32))[:, :, :]
        self.psum_bf = self.st.enter_context(self.nc.psum_tensor("psb", [128, 2, 1024], BF16))[:, :, :]
        self.pbuf = [Buf("ps%d" % i) for i in range(6)]
        self.pbuf_bf = Buf("psb")

    def new_sem(self):
        self.nsem += 1
        return self.st.enter_context(self.nc.semaphore("s%d" % self.nsem))

    def din(self, name, shape, dt=F32):
        return self.nc.dram_tensor(name, list(shape), dt, kind="ExternalInput").ap()

    def dout(self, name, shape, dt=F32):
        return self.nc.dram_tensor(name, list(shape), dt, kind="ExternalOutput").ap()

    def dscratch(self, name, shape, dt=F32):
        return self.nc.dram_tensor(name, list(shape), dt, kind="Internal").ap()

    def sb(self, name, shape, dt):
        h = self.st.enter_context(self.nc.sbuf_tensor(name, list(shape), dt))
        return h[tuple(slice(None) for _ in shape)]

    def buf(self, name=""):
        return Buf(name)

    def bufs(self, name, n):
        return [Buf("%s%d" % (name, i)) for i in range(n)]

    def _access(self, E, reads, writes):
        for b in reads:
            E.wait(b.w)
        for b in writes:
            E.wait(b.w)
            for ev in list(b.r.values()):
                E.wait(ev)

    def _done(self, ev, reads, writes):
        sem, val = ev
        for b in reads:
            cur = b.r.get(sem.num)
            if cur is None or cur[1] < val:
                b.r[sem.num] = ev
        for b in writes:
            b.w = ev
            b.r = {}

    def op(self, eng, method, reads, writes, *args, **kw):
        E = self.e[eng]
        self._access(E, reads, writes)
        ev = E.emit(lambda e: getattr(e, method)(*args, **kw))
        self._done(ev, reads, writes)

    def mms(self, items, reads, writes):
        E = self.e["tensor"]
        self._access(E, reads, writes)
        n = len(items)
        ev = None
        for i, (o, l, r, s0, s1) in enumerate(items):
            fn = (lambda e, o=o, l=l, r=r, s0=s0, s1=s1: e.matmul(o, l, r, start=s0, stop=s1))
            ev = E.emit(fn, inc=(i == n - 1))
        self._done(ev, reads, writes)

    def mm(self, out, lhsT, rhs, start, stop, reads, writes):
        self.mms([(out, lhsT, rhs, start, stop)], reads, writes)

    def tr(self, out, in_, ident, reads, writes):
        E = self.e["tensor"]
        self._access(E, reads, writes)
        ev = E.emit(lambda e: e.transpose(out, in_, ident))
        self._done(ev, reads, writes)

    def trs(self, items, reads, writes):
        E = self.e["tensor"]
        self._access(E, reads, writes)
        n = len(items)
        ev = None
        for i, (o, a, idn) in enumerate(items):
            ev = E.emit(lambda e, o=o, a=a, idn=idn: e.transpose(o, a, idn), inc=(i == n - 1))
        self._done(ev, reads, writes)

    def act(self, out, in_, func, reads, writes, **kw):
        self.op("scalar", "activation", reads, writes, out, in_, func, **kw)

    def dve_ts(self, out, in0, s1, s2, op0, op1, reads, writes, eng="vector"):
        if op1 is None:
            self.op(eng, "tensor_scalar", reads, writes, out, in0, s1, None, op0)
        else:
            self.op(eng, "tensor_scalar", reads, writes, out, in0, s1, s2, op0, op1)

    def dve_tt(self, out, in0, in1, op, reads, writes, eng="vector"):
        self.op(eng, "tensor_tensor", reads, writes, out, in0, in1, op)

    def dve_stt(self, out, in0, scalar, in1, op0, op1, reads, writes, eng="vector"):
        self.op(eng, "scalar_tensor_tensor", reads, writes, out, in0, scalar, in1, op0, op1)

    def dve_copy(self, out, in_, reads, writes, eng="vector"):
        self.op(eng, "tensor_copy", reads, writes, out, in_)

    def dma(self, q, out, in_, reads, writes, pool=None, npool=4, out_final=False, **kw):
        E = self.e[q]
        pool = pool or (q + "_d")
        if pool not in self.dpools:
            self.dpools[pool] = [[self.new_sem(), 0] for _ in range(npool)] + [0]
        P = self.dpools[pool]
        slot = P[P[-1] % (len(P) - 1)]
        P[-1] += 1
        self._access(E, reads, writes)
        sem, tot = slot
        if tot:
            E.wait((sem, tot))
        slot[1] = tot + 16
        E.emit(lambda e: e.dma_start(out=out, in_=in_, **kw).then_inc(sem, 16), inc=False)
        ev = (sem, tot + 16)
        self._done(ev, reads, writes)
        if out_final:
            self.out_events.append(ev)
        return ev

    def cc_allgather(self, in_ap, out_ap, reads, writes, groups):
        E = self.e["gpsimd"]
        self._access(E, reads, writes)
        sem = self.new_sem()
        E.emit(lambda e: e.collective_compute("AllGather", ALU.bypass, replica_groups=groups, ins=[in_ap], outs=[out_ap]).then_inc(sem), inc=False)
        self._done((sem, 1), reads, writes)

    def finish(self):
        S = self.e["sync"]
        for ev in self.out_events:
            S.wait(ev)
        for n in ("tensor", "vector", "scalar"):
            E = self.e[n]
            if E.sem is not None:
                S.wait((E.sem, E.cnt))
        with self.nc.Block() as block:
            for name in ("tensor", "vector", "scalar", "gpsimd", "sync"):
                prog = self.e[name].prog

                def run(e, prog=prog):
                    for fn in prog:
                        fn(e)
                getattr(block, name)(run)
        self.st.close()


T = 2048
TS = T + 4
DILS = (1, 4, 16)
WINS = (128, 512, 2048)
SCALE = 128.0 ** -0.5
EPS = 1e-6
NWS = 8
STQ = "sync"


def block_cols(g, b):
    if g == 0:
        return (b * 128, (b + 1) * 128, 1)
    if g == 1:
        j, r = b // 4, b % 4
        return (j * 512 + r, (j + 1) * 512, 4)
    return (b, T, 16)


SKEW_D = 2


def skew(items, d):
    d = min(d, SKEW_D)
    n = len(items)
    for t in range(n + d):
        if t < n:
            items[t][0]()
        if t - d >= 0 and items[t - d][1] is not None:
            items[t - d][1]()


def build(stop_after=None):
    k = K()
    xY = k.din("xY", [128, 8, T]); xS = k.din("xS", [128, 8, 4])
    cT = k.din("cT", [128, 8, 8]); hmask = k.din("hmask", [128, 1])
    ident_d = k.din("ident", [128, 128], BF16); ones_d = k.din("ones", [128, 128], BF16)
    E_d = k.din("E", [128, 12, 2, 128], BF16); AL_d = k.din("AL", [128, 12]); tril_d = k.din("tril", [128, 128])
    w_ada = k.din("w_ada", [2, 1024, 6144]); b_adaT = k.din("b_adaT", [128, 2, 48])
    norm1T = k.din("norm1T", [128, 2, 8]); norm2T = k.din("norm2T", [128, 2, 8])
    w_in = k.din("w_in", [2, 1024, 7680])
    qnw = k.din("q_norm_w", [2, 3, 128]); knw = k.din("k_norm_w", [2, 3, 128])
    lnw = k.din("sgu_ln_w", [2, 512]); lnb = k.din("sgu_ln_b", [2, 512])
    w_sp = k.din("w_spatial", [2, 8, 128, 128]); b_sp = k.din("b_spatial", [2, 8, 128])
    w_pa = k.din("w_proj_att", [2, 512, 1024]); w_ps = k.din("w_proj_sgu", [2, 512, 1024])
    w_o = k.din("w_out", [2, 1024, 1024]); w_fi = k.din("w_ffn_in", [2, 1024, 5632]); w_fo = k.din("w_ffn_out", [2, 2816, 1024])
    cch = [k.din("cache%d" % g, [2, 4, WINS[g], 2, 4, 128]) for g in range(3)]
    yT = k.dout("yT", [128, 8, T]); ysT = k.dout("ysT", [128, 8, 4])
    kvp = [k.dout("kvp%d" % g, [2, WINS[g], 2, 4, 128]) for g in range(3)]
    kvs = [k.dout("kvs%d" % g, [2, 4, 2, 4, 128]) for g in range(3)]
    sguv = k.dout("sguv", [2, 4, 512])
    x1Y = k.dscratch("x1Y", [128, 8, TS]); xmD = k.dscratch("xmD", [128, 8, TS])
    b_x1Y = k.bufs("x1Y", 5); b_xmD = k.bufs("xmD", 5)
    NSL = {"A": 5, "B0": 4, "B1": 4, "B2": 4, "B3": 4}
    hs = {(l, ab): k.dscratch("hs%d%s" % (l, ab), [256, NSL[ab] * 512], BF16) for l in range(2) for ab in NSL}
    hr = {(l, ab): k.dscratch("hr%d%s" % (l, ab), [512, NSL[ab] * 512], BF16) for l in range(2) for ab in NSL}
    b_hs = {(l, ab): k.bufs("hs", 2 * 2 * NSL[ab]) for l in range(2) for ab in NSL}
    b_hr = {(l, ab): k.buf("hr") for l in range(2) for ab in NSL}
    PAIRS = [[0, 1], [2, 3], [4, 5], [6, 7]]
    pending = []

    def hslot(g, b):
        if g == 0:
            return "A", 0
        if g == 1:
            return "A", 1 + b - 12
        return "B%d" % (b // 4), b % 4

    RB = 73760
    R = k.sb("R", [128, RB // 2], BF16)

    def view(off, shape, dt):
        n = int(np.prod(shape)) * (4 if dt == F32 else 2)
        v = R[:, off // 2:(off + n) // 2]
        if dt == F32:
            v = v.bitcast(F32)
        names = " ".join("d%d" % i for i in range(len(shape)))
        if len(shape) > 1:
            v = v.rearrange("p (%s) -> p %s" % (names, names), **{"d%d" % i: s for i, s in enumerate(shape)})
        return v
    yatt = view(0, [4, TS], BF16)
    acc = view(16416, [2, 2, T], F32)
    QT = view(49184, [2, 16, 128], BF16); KT = view(49184 + 8192, [2, 16, 128], BF16); Vt = view(49184 + 16384, [16, 256], BF16)
    ysgu = view(16416, [4, TS], BF16); uT = view(32832, [4, TS], BF16); mg = view(32832, [8, TS], BF16)
    hid = view(0, [22, 1028], BF16)
    hT = k.sb("hT", [128, 8, TS], BF16)
    WS = [k.sb("ws%d" % i, [128, 2048], BF16) for i in range(NWS)]
    b_w = k.bufs("w", NWS)
    xrs = [k.sb("xr%d" % i, [128, 8, 256], F32) for i in range(2)]; b_xrs = [k.bufs("xr%d_" % i, 8) for i in range(2)]
    sqt = k.sb("sqt", [128, 8, 256], BF16); b_sq = k.bufs("sq", 8)
    tmpb = sqt; b_tmpb = b_sq
    ident = k.sb("identS", [128, 128], BF16); ones = k.sb("onesS", [128, 128], BF16)
    Et = k.sb("Et", [128, 12, 2, 128], BF16)
    ALt = k.sb("ALt", [128, 12], F32); tril = k.sb("trilS", [128, 128], F32); hm = k.sb("hm", [128, 1], F32)
    MOD = k.sb("MOD", [128, 2, 6, 8, 8], F32); b_mod = [k.bufs("mod%d_" % l, 6) for l in range(2)]
    cst = k.sb("cst", [128, 8, 8], F32); csb = k.sb("csb", [128, 8, 8], BF16)
    bad = k.sb("bad", [128, 2, 48], F32); n1t = k.sb("n1t", [128, 2, 8], F32); n2t = k.sb("n2t", [128, 2, 8], F32)
    QW = k.sb("QW", [128, 1, 3, 128], F32); KW = k.sb("KW", [128, 1, 3, 128], F32)
    LNW = k.sb("LNW", [128, 1, 512], F32); LNB = k.sb("LNB", [128, 1, 512], F32)
    WMT = k.sb("WMT", [128, 1, 8, 128], BF16); BSP = k.sb("BSP", [1, 1, 8, 128], BF16)
    W00 = k.sb("W00", [1, 1, 512], F32); B00 = k.sb("B00", [1, 1, 512], F32)
    b_lc = k.buf("lc")
    w8 = k.sb("w8", [1, 16], F32); b_w8 = k.buf("w8")
    b_c = k.buf("consts")
    NSTG = 3
    stg = [k.sb("stg%d" % i, [128, 512], F32) for i in range(NSTG)]; b_stg = k.bufs("stg", NSTG)
    rsts = [stg[i][:, 0:256] for i in range(2)]; b_rsts = [b_stg[i] for i in range(2)]
    sbf = [k.sb("sbf%d" % i, [128, 512], BF16) for i in range(3)]; b_sbf = k.bufs("sbf", 3)
    hKt = [k.sb("hK%d" % i, [128, 2, 128], BF16) for i in range(3)]; hVt = [k.sb("hV%d" % i, [128, 256], BF16) for i in range(3)]
    b_hK = k.bufs("hK", 3); b_hV = k.bufs("hV", 3)
    sm = [k.sb("sm%d" % i, [128, 16], F32) for i in range(6)]; b_sm = k.bufs("sm", 6)
    xmc = [k.sb("xmc%d" % i, [128, 512], F32) for i in range(2)]; b_xmc = k.bufs("xmc", 2)
    sQb = k.sb("sQb", [1, 4, 256], BF16); sKb = k.sb("sKb", [1, 4, 256], BF16); sVb = k.sb("sVb", [1, 4, 256], BF16)
    b_sQ = k.bufs("sQ", 4); b_sK = k.bufs("sK", 4); b_sV = k.bufs("sV", 4)
    cK = [xmc[0].rearrange("p (a b d) -> p a b d", a=2, b=2)]; b_cK = [b_xmc[0]]
    cVb = k.sb("cVb", [128, 2, 128], BF16); b_cVb = k.buf("cVb")
    sacc = k.sb("sacc", [128, 4, 4], F32); b_sacc = k.bufs("sacc", 4)
    psm = k.sb("psm", [128, 4], BF16); b_psm = k.buf("psm")
    pself = k.sb("pself", [1, 4], BF16); b_pself = k.buf("pself")

    ps = k.psum_f32; pb = k.pbuf; psb = k.psum_bf
    pbq = k.bufs("psbq", 2)
    cnt = {}

    def nxt(name, n):
        i = cnt.get(name, 0) % n
        cnt[name] = cnt.get(name, 0) + 1
        return i

    def barrier():
        evs = []
        for n in ("tensor", "vector", "scalar"):
            E = k.e[n]
            if E.sem is not None:
                evs.append((E.sem, E.cnt))
        for P in k.dpools.values():
            for sem, tot in P[:-1]:
                if tot:
                    evs.append((sem, tot))
        for n in ("vector", "scalar"):
            for ev in evs:
                k.e[n].wait(ev)

    def wload(src, kc, n, ada_ok=True):
        assert kc * n <= 2048
        i = nxt("w", NWS)
        v = WS[i][:, 0:kc * n].rearrange("p (a b) -> p a b", a=kc)
        k.dma("gpsimd", v, src.rearrange("(a p) n -> p a n", p=128), [], [b_w[i]], pool="w", npool=NWS)
        if ada_ok and ada_en[0] and ada_jobs:
            cnt["wm"] = cnt.get("wm", 0) + 1
            if cnt["wm"] % 3 == 0:
                ada_run(1)
        if pending:
            pending[0] -= 1
            if pending[0] <= 0:
                for fn in pending[1:]:
                    fn()
                del pending[:]
        return v, b_w[i]

    S_ = "sync"
    for dst, src in ((ident, ident_d), (ones, ones_d), (Et, E_d), (ALt, AL_d), (tril, tril_d), (hm, hmask), (cst, cT),
                     (bad, b_adaT), (n1t, norm1T), (n2t, norm2T)):
        k.dma(S_, dst, src, [], [Buf("c")], pool="cst", npool=10)
    for ev_ in [(sem_, tot_) for sem_, tot_ in k.dpools["cst"][:-1] if tot_]:
        for n_ in ("tensor", "vector", "scalar"):
            k.e[n_].wait(ev_)
    wst = xrs[0][:, :, 0:128]; wsb = sqt[:, :, 0:128]

    def load_layer_consts(l):
        for g in range(3):
            k.dma(S_, QW[:, 0, g, :], qnw[l, g:g + 1, :].partition_broadcast(128), [], [b_lc])
            k.dma(S_, KW[:, 0, g, :], knw[l, g:g + 1, :].partition_broadcast(128), [], [b_lc])
        k.dma(S_, LNW[:, 0, :], lnw[l:l + 1, :].partition_broadcast(128), [], [b_lc])
        k.dma(S_, LNB[:, 0, :], lnb[l:l + 1, :].partition_broadcast(128), [], [b_lc])
        k.dma("gpsimd", BSP[0:1, 0, :, :], b_sp[l:l + 1, :, :], [], [b_lc], pool="w", npool=NWS)
        k.dma(S_, w8[0:1, 0:8].rearrange("p (g o) -> p g o", o=1), b_sp[l:l + 1, :, 0:1], [], [b_w8], allow_slow_non_contiguous=True)
        k.dma(S_, w8[0:1, 8:16].rearrange("p (g o) -> p g o", o=1), w_sp[l:l + 1, :, 0, 0:1], [], [b_w8], allow_slow_non_contiguous=True)
        k.dve_copy(B00[0:1, 0, :].rearrange("p (g c) -> p g c", g=8), w8[0:1, 0:8].unsqueeze(2).to_broadcast([1, 8, 64]), [b_w8], [b_lc])
        k.dve_copy(W00[0:1, 0, :].rearrange("p (g c) -> p g c", g=8), w8[0:1, 8:16].unsqueeze(2).to_broadcast([1, 8, 64]), [b_w8], [b_lc])
        k.dma(S_, wst, w_sp[l].rearrange("g t s -> t g s"), [], b_xrs[0])
        for g8 in range(8):
            k.dve_tt(wsb[:, g8, :], wst[:, g8, :], tril, ALU.mult, b_xrs[0] + [b_c], [b_sq[g8]])
        k.trs([(psb[:, 0, g8 * 128:(g8 + 1) * 128], wsb[:, g8, :], ident) for g8 in range(8)], b_sq + [b_c], [pbq[0]])
        k.dve_copy(WMT[:, 0].rearrange("p g t -> p (g t)"), psb[:, 0, :], [pbq[0]], [b_lc])

    k.act(csb, cst, AF.Silu, [b_c], [b_c])
    ada_jobs = []
    ada_en = [True]

    def mk_ada(l, t):
        def job():
            wv, bw = wload(w_ada[l, :, t * 256:(t + 1) * 256], 8, 256, ada_ok=False)
            for oc in range(2):
                j = t * 2 + oc
                bk = nxt("pb", 6)
                k.mms([(ps[:, bk, 0:8], wv[:, kc, oc * 128:(oc + 1) * 128], csb[:, kc, :], kc == 0, kc == 7) for kc in range(8)],
                      [bw, b_c], [pb[bk]])
                k.dve_ts(MOD[:, l, j // 8, j % 8, :], ps[:, bk, 0:8], bad[:, l, j:j + 1], None, ALU.add, None, [pb[bk], b_c], [b_mod[l][j // 8]])
            if t in (7, 19):
                a_ = 1 if t == 7 else 4
                nt = n1t if t == 7 else n2t
                for kc in range(8):
                    k.dve_ts(MOD[:, l, a_, kc, :], MOD[:, l, a_, kc, :], 1.0, nt[:, l, kc:kc + 1], ALU.add, ALU.mult, [b_mod[l][a_], b_c], [b_mod[l][a_]])
        return job
    for l in range(2):
        for t in range(24):
            ada_jobs.append((l, mk_ada(l, t)))

    def ada_run(n=1, upto_layer=None):
        while ada_jobs and (n > 0 or (upto_layer is not None and ada_jobs[0][0] <= upto_layer)):
            ada_jobs.pop(0)[1]()
            n -= 1
    ada_run(8)

    REG_P = [(j * 512, 512, "p", j) for j in range(4)]
    REG_S = (T, 4, "s", 4)
    b_hT = [k.bufs("hT%d_" % j, 8) for j in range(5)]

    def modmul(dst, src, l, a, kc, reg, other=None, op1=None, eng="vector", reads=(), writes=()):
        c0, n, kind, ri = reg
        rd = list(reads) + [b_mod[l][a]]
        if kind == "p":
            sc = MOD[:, l, a, kc, 0:1]
            if other is None:
                k.dve_ts(dst, src, sc, None, ALU.mult, None, rd, list(writes), eng=eng)
            else:
                k.dve_stt(dst, src, sc, other, ALU.mult, op1, rd, list(writes), eng=eng)
        else:
            mt = MOD[:, l, a, kc, 1:5]
            if other is None:
                k.dve_tt(dst, src, mt, ALU.mult, rd, list(writes), eng=eng)
            else:
                i = nxt("sm", 6)
                k.dve_tt(sm[i][:, 0:4], src, mt, ALU.mult, rd, [b_sm[i]], eng=eng)
                k.dve_tt(dst, sm[i][:, 0:4], other, op1, [b_sm[i]] + list(reads), list(writes), eng=eng)

    def norm_a(i, nsub, xr, b_xr):
        rst, b_rst = rsts[i % 2], b_rsts[i % 2]
        for kc in range(8):
            k.act(sqt[:, kc, 0:nsub], xr[:, kc, 0:nsub], AF.Square, [b_xr[kc]], [b_sq[kc]])
        bk = nxt("pb", 6)
        k.mms([(ps[:, bk, 0:nsub], ones, sqt[:, kc, 0:nsub], kc == 0, kc == 7) for kc in range(8)], b_sq + [b_c], [pb[bk]])
        k.act(rst[:, 0:nsub], ps[:, bk, 0:nsub], AF.Sqrt, [pb[bk]], [b_rst], scale=1.0 / 1024, bias=EPS)

    def norm_a2(i, nsub):
        rst, b_rst = rsts[i % 2], b_rsts[i % 2]
        k.op("vector", "reciprocal", [b_rst], [b_rst], rst[:, 0:nsub], rst[:, 0:nsub])

    def norm_b(i, l, a, bidx, reg, sub, nsub, xr, b_xr):
        c0, n, kind, ri = reg
        rst, b_rst = rsts[i % 2], b_rsts[i % 2]
        for kc in range(8):
            xv = xr[:, kc, 0:nsub]
            tv = tmpb[:, kc, 0:nsub]
            k.dve_tt(tv, xv, rst[:, 0:nsub], ALU.mult, [b_xr[kc], b_rst], [b_tmpb[kc]])
        for kc in range(8):
            tv = tmpb[:, kc, 0:nsub]
            hv = hT[:, kc, c0 + sub:c0 + sub + nsub]
            if kind == "p":
                k.act(hv, tv, AF.Identity, [b_tmpb[kc], b_mod[l][a], b_mod[l][bidx]], [b_hT[ri][kc]],
                      scale=MOD[:, l, a, kc, 0:1], bias=MOD[:, l, bidx, kc, 0:1])
            else:
                k.dve_tt(tv, tv, MOD[:, l, a, kc, 1:5], ALU.mult, [b_tmpb[kc], b_mod[l][a]], [b_tmpb[kc]])
                k.dve_tt(hv, tv, MOD[:, l, bidx, kc, 1:5], ALU.add, [b_tmpb[kc], b_mod[l][bidx]], [b_hT[ri][kc]])

    def subregs(reg):
        c0, n, kind, ri = reg
        return [(0, 256), (256, 256)] if kind == "p" else [(0, 4)]

    def run_pass(l):
        isY = True
        regs = REG_P + [REG_S]
        if l == 0:
            def xsrc(reg, sub, ns):
                c0, n, kind, ri = reg
                if kind == "s":
                    return xS[:, :, 0:4], []
                return xY[:, :, c0 + sub:c0 + sub + ns], []
        else:
            def xsrc(reg, sub, ns):
                c0, n, kind, ri = reg
                return x1Y[:, :, c0 + sub:c0 + sub + ns], [b_x1Y[ri]]
        xdst, bxd = (x1Y, b_x1Y)
        final = l == 1
        barrier()
        load_layer_consts(l)
        subs = [(reg, sub, ns) for reg in regs for sub, ns in subregs(reg)]

        def xload(i):
            reg, sub, ns = subs[i]
            src, rb = xsrc(reg, sub, ns)
            k.dma(S_, xrs[i % 2][:, :, 0:ns], src, rb, b_xrs[i % 2])
        xload(0)
        xload(1)
        for t in range(len(subs) + 1):
            if t >= 1:
                reg, sub, ns = subs[t - 1]
                norm_b(t - 1, l, 1, 0, reg, sub, ns, xrs[(t - 1) % 2], b_xrs[(t - 1) % 2])
                if t + 1 < len(subs):
                    xload(t + 1)
            if t < len(subs):
                reg, sub, ns = subs[t]
                norm_a(t, ns, xrs[t % 2], b_xrs[t % 2])
                norm_a2(t, ns)
        bh_all = [b for j in range(5) for b in b_hT[j]]
        if stop_after == "N1":
            return True
        def sweep(kv_only):
            for hp in range(2):
                for g in range(3):
                    dil = DILS[g]
                    if kv_only:
                        blocks = {0: [15], 1: [12, 13, 14, 15], 2: list(range(16))}[g]
                    else:
                        blocks = list(range(16))
                    items = []
                    for part in ((1, 2) if kv_only else (0, 1, 2)):
                        col0 = part * 1536 + g * 512 + hp * 256
                        blist = [(b, 128) for b in blocks] + ([(16 + i, 1) for i in range(4)] if not kv_only else [])
                        wref = {}
                        for bi_, (b, M) in enumerate(blist):
                            def s0(part=part, b=b, M=M, col0=col0, first=(bi_ == 0), wref=wref, st={}):
                                if first:
                                    wref["w"] = wload(w_in[l, :, col0:col0 + 256], 8, 256)
                                wv, bw = wref["w"]
                                nw = (QW, KW, None)[part]
                                if M == 128:
                                    s0_, s1_, st_ = block_cols(g, b)
                                    lc = lambda kc: hT[:, kc, s0_:s1_:st_]
                                    rdh = bh_all[0:32]
                                else:
                                    i_s = b - 16
                                    lc = lambda kc: hT[:, kc, T + i_s:T + i_s + 1]
                                    rdh = b_hT[4]
                                bk = nxt("pb", 6)
                                k.mms([(ps[0:M, bk, 0:256], lc(kc), wv[:, kc, :], kc == 0, kc == 7) for kc in range(8)], [bw] + rdh, [pb[bk]])
                                pv = ps[0:M, bk, 0:256]
                                si = nxt("stg", NSTG); st4 = stg[si]; bst = b_stg[si]
                                if part < 2:
                                    mi = nxt("sm", 6); smv = sm[mi]; bsm = b_sm[mi]
                                    for h2 in range(2):
                                        k.act(st4[0:M, 256 + h2 * 128:256 + (h2 + 1) * 128], pv[:, h2 * 128:(h2 + 1) * 128], AF.Square, [pb[bk]], [bst, bsm],
                                              accum_out=smv[0:M, h2:h2 + 1])
                                    k.act(smv[0:M, 0:2], smv[0:M, 0:2], AF.Sqrt, [bsm], [bsm], scale=1.0 / 128, bias=EPS)
                                    k.op("vector", "reciprocal", [bsm], [bsm], smv[0:M, 0:2], smv[0:M, 0:2])
                                    for h2 in range(2):
                                        k.dve_stt(st4[0:M, h2 * 128:(h2 + 1) * 128], pv[:, h2 * 128:(h2 + 1) * 128], smv[0:M, h2:h2 + 1],
                                                  nw[0:M, 0, g, :], ALU.mult, ALU.mult, [pb[bk], bsm, b_lc], [bst])
                                else:
                                    k.act(st4[0:M, 0:256], pv, AF.Copy, [pb[bk]], [bst])
                                if M == 1:
                                    dstt, bd = ((sQb, b_sQ), (sKb, b_sK), (sVb, b_sV))[part]
                                    k.dve_copy(dstt[0:1, i_s, :], st4[0:1, 0:256], [bst], [bd[i_s]])
                                    if part > 0:
                                        k.dma(STQ, kvs[g][l, i_s:i_s + 1, part - 1, hp * 2:hp * 2 + 2, :],
                                              st4[0:1, 0:256].rearrange("p (h d) -> p h d", h=2), [bst], [], pool="st", npool=6, out_final=True)
                                    return
                                if part > 0 and not kv_only:
                                    rows = None
                                    if g == 0 and b == 15:
                                        rows = (0, 128, 1)
                                    elif g == 1 and b >= 12:
                                        rows = (b % 4, 512, 4)
                                    elif g == 2:
                                        rows = (b, 2048, 16)
                                    if rows is not None:
                                        k.dma(STQ, kvp[g][l, rows[0]:rows[1]:rows[2], part - 1, hp * 2:hp * 2 + 2, :],
                                              st4[:, 0:256].rearrange("p (h d) -> p h d", h=2), [bst], [], pool="st", npool=6, out_final=True)
                                if part == 2:
                                    k.dve_copy(Vt[:, b, :], st4[:, 0:256], [bst], [b_V[b]])
                                else:
                                    bi = nxt("sbf", 3)
                                    st["bi"] = bi
                                    k.dve_copy(sbf[bi][:, 0:256], st4[:, 0:256], [bst], [b_sbf[bi]])

                            def s1(part=part, b=b, M=M, st=s0.__defaults__[-1]):
                                if M == 1 or part == 2:
                                    return
                                bi = st["bi"]
                                qi = nxt("pq", 2)
                                k.trs([(psb[:, qi, h2 * 128:(h2 + 1) * 128], sbf[bi][:, h2 * 128:(h2 + 1) * 128], ident) for h2 in range(2)],
                                      [b_sbf[bi], b_c], [pbq[qi]])
                                dT_, bT_ = (QT, b_QT) if part == 0 else (KT, b_KT)
                                k.act(dT_[:, :, b, :], psb[:, qi, 0:256].rearrange("p (h q) -> p h q", h=2), AF.Copy, [pbq[qi]], [bT_[b]])
                            items.append((s0, s1))
                    skew(items, 2)
                    if stop_after == "QKV0":
                        return True
                    if kv_only:
                        hb = {0: [15], 1: [12, 13, 14, 15], 2: list(range(16))}[g]
                        for b in hb:
                            ab, sl = hslot(g, b)
                            c0_ = sl * 512 + hp * 256
                            k.dma(STQ, hs[(l, ab)][0:128, c0_:c0_ + 256].rearrange("p (h q) -> p h q", h=2), KT[:, :, b, :], [b_KT[b]],
                                  [b_hs[(l, ab)][(sl * 2 + hp) * 2]], pool="st", npool=6)
                            k.dma(STQ, hs[(l, ab)][128:256, c0_:c0_ + 256], Vt[:, b, :], [b_V[b]], [b_hs[(l, ab)][(sl * 2 + hp) * 2 + 1]], pool="st", npool=6)
                    if kv_only:
                        continue
                    items = []
                    for b in {0: list(range(1, 16)) + [0], 1: list(range(4, 16)) + [0, 1, 2, 3], 2: list(range(16))}[g]:
                        if g == 0:
                            pvb = b - 1 if b > 0 else None
                            hb_ = 15
                        elif g == 1:
                            pvb = b - 4 if b >= 4 else None
                            hb_ = 12 + b
                        else:
                            pvb = None
                            hb_ = b
                        use_halo = pvb is None
                        hasprev = use_halo or pvb is not None
                        def a0(b=b, pvb=pvb, hb_=hb_, use_halo=use_halo, hasprev=hasprev, st={}):
                            hi = None
                            if use_halo:
                                hi = nxt("h", 3)
                                ab, sl = hslot(g, hb_)
                                c0_ = sl * 512 + hp * 256
                                k.dma(S_, hKt[hi], hr[(l, ab)][0:128, c0_:c0_ + 256].rearrange("p (h q) -> p h q", h=2), [b_hr[(l, ab)]], [b_hK[hi]], pool="hl", npool=3)
                                k.dma(S_, hVt[hi], hr[(l, ab)][128:256, c0_:c0_ + 256], [b_hr[(l, ab)]], [b_hV[hi]], pool="hl2", npool=3)
                            st["hi"] = hi
                            gh0 = g * 4 + hp * 2
                            bk = nxt("pb", 6)
                            its = []; rd = [b_QT[b], b_KT[b]]
                            for h2 in range(2):
                                if hasprev:
                                    if use_halo:
                                        kprev = hKt[hi][:, h2, :]; rd.append(b_hK[hi])
                                    else:
                                        kprev = KT[:, h2, pvb, :]; rd.append(b_KT[pvb])
                                    its.append((ps[:, bk, h2 * 256:h2 * 256 + 128], kprev, QT[:, h2, b, :], True, True))
                                its.append((ps[:, bk, h2 * 256 + 128:h2 * 256 + 256], KT[:, h2, b, :], QT[:, h2, b, :], True, True))
                            k.mms(its, rd, [pb[bk]])
                            pi = nxt("sbf", 3); P_ = sbf[pi]; bP = b_sbf[pi]
                            st["pi"] = pi
                            P4 = P_[:, :].rearrange("p (h a q) -> p h a q", h=2, a=2)
                            S4 = ps[:, bk, :].rearrange("p (h a q) -> p h a q", h=2, a=2)
                            if hasprev:
                                k.act(P_[:, :], ps[:, bk, :], AF.Exp, [pb[bk]], [bP], scale=SCALE)
                            else:
                                k.act(P4[:, :, 1, :], S4[:, :, 1, :], AF.Exp, [pb[bk]], [bP], scale=SCALE)
                            if use_halo:
                                k.dve_stt(P4[:, :, 0, :], P4[:, :, 0, :], hm[:, 0:1], Et[:, gh0:gh0 + 2, 0, :], ALU.mult, ALU.mult, [bP, b_c], [bP])
                                k.dve_tt(P4[:, :, 1, :], P4[:, :, 1, :], Et[:, gh0:gh0 + 2, 1, :], ALU.mult, [bP, b_c], [bP])
                            elif hasprev:
                                k.dve_tt(P_[:, :], P_[:, :], Et[:, gh0:gh0 + 2, :, :].rearrange("p h a q -> p (h a q)"), ALU.mult, [bP, b_c], [bP])
                            else:
                                k.dve_tt(P4[:, :, 1, :], P4[:, :, 1, :], Et[:, gh0:gh0 + 2, 1, :], ALU.mult, [bP, b_c], [bP])

                        def a1(b=b, pvb=pvb, use_halo=use_halo, hasprev=hasprev, st=a0.__defaults__[-1]):
                            hi = st["hi"]
                            pi = st["pi"]; P_ = sbf[pi]; bP = b_sbf[pi]
                            s0_, s1_, st_ = block_cols(g, b)
                            bk2 = nxt("pb", 6)
                            its = []; rd = [bP, b_V[b], b_c]
                            for h2 in range(2):
                                Pp = P_[:, h2 * 256:h2 * 256 + 128]; Po = P_[:, h2 * 256 + 128:h2 * 256 + 256]
                                oO = ps[:, bk2, h2 * 256:h2 * 256 + 128]; oD = ps[:, bk2, h2 * 256 + 128:h2 * 256 + 256]
                                if hasprev:
                                    if use_halo:
                                        vprev = hVt[hi][:, h2 * 128:(h2 + 1) * 128]; rd.append(b_hV[hi])
                                    else:
                                        vprev = Vt[:, pvb, h2 * 128:(h2 + 1) * 128]; rd.append(b_V[pvb])
                                    its.append((oO, vprev, Pp, True, False))
                                its.append((oO, Vt[:, b, h2 * 128:(h2 + 1) * 128], Po, not hasprev, True))
                                if hasprev:
                                    its.append((oD, ones, Pp, True, False))
                                its.append((oD, ones, Po, not hasprev, True))
                            k.mms(its, rd, [pb[bk2]])
                            av = acc[:, :, :, s0_:s1_:st_]
                            pv2 = ps[:, bk2, :].rearrange("p (h a q) -> p h a q", h=2, a=2)
                            if g == 0:
                                k.act(av, pv2, AF.Copy, [pb[bk2]], b_accb)
                            else:
                                k.dve_tt(av, pv2, av, ALU.add, [pb[bk2]] + b_accb, b_accb)
                        items.append((a0, a1))
                    skew(items, 2)
                    if stop_after == "ATT0":
                        return True
                    if isY:
                        for i_s in range(4):
                            ci = nxt("cK", 1)
                            k.dma(S_, cK[ci], cch[g][l, i_s, 0:WINS[g]:dil, :, hp * 2:hp * 2 + 2, :], [], [b_cK[ci]], pool="ck", npool=2)
                            bk = nxt("pb", 6)
                            k.mm(ps[:, bk, 0:256], ones[0:1, :], sQb[0:1, i_s, :], True, True, [b_sQ[i_s], b_c], [pb[bk]])
                            si = nxt("stg", NSTG); st4 = stg[si]; bst = b_stg[si]
                            k.dve_tt(st4[:, 0:256].rearrange("p (h d) -> p h d", h=2), cK[ci][:, 0, :, :], ps[:, bk, 0:256].rearrange("p (h d) -> p h d", h=2),
                                     ALU.mult, [b_cK[ci], pb[bk]], [bst])
                            mi = nxt("sm", 6); smv = sm[mi]; bsm = b_sm[mi]
                            k.op("vector", "tensor_reduce", [bst], [bsm], smv[:, 0:2], st4[:, 0:256].rearrange("p (h d) -> p h d", h=2), AX.X, ALU.add)
                            gh0 = g * 4 + hp * 2
                            k.dve_stt(smv[:, 0:2], smv[:, 0:2], SCALE, ALt[:, gh0:gh0 + 2], ALU.mult, ALU.add, [bsm, b_c], [bsm])
                            k.act(psm[:, 0:2], smv[:, 0:2], AF.Exp, [bsm], [b_psm])
                            k.dve_tt(st4[0:1, 256:512], sQb[0:1, i_s, :], sKb[0:1, i_s, :], ALU.mult, [b_sQ[i_s], b_sK[i_s]], [bst])
                            k.op("vector", "tensor_reduce", [bst], [bsm], smv[0:1, 4:6], st4[0:1, 256:512].rearrange("p (h d) -> p h d", h=2), AX.X, ALU.add)
                            k.act(pself[0:1, 0:2], smv[0:1, 4:6], AF.Exp, [bsm], [b_pself], scale=SCALE)
                            k.dve_copy(cVb, cK[ci][:, 1, :, :], [b_cK[ci]], [b_cVb])
                            bk2 = nxt("pb", 6)
                            its = []
                            for h2 in range(2):
                                its.append((ps[:, bk2, h2:h2 + 1], cVb[:, h2, :], psm[:, h2:h2 + 1], True, False))
                                its.append((ps[:, bk2, h2:h2 + 1], sVb[0:1, i_s, h2 * 128:(h2 + 1) * 128], pself[0:1, h2:h2 + 1], False, True))
                            its.append((ps[:, bk2, 2:4], ones, psm[:, 0:2], True, False))
                            its.append((ps[:, bk2, 2:4], ones[0:1, :], pself[0:1, 0:2], False, True))
                            k.mms(its, [b_cVb, b_psm, b_pself, b_sV[i_s], b_c], [pb[bk2]])
                            if g == 0:
                                k.dve_copy(sacc[:, i_s, :], ps[:, bk2, 0:4], [pb[bk2]], [b_sacc[i_s]])
                            else:
                                k.dve_tt(sacc[:, i_s, :], ps[:, bk2, 0:4], sacc[:, i_s, :], ALU.add, [pb[bk2], b_sacc[i_s]], [b_sacc[i_s]])
                if kv_only:
                    continue
                for h2 in range(2):
                    h = hp * 2 + h2
                    for q4 in range(4):
                        cs_ = slice(q4 * 512, (q4 + 1) * 512)
                        k.op("vector", "reciprocal", [b_accb[h2]], [b_accb[h2]], acc[:, h2, 1, cs_], acc[:, h2, 1, cs_])
                        k.dve_tt(yatt[:, h, cs_], acc[:, h2, 0, cs_], acc[:, h2, 1, cs_], ALU.mult, [b_accb[h2]], [b_yatt[h]])
                if isY:
                    for i_s in range(4):
                        k.op("vector", "reciprocal", [b_sacc[i_s]], [b_sacc[i_s]], sacc[:, i_s, 2:4], sacc[:, i_s, 2:4])
                        k.dve_tt(yatt[:, hp * 2:hp * 2 + 2, T + i_s], sacc[:, i_s, 0:2], sacc[:, i_s, 2:4], ALU.mult, [b_sacc[i_s]], [b_yatt[hp * 2], b_yatt[hp * 2 + 1]])

        sweep(True)
        def mk(ab):
            def fn():
                k.cc_allgather(hs[(l, ab)], hr[(l, ab)], b_hs[(l, ab)], [b_hr[(l, ab)]], PAIRS)
            return fn
        pending[:] = [2] + [mk(ab) for ab in NSL]
        sweep(False)
        if stop_after == "att":
            return True
        barrier()
        b_u = [[Buf("u") for _ in range(5)] for _ in range(4)]
        for t2 in range(2):
            wv, bw = wload(w_in[l, :, 4608 + t2 * 256:4608 + (t2 + 1) * 256], 8, 256)
            for o2 in range(2):
                oc = t2 * 2 + o2
                for reg in regs:
                    c0, n, kind, ri = reg
                    bk = nxt("pb", 6)
                    k.mms([(ps[:, bk, 0:n], wv[:, kc, o2 * 128:(o2 + 1) * 128], hT[:, kc, c0:c0 + n], kc == 0, kc == 7) for kc in range(8)],
                          [bw] + b_hT[ri], [pb[bk]])
                    k.act(uT[:, oc, c0:c0 + n], ps[:, bk, 0:n], AF.Gelu, [pb[bk]], [b_u[oc][ri]])
        wvA, bwA = wload(w_in[l, :, 5120:5376], 8, 256)
        wvB, bwB = wload(w_in[l, :, 5376:5632], 8, 256)
        b_ys = k.bufs("ys", 5)
        blist = [(b, 128) for b in range(16)] + ([(16 + i, 1) for i in range(4)] if isY else [])
        items = []
        for b, M in blist:
            def g0(b=b, M=M, st={}):
                if M == 128:
                    lc = lambda kc: hT[:, kc, b * 128:(b + 1) * 128]
                    rdh = b_hT[b // 4]
                else:
                    i_s = b - 16
                    lc = lambda kc: hT[:, kc, T + i_s:T + i_s + 1]
                    rdh = b_hT[4]
                bk = nxt("pb", 6)
                k.mms([(ps[0:M, bk, 0:256], lc(kc), wvA[:, kc, :], kc == 0, kc == 7) for kc in range(8)]
                      + [(ps[0:M, bk, 256:512], lc(kc), wvB[:, kc, :], kc == 0, kc == 7) for kc in range(8)], [bwA, bwB] + rdh, [pb[bk]])
                si = nxt("stg", NSTG); gv = stg[si]; bst = b_stg[si]
                mi = nxt("sm", 6); smv = sm[mi]; bsm = b_sm[mi]
                k.act(gv[0:M, :], ps[0:M, bk, 0:512], AF.Gelu, [pb[bk]], [bst, bsm], accum_out=smv[0:M, 0:1])
                si2 = nxt("stg", NSTG); jk = stg[si2]; bjk = b_stg[si2]
                k.act(jk[0:M, :], gv[0:M, :], AF.Square, [bst], [bjk, bsm], accum_out=smv[0:M, 1:2])
                k.dve_ts(smv[0:M, 0:2], smv[0:M, 0:2], 1.0 / 512, None, ALU.mult, None, [bsm], [bsm])
                k.dve_tt(smv[0:M, 2:3], smv[0:M, 0:1], smv[0:M, 0:1], ALU.mult, [bsm], [bsm])
                k.dve_tt(smv[0:M, 3:4], smv[0:M, 1:2], smv[0:M, 2:3], ALU.subtract, [bsm], [bsm])
                k.act(smv[0:M, 3:4], smv[0:M, 3:4], AF.Sqrt, [bsm], [bsm], scale=1.0, bias=EPS)
                k.op("vector", "reciprocal", [bsm], [bsm], smv[0:M, 3:4], smv[0:M, 3:4])
                k.dve_ts(gv[0:M, :], gv[0:M, :], smv[0:M, 0:1], smv[0:M, 3:4], ALU.subtract, ALU.mult, [bst, bsm], [bst])
                k.dve_tt(gv[0:M, :], gv[0:M, :], LNW[0:M, 0, :], ALU.mult, [bst, b_lc], [bst])
                bi = nxt("sbf", 3); vb = sbf[bi]; bvb = b_sbf[bi]
                st["bi"] = bi
                if M == 128:
                    k.dve_tt(vb[:, :], gv[:, :], LNB[:, 0, :], ALU.add, [bst, b_lc], [bvb])
                else:
                    k.dve_tt(gv[0:1, :], gv[0:1, :], LNB[0:1, 0, :], ALU.add, [bst, b_lc], [bst])
                    k.dma(STQ, sguv[l, i_s:i_s + 1, :], gv[0:1, :], [bst], [], pool="st", npool=6, out_final=True)
                    k.dve_tt(jk[0:1, :], gv[0:1, :], W00[0:1, 0, :], ALU.mult, [bst, b_lc], [bjk])
                    k.dve_tt(vb[0:1, :], jk[0:1, :], B00[0:1, 0, :], ALU.add, [bjk, b_lc], [bvb])

            def g1(b=b, M=M, st=g0.__defaults__[-1]):
                bi = st["bi"]; vb = sbf[bi]; bvb = b_sbf[bi]
                bk2 = nxt("pb", 6)
                if M == 128:
                    its = []
                    for g8 in range(8):
                        c4, gg = g8 // 2, g8 % 2
                        o_ = ps[64 * gg:64 * gg + 64, bk2, c4 * 128:(c4 + 1) * 128]
                        its.append((o_, vb[:, g8 * 64:(g8 + 1) * 64], WMT[:, 0, g8, :], True, False))
                        its.append((o_, ones[0:1, 0:64], BSP[0:1, 0, g8, :], False, True))
                    k.mms(its, [bvb, b_c, b_lc], [pb[bk2]])
                    k.dve_tt(ysgu[:, :, b * 128:(b + 1) * 128], ps[:, bk2, :].rearrange("p (c t) -> p c t", c=4), uT[:, :, b * 128:(b + 1) * 128], ALU.mult,
                             [pb[bk2]] + [b_u[oc][b // 4] for oc in range(4)], [b_ys[b // 4]])
                else:
                    i_s = b - 16
                    k.mms([(ps[:, bk2, c4:c4 + 1], vb[0:1, c4 * 128:(c4 + 1) * 128], ones[0:1, 0:1], True, True) for c4 in range(4)], [bvb, b_c], [pb[bk2]])
                    k.dve_tt(ysgu[:, :, T + i_s], ps[:, bk2, 0:4], uT[:, :, T + i_s], ALU.mult, [pb[bk2]] + [b_u[oc][4] for oc in range(4)], [b_ys[4]])
            items.append((g0, g1))
        skew(items, 2)
        if stop_after == "sgu":
            return True
        ada_run(0, upto_layer=l)
        ada_en[0] = False
        barrier()
        b_mg = [[Buf("mg") for _ in range(5)] for _ in range(8)]
        for t4 in range(4):
            wga, bga = wload(w_in[l, :, 5632 + t4 * 256:5632 + (t4 + 1) * 256], 8, 256)
            wgb, bgb = wload(w_in[l, :, 6656 + t4 * 256:6656 + (t4 + 1) * 256], 8, 256)
            wpa, bpa = wload(w_pa[l, :, t4 * 256:(t4 + 1) * 256], 4, 256)
            wps, bps = wload(w_ps[l, :, t4 * 256:(t4 + 1) * 256], 4, 256)
            for oc in range(2):
                c = t4 * 2 + oc
                for reg in regs:
                    c0, n, kind, ri = reg
                    b1, b2, b3, b4 = nxt("pb", 6), nxt("pb", 6), nxt("pb", 6), nxt("pb", 6)
                    osl = slice(oc * 128, (oc + 1) * 128)
                    k.mms([(ps[:, b1, 0:n], wga[:, kc, osl], hT[:, kc, c0:c0 + n], kc == 0, kc == 7) for kc in range(8)], [bga] + b_hT[ri], [pb[b1]])
                    k.mms([(ps[:, b2, 0:n], wgb[:, kc, osl], hT[:, kc, c0:c0 + n], kc == 0, kc == 7) for kc in range(8)], [bgb] + b_hT[ri], [pb[b2]])
                    k.mms([(ps[:, b3, 0:n], wpa[:, kc, osl], yatt[:, kc, c0:c0 + n], kc == 0, kc == 3) for kc in range(4)], [bpa] + b_yatt, [pb[b3]])
                    k.mms([(ps[:, b4, 0:n], wps[:, kc, osl], ysgu[:, kc, c0:c0 + n], kc == 0, kc == 3) for kc in range(4)], [bps, b_ys[ri]], [pb[b4]])
                    s1 = nxt("stg", NSTG); s2 = nxt("stg", NSTG)
                    k.act(stg[s1][:, 0:n], ps[:, b1, 0:n], AF.Sigmoid, [pb[b1]], [b_stg[s1]])
                    k.act(stg[s2][:, 0:n], ps[:, b2, 0:n], AF.Sigmoid, [pb[b2]], [b_stg[s2]])
                    k.dve_tt(stg[s1][:, 0:n], stg[s1][:, 0:n], ps[:, b3, 0:n], ALU.mult, [b_stg[s1], pb[b3]], [b_stg[s1]])
                    k.dve_tt(stg[s2][:, 0:n], stg[s2][:, 0:n], ps[:, b4, 0:n], ALU.mult, [b_stg[s2], pb[b4]], [b_stg[s2]])
                    k.dve_tt(mg[:, c, c0:c0 + n], stg[s1][:, 0:n], stg[s2][:, 0:n], ALU.add, [b_stg[s1], b_stg[s2]], [b_mg[c][ri]])
        if stop_after == "merge":
            return True
        wo = [wload(w_o[l, :, t4 * 256:(t4 + 1) * 256], 8, 256) for t4 in range(4)]
        xload(0)
        xload(1)
        for t in range(len(subs) + 1):
            if t >= 1:
                reg, sub, ns = subs[t - 1]
                norm_b(t - 1, l, 4, 3, reg, sub, ns, xrs[(t - 1) % 2], b_xrs[(t - 1) % 2])
                if t + 1 < len(subs):
                    xload(t + 1)
            if t < len(subs):
                reg, sub, ns = subs[t]
                c0, n, kind, ri = reg
                xr, b_xr = xrs[t % 2], b_xrs[t % 2]
                for c in range(8):
                    wv_, bw_ = wo[c // 2]
                    bk = nxt("pb", 6)
                    k.mms([(ps[:, bk, 0:ns], wv_[:, kc, (c % 2) * 128:(c % 2 + 1) * 128], mg[:, kc, c0 + sub:c0 + sub + ns], kc == 0, kc == 7) for kc in range(8)],
                          [bw_] + [b_mg[kc][ri] for kc in range(8)], [pb[bk]])
                    modmul(xr[:, c, 0:ns], ps[:, bk, 0:ns], l, 2, c, reg, other=xr[:, c, 0:ns], op1=ALU.add, reads=[pb[bk], b_xr[c]], writes=[b_xr[c]])
                k.dma(STQ, xmD[:, :, c0 + sub:c0 + sub + ns], xr[:, :, 0:ns], b_xr, [b_xmD[ri]], pool="st", npool=6)
                norm_a(t, ns, xr, b_xr)
                norm_a2(t, ns)
        if stop_after == "wout":
            return True
        ada_en[0] = True
        barrier()
        for grp in ([regs[0:2], regs[2:]]):
            hoff = grp[0][0]
            b_hd = [[Buf("hd") for _ in range(5)] for _ in range(22)]
            for f2 in range(11):
                wa, ba_ = wload(w_fi[l, :, f2 * 256:(f2 + 1) * 256], 8, 256)
                wb_, bb_ = wload(w_fi[l, :, 2816 + f2 * 256:2816 + (f2 + 1) * 256], 8, 256)
                for oc in range(2):
                    f = f2 * 2 + oc
                    for reg in grp:
                        c0, n, kind, ri = reg
                        b1, b2 = nxt("pb", 6), nxt("pb", 6)
                        osl = slice(oc * 128, (oc + 1) * 128)
                        k.mms([(ps[:, b1, 0:n], wa[:, kc, osl], hT[:, kc, c0:c0 + n], kc == 0, kc == 7) for kc in range(8)], [ba_] + b_hT[ri], [pb[b1]])
                        k.mms([(ps[:, b2, 0:n], wb_[:, kc, osl], hT[:, kc, c0:c0 + n], kc == 0, kc == 7) for kc in range(8)], [bb_] + b_hT[ri], [pb[b2]])
                        s1 = nxt("stg", NSTG)
                        k.act(stg[s1][:, 0:n], ps[:, b1, 0:n], AF.Silu, [pb[b1]], [b_stg[s1]])
                        k.dve_tt(hid[:, f, c0 - hoff:c0 - hoff + n], stg[s1][:, 0:n], ps[:, b2, 0:n], ALU.mult, [b_stg[s1], pb[b2]], [b_hd[f][ri]])
            cr = [(c, reg) for c in range(8) for reg in grp]

            def xmload(j):
                c, reg = cr[j]
                c0, n, kind, ri = reg
                k.dma(S_, xmc[j % 2][:, 0:n], xmD[:, c, c0:c0 + n], [b_xmD[ri]], [b_xmc[j % 2]], pool="xm", npool=2)
            xmload(0)
            wfo = None
            for j, (c, reg) in enumerate(cr):
                c0, n, kind, ri = reg
                if reg is grp[0]:
                    wfo = [wload(w_fo[l, hf * 1408:(hf + 1) * 1408, c * 128:(c + 1) * 128], 11, 128) for hf in range(2)]
                if j + 1 < len(cr):
                    xmload(j + 1)
                xi = j % 2
                bk = nxt("pb", 6)
                k.mms([(ps[:, bk, 0:n], wfo[kc // 11][0][:, kc % 11, :], hid[:, kc, c0 - hoff:c0 - hoff + n], kc == 0, kc == 21) for kc in range(22)],
                      [wfo[0][1], wfo[1][1]] + [b_hd[kc][ri] for kc in range(22)], [pb[bk]])
                modmul(xmc[xi][:, 0:n], ps[:, bk, 0:n], l, 5, c, reg, other=xmc[xi][:, 0:n], op1=ALU.add, reads=[pb[bk], b_xmc[xi]], writes=[b_xmc[xi]])
                if final:
                    dst = ysT[:, c, 0:4] if kind == "s" else yT[:, c, c0:c0 + n]
                    k.dma(STQ, dst, xmc[xi][:, 0:n], [b_xmc[xi]], [], pool="st", npool=6, out_final=True)
                else:
                    k.dma(STQ, xdst[:, c, c0:c0 + n], xmc[xi][:, 0:n], [b_xmc[xi]], [bxd[ri]], pool="st", npool=6)

    b_accb = k.bufs("acc", 2)
    b_QT = k.bufs("QT", 16); b_KT = k.bufs("KT", 16); b_V = k.bufs("V", 16)
    b_yatt = k.bufs("yatt", 4)
    for l_ in range(2):
        ada_run(0, upto_layer=l_ - 1)
        if l_ == 1:
            ada_run(8)
        if run_pass(l_) or stop_after == "pass%d" % l_:
            break
    k.finish()
    return k


def _fm(a):
    n = a.shape[0]
    return np.ascontiguousarray(a.T.reshape(8, 128, n).transpose(1, 0, 2))


def _consts():
    hh = np.arange(1, 13, dtype=np.float32)
    slopes = np.power(np.float32(2.0), -8.0 * hh / 12).astype(np.float32).reshape(3, 4)
    kk = np.arange(128)[:, None].astype(np.float64)
    qq = np.arange(128)[None, :].astype(np.float64)
    E = np.zeros((128, 12, 2, 128), np.float32)
    AL = np.zeros((128, 12), np.float32)
    for g in range(3):
        for h in range(4):
            s = float(slopes[g, h]) * DILS[g]
            E[:, g * 4 + h, 0, :] = np.where(kk >= qq, np.exp(-s * (128 + qq - kk)), 0.0)
            E[:, g * 4 + h, 1, :] = np.where(kk <= qq, np.exp(-s * (qq - kk)), 0.0)
            AL[:, g * 4 + h] = -s * (128 - np.arange(128))
    tril = (np.arange(128)[None, :] <= np.arange(128)[:, None]).astype(np.float32)
    return dict(ident=np.eye(128).astype(ml_dtypes.bfloat16), ones=np.ones((128, 128), ml_dtypes.bfloat16),
                E=E.astype(ml_dtypes.bfloat16), AL=AL, tril=tril)


_CACHE = {}


def kernel(x_prompt, x_sample, cache_kv_w128, cache_kv_w512, cache_kv_w2048, c_prompt, c_sample,
           w_ada, b_ada, norm1_w, w_in, q_norm_w, k_norm_w, sgu_ln_w, sgu_ln_b, w_spatial, b_spatial,
           w_proj_att, w_proj_sgu, w_out, norm2_w, w_ffn_in, w_ffn_out):
    f = lambda a: np.ascontiguousarray(np.asarray(a, dtype=np.float32))
    x_prompt, x_sample = f(x_prompt), f(x_sample)
    caches = [f(cache_kv_w128), f(cache_kv_w512), f(cache_kv_w2048)]
    c_prompt, c_sample = f(c_prompt), f(c_sample)
    if "k" not in _CACHE:
        _CACHE["k"] = build()
    kk = _CACHE["k"]
    shared = dict(w_ada=f(w_ada), w_in=f(w_in), q_norm_w=f(q_norm_w), k_norm_w=f(k_norm_w), sgu_ln_w=f(sgu_ln_w), sgu_ln_b=f(sgu_ln_b),
                  w_spatial=f(w_spatial), b_spatial=f(b_spatial), w_proj_att=f(w_proj_att), w_proj_sgu=f(w_proj_sgu), w_out=f(w_out),
                  w_ffn_in=f(w_ffn_in), w_ffn_out=f(w_ffn_out))
    shared["b_adaT"] = np.ascontiguousarray(f(b_ada).reshape(2, 48, 128).transpose(2, 0, 1))
    shared["norm1T"] = np.ascontiguousarray(f(norm1_w).reshape(2, 8, 128).transpose(2, 0, 1))
    shared["norm2T"] = np.ascontiguousarray(f(norm2_w).reshape(2, 8, 128).transpose(2, 0, 1))
    shared.update(_consts())
    in_maps = []
    for c in range(8):
        b, r = c // 2, c % 2
        m = dict(shared)
        m["xY"] = _fm(x_prompt[b, r * T:(r + 1) * T])
        m["xS"] = _fm(x_sample[4 * c:4 * c + 4, 0, :])
        cc = np.zeros((8, 1024), np.float32)
        cc[0] = c_prompt[b]
        cc[1:5] = c_sample[4 * c:4 * c + 4]
        m["cT"] = _fm(cc)
        m["hmask"] = np.full((128, 1), float(r), np.float32)
        for g in range(3):
            m["cache%d" % g] = np.ascontiguousarray(caches[g][:, 4 * c:4 * c + 4])
        in_maps.append(m)
    res = run_bass_kernel_spmd(kk.nc, in_maps, core_ids=list(range(8)))
    R_ = res.results
    y_prompt = np.zeros((4, 4096, 1024), np.float32)
    y_sample = np.zeros((32, 1, 1024), np.float32)
    kvp = [np.zeros((2, 4, WINS[g], 2, 4, 128), np.float32) for g in range(3)]
    kvs = [np.zeros((2, 32, 1, 2, 4, 128), np.float32) for g in range(3)]
    sguv = np.zeros((2, 32, 1, 512), np.float32)
    for c in range(8):
        b, r = c // 2, c % 2
        o = R_[c]
        y_prompt[b, r * T:(r + 1) * T] = np.asarray(o["yT"]).transpose(1, 0, 2).reshape(1024, T).T
        y_sample[4 * c:4 * c + 4, 0] = np.asarray(o["ysT"]).transpose(1, 0, 2).reshape(1024, 4).T
        for g in range(3):
            if r == 1:
                kvp[g][:, b] = np.asarray(o["kvp%d" % g])
            kvs[g][:, 4 * c:4 * c + 4, 0] = np.asarray(o["kvs%d" % g])
        sguv[:, 4 * c:4 * c + 4, 0] = np.asarray(o["sguv"])
    return (y_prompt, y_sample, kvp[0], kvp[1], kvp[2], kvs[0], kvs[1], kvs[2], sguv)
```

```python
import contextlib
import numpy as np
import ml_dtypes
import concourse.bass as bass
import concourse.mybir as mybir
from concourse.bass_utils import run_bass_kernel_spmd

F32 = mybir.dt.float32
BF16 = mybir.dt.bfloat16
AF = mybir.ActivationFunctionType
ALU = mybir.AluOpType
AX = mybir.AxisListType
EPOCH = 12000
SAME_ENGINE_WAITS = True


class Buf:
    __slots__ = ("name", "w", "r")

    def __init__(self, name):
        self.name = name
        self.w = None
        self.r = {}


class Eng:
    def __init__(self, k, name):
        self.k = k
        self.name = name
        self.prog = []
        self.sem = None
        self.cnt = 0
        self.waited = {}
        self.own = set()

    def wait(self, ev):
        if ev is None:
            return
        sem, val = ev
        if sem.num in self.own and (self.name == "tensor" or not SAME_ENGINE_WAITS):
            return
        if self.waited.get(sem.num, 0) >= val:
            return
        self.waited[sem.num] = val
        self.prog.append(lambda e, sem=sem, val=val: e.wait_ge(sem, val))

    def emit(self, fn, inc=True):
        if not inc:
            self.prog.append(fn)
            return None
        if self.sem is None or self.cnt >= EPOCH:
            self.sem = self.k.new_sem()
            self.own.add(self.sem.num)
            self.cnt = 0
        self.cnt += 1
        sem = self.sem
        self.prog.append(lambda e, fn=fn, sem=sem: fn(e).then_inc(sem, 1))
        return (sem, self.cnt)


class K:
    def __init__(self):
        self.nc = bass.Bass("TRN2", target_bir_lowering=False)
        self.st = contextlib.ExitStack()
        self.e = {n: Eng(self, n) for n in ("tensor", "vector", "scalar", "gpsimd", "sync")}
        self.nsem = 0
        self.dpools = {}
        self.out_events = []
        self.psum_f32 = self.st.enter_context(self.nc.psum_tensor("psf", [128, 6, 512], F32))[:, :, :]
        self.psum_bf = self.st.enter_context(self.nc.psum_tensor("psb", [128, 2, 1024], BF16))[:, :, :]
        self.pbuf = [Buf("ps%d" % i) for i in range(6)]
        self.pbuf_bf = Buf("psb")

    def new_sem(self):
        self.nsem += 1
        return self.st.enter_context(self.nc.semaphore("s%d" % self.nsem))

    def din(self, name, shape, dt=F32):
        return self.nc.dram_tensor(name, list(shape), dt, kind="ExternalInput").ap()

    def dout(self, name, shape, dt=F32):
        return self.nc.dram_tensor(name, list(shape), dt, kind="ExternalOutput").ap()

    def dscratch(self, name, shape, dt=F32):
        return self.nc.dram_tensor(name, list(shape), dt, kind="Internal").ap()

    def sb(self, name, shape, dt):
        h = self.st.enter_context(self.nc.sbuf_tensor(name, list(shape), dt))
        return h[tuple(slice(None) for _ in shape)]

    def buf(self, name=""):
        return Buf(name)

    def bufs(self, name, n):
        return [Buf("%s%d" % (name, i)) for i in range(n)]

    def _access(self, E, reads, writes):
        for b in reads:
            E.wait(b.w)
        for b in writes:
            E.wait(b.w)
            for ev in list(b.r.values()):
                E.wait(ev)

    def _done(self, ev, reads, writes):
        sem, val = ev
        for b in reads:
            cur = b.r.get(sem.num)
            if cur is None or cur[1] < val:
                b.r[sem.num] = ev
        for b in writes:
            b.w = ev
            b.r = {}

    def op(self, eng, method, reads, writes, *args, **kw):
        E = self.e[eng]
        self._access(E, reads, writes)
        ev = E.emit(lambda e: getattr(e, method)(*args, **kw))
        self._done(ev, reads, writes)

    def mms(self, items, reads, writes):
        E = self.e["tensor"]
        self._access(E, reads, writes)
        n = len(items)
        ev = None
        for i, (o, l, r, s0, s1) in enumerate(items):
            fn = (lambda e, o=o, l=l, r=r, s0=s0, s1=s1: e.matmul(o, l, r, start=s0, stop=s1))
            ev = E.emit(fn, inc=(i == n - 1))
        self._done(ev, reads, writes)

    def mm(self, out, lhsT, rhs, start, stop, reads, writes):
        self.mms([(out, lhsT, rhs, start, stop)], reads, writes)

    def tr(self, out, in_, ident, reads, writes):
        E = self.e["tensor"]
        self._access(E, reads, writes)
        ev = E.emit(lambda e: e.transpose(out, in_, ident))
        self._done(ev, reads, writes)

    def trs(self, items, reads, writes):
        E = self.e["tensor"]
        self._access(E, reads, writes)
        n = len(items)
        ev = None
        for i, (o, a, idn) in enumerate(items):
            ev = E.emit(lambda e, o=o, a=a, idn=idn: e.transpose(o, a, idn), inc=(i == n - 1))
        self._done(ev, reads, writes)

    def act(self, out, in_, func, reads, writes, **kw):
        self.op("scalar", "activation", reads, writes, out, in_, func, **kw)

    def dve_ts(self, out, in0, s1, s2, op0, op1, reads, writes, eng="vector"):
        if op1 is None:
            self.op(eng, "tensor_scalar", reads, writes, out, in0, s1, None, op0)
        else:
            self.op(eng, "tensor_scalar", reads, writes, out, in0, s1, s2, op0, op1)

    def dve_tt(self, out, in0, in1, op, reads, writes, eng="vector"):
        self.op(eng, "tensor_tensor", reads, writes, out, in0, in1, op)

    def dve_stt(self, out, in0, scalar, in1, op0, op1, reads, writes, eng="vector"):
        self.op(eng, "scalar_tensor_tensor", reads, writes, out, in0, scalar, in1, op0, op1)

    def dve_copy(self, out, in_, reads, writes, eng="vector"):
        self.op(eng, "tensor_copy", reads, writes, out, in_)

    def dma(self, q, out, in_, reads, writes, pool=None, npool=4, out_final=False, **kw):
        E = self.e[q]
        pool = pool or (q + "_d")
        if pool not in self.dpools:
            self.dpools[pool] = [[self.new_sem(), 0] for _ in range(npool)] + [0]
        P = self.dpools[pool]
        slot = P[P[-1] % (len(P) - 1)]
        P[-1] += 1
        self._access(E, reads, writes)
        sem, tot = slot
        if tot:
            E.wait((sem, tot))
        slot[1] = tot + 16
        E.emit(lambda e: e.dma_start(out=out, in_=in_, **kw).then_inc(sem, 16), inc=False)
        ev = (sem, tot + 16)
        self._done(ev, reads, writes)
        if out_final:
            self.out_events.append(ev)
        return ev

    def cc_allgather(self, in_ap, out_ap, reads, writes, groups):
        E = self.e["gpsimd"]
        self._access(E, reads, writes)
        sem = self.new_sem()
        E.emit(lambda e: e.collective_compute("AllGather", ALU.bypass, replica_groups=groups, ins=[in_ap], outs=[out_ap]).then_inc(sem), inc=False)
        self._done((sem, 1), reads, writes)

    def finish(self):
        S = self.e["sync"]
        for ev in self.out_events:
            S.wait(ev)
        for n in ("tensor", "vector", "scalar"):
            E = self.e[n]
            if E.sem is not None:
                S.wait((E.sem, E.cnt))
        with self.nc.Block() as block:
            for name in ("tensor", "vector", "scalar", "gpsimd", "sync"):
                prog = self.e[name].prog

                def run(e, prog=prog):
                    for fn in prog:
                        fn(e)
                getattr(block, name)(run)
        self.st.close()


T = 2048
TS = T + 4
DILS = (1, 4, 16)
WINS = (128, 512, 2048)
SCALE = 128.0 ** -0.5
EPS = 1e-6
NWS = 7
STQ = "sync"


def block_cols(g, b):
    if g == 0:
        return (b * 128, (b + 1) * 128, 1)
    if g == 1:
        j, r = b // 4, b % 4
        return (j * 512 + r, (j + 1) * 512, 4)
    return (b, T, 16)


SKEW_D = 2


def skew(items, d):
    d = min(d, SKEW_D)
    n = len(items)
    for t in range(n + d):
        if t < n:
            items[t][0]()
        if t - d >= 0 and items[t - d][1] is not None:
            items[t - d][1]()


def build(stop_after=None):
    k = K()
    xY = k.din("xY", [128, 8, T]); xS = k.din("xS", [128, 8, 4])
    cT = k.din("cT", [128, 8, 8]); hmask = k.din("hmask", [128, 1])
    ident_d = k.din("ident", [128, 128], BF16); ones_d = k.din("ones", [128, 128], BF16)
    E_d = k.din("E", [128, 12, 2, 128], BF16); AL_d = k.din("AL", [128, 12]); tril_d = k.din("tril", [128, 128])
    w_ada = k.din("w_ada", [2, 1024, 6144]); b_adaT = k.din("b_adaT", [128, 2, 48])
    norm1T = k.din("norm1T", [128, 2, 8]); norm2T = k.din("norm2T", [128, 2, 8])
    w_in = k.din("w_in", [2, 1024, 7680])
    qnw = k.din("q_norm_w", [2, 3, 128]); knw = k.din("k_norm_w", [2, 3, 128])
    lnw = k.din("sgu_ln_w", [2, 512]); lnb = k.din("sgu_ln_b", [2, 512])
    w_sp = k.din("w_spatial", [2, 8, 128, 128]); b_sp = k.din("b_spatial", [2, 8, 128])
    w_pa = k.din("w_proj_att", [2, 512, 1024]); w_ps = k.din("w_proj_sgu", [2, 512, 1024])
    w_o = k.din("w_out", [2, 1024, 1024]); w_fi = k.din("w_ffn_in", [2, 1024, 5632]); w_fo = k.din("w_ffn_out", [2, 2816, 1024])
    cch = [k.din("cache%d" % g, [2, 4, WINS[g], 2, 4, 128]) for g in range(3)]
    yT = k.dout("yT", [128, 8, T]); ysT = k.dout("ysT", [128, 8, 4])
    kvp = [k.dout("kvp%d" % g, [2, WINS[g], 2, 4, 128]) for g in range(3)]
    kvs = [k.dout("kvs%d" % g, [2, 4, 2, 4, 128]) for g in range(3)]
    sguv = k.dout("sguv", [2, 4, 512])
    x1Y = k.dscratch("x1Y", [128, 8, TS]); xmD = k.dscratch("xmD", [128, 8, TS])
    b_x1Y = k.bufs("x1Y", 5); b_xmD = k.bufs("xmD", 5)
    NSL = {"A": 5, "B0": 4, "B1": 4, "B2": 4, "B3": 4}
    hs = {(l, ab): k.dscratch("hs%d%s" % (l, ab), [256, NSL[ab] * 512], BF16) for l in range(2) for ab in NSL}
    hr = {(l, ab): k.dscratch("hr%d%s" % (l, ab), [512, NSL[ab] * 512], BF16) for l in range(2) for ab in NSL}
    b_hs = {(l, ab): k.bufs("hs", 2 * 2 * NSL[ab]) for l in range(2) for ab in NSL}
    b_hr = {(l, ab): k.buf("hr") for l in range(2) for ab in NSL}
    PAIRS = [[0, 1], [2, 3], [4, 5], [6, 7]]
    pending = []

    def hslot(g, b):
        if g == 0:
            return "A", 0
        if g == 1:
            return "A", 1 + b - 12
        return "B%d" % (b // 4), b % 4

    RB = 73760
    R = k.sb("R", [128, RB // 2], BF16)

    def view(off, shape, dt):
        n = int(np.prod(shape)) * (4 if dt == F32 else 2)
        v = R[:, off // 2:(off + n) // 2]
        if dt == F32:
            v = v.bitcast(F32)
        names = " ".join("d%d" % i for i in range(len(shape)))
        if len(shape) > 1:
            v = v.rearrange("p (%s) -> p %s" % (names, names), **{"d%d" % i: s for i, s in enumerate(shape)})
        return v
    yatt = view(0, [4, TS], BF16)
    acc = view(16416, [2, 2, T], F32)
    QT = view(49184, [2, 16, 128], BF16); KT = view(49184 + 8192, [2, 16, 128], BF16); Vt = view(49184 + 16384, [16, 256], BF16)
    ysgu = view(16416, [4, TS], BF16); uT = view(32832, [4, TS], BF16); mg = view(32832, [8, TS], BF16)
    hid = view(0, [22, 1028], BF16)
    hT = k.sb("hT", [128, 8, TS], BF16)
    WS = [k.sb("ws%d" % i, [128, 2048], BF16) for i in range(NWS)]
    b_w = k.bufs("w", NWS)
    xrs = [k.sb("xr%d" % i, [128, 8, 256], F32) for i in range(2)]; b_xrs = [k.bufs("xr%d_" % i, 8) for i in range(2)]
    sqt = k.sb("sqt", [128, 8, 256], BF16); b_sq = k.bufs("sq", 8)
    tmpb = sqt; b_tmpb = b_sq
    ident = k.sb("identS", [128, 128], BF16); ones = k.sb("onesS", [128, 128], BF16)
    Et = k.sb("Et", [128, 12, 2, 128], BF16); Eh = k.sb("Eh", [128, 12, 128], BF16)
    ALt = k.sb("ALt", [128, 12], F32); tril = k.sb("trilS", [128, 128], F32); hm = k.sb("hm", [128, 1], F32)
    MOD = k.sb("MOD", [128, 2, 6, 8, 8], F32); b_mod = [k.bufs("mod%d_" % l, 6) for l in range(2)]
    cst = k.sb("cst", [128, 8, 8], F32); csb = k.sb("csb", [128, 8, 8], BF16)
    bad = k.sb("bad", [128, 2, 48], F32); n1t = k.sb("n1t", [128, 2, 8], F32); n2t = k.sb("n2t", [128, 2, 8], F32)
    QW = k.sb("QW", [128, 1, 3, 128], F32); KW = k.sb("KW", [128, 1, 3, 128], F32)
    LNW = k.sb("LNW", [128, 1, 512], F32); LNB = k.sb("LNB", [128, 1, 512], F32)
    WMT = k.sb("WMT", [128, 1, 8, 128], BF16); BSP = k.sb("BSP", [1, 1, 8, 128], BF16)
    W00 = k.sb("W00", [1, 1, 512], F32); B00 = k.sb("B00", [1, 1, 512], F32)
    b_lc = k.buf("lc")
    w8 = k.sb("w8", [1, 16], F32); b_w8 = k.buf("w8")
    b_c = k.buf("consts")
    NSTG = 4
    stg = [k.sb("stg%d" % i, [128, 512], F32) for i in range(NSTG)]; b_stg = k.bufs("stg", NSTG)
    rsts = [stg[i][:, 0:256] for i in range(2)]; b_rsts = [b_stg[i] for i in range(2)]
    sbf = [k.sb("sbf%d" % i, [128, 512], BF16) for i in range(3)]; b_sbf = k.bufs("sbf", 3)
    hKt = [k.sb("hK%d" % i, [128, 2, 128], BF16) for i in range(3)]; hVt = [k.sb("hV%d" % i, [128, 256], BF16) for i in range(3)]
    b_hK = k.bufs("hK", 3); b_hV = k.bufs("hV", 3)
    sm = [k.sb("sm%d" % i, [128, 16], F32) for i in range(6)]; b_sm = k.bufs("sm", 6)
    xmc = [k.sb("xmc%d" % i, [128, 512], F32) for i in range(2)]; b_xmc = k.bufs("xmc", 2)
    sQb = k.sb("sQb", [1, 4, 256], BF16); sKb = k.sb("sKb", [1, 4, 256], BF16); sVb = k.sb("sVb", [1, 4, 256], BF16)
    b_sQ = k.bufs("sQ", 4); b_sK = k.bufs("sK", 4); b_sV = k.bufs("sV", 4)
    cK = [xmc[0].rearrange("p (a b d) -> p a b d", a=2, b=2)]; b_cK = [b_xmc[0]]
    cVb = k.sb("cVb", [128, 2, 128], BF16); b_cVb = k.buf("cVb")
    sacc = k.sb("sacc", [128, 4, 4], F32); b_sacc = k.bufs("sacc", 4)
    psm = k.sb("psm", [128, 4], BF16); b_psm = k.buf("psm")
    pself = k.sb("pself", [1, 4], BF16); b_pself = k.buf("pself")

    ps = k.psum_f32; pb = k.pbuf; psb = k.psum_bf
    pbq = k.bufs("psbq", 2)
    cnt = {}

    def nxt(name, n):
        i = cnt.get(name, 0) % n
        cnt[name] = cnt.get(name, 0) + 1
        return i

    def barrier():
        evs = []
        for n in ("tensor", "vector", "scalar"):
            E = k.e[n]
            if E.sem is not None:
                evs.append((E.sem, E.cnt))
        for P in k.dpools.values():
            for sem, tot in P[:-1]:
                if tot:
                    evs.append((sem, tot))
        for n in ("vector", "scalar"):
            for ev in evs:
                k.e[n].wait(ev)

    def wload(src, kc, n, ada_ok=True):
        assert kc * n <= 2048
        i = nxt("w", NWS)
        v = WS[i][:, 0:kc * n].rearrange("p (a b) -> p a b", a=kc)
        k.dma("gpsimd", v, src.rearrange("(a p) n -> p a n", p=128), [], [b_w[i]], pool="w", npool=NWS)
        if ada_ok and ada_en[0] and ada_jobs:
            cnt["wm"] = cnt.get("wm", 0) + 1
            if cnt["wm"] % 3 == 0:
                ada_run(1)
        if pending:
            pending[0] -= 1
            if pending[0] <= 0:
                for fn in pending[1:]:
                    fn()
                del pending[:]
        return v, b_w[i]

    S_ = "sync"
    for dst, src in ((ident, ident_d), (ones, ones_d), (Et, E_d), (ALt, AL_d), (tril, tril_d), (hm, hmask), (cst, cT),
                     (bad, b_adaT), (n1t, norm1T), (n2t, norm2T)):
        k.dma(S_, dst, src, [], [Buf("c")], pool="cst", npool=10)
    for ev_ in [(sem_, tot_) for sem_, tot_ in k.dpools["cst"][:-1] if tot_]:
        for n_ in ("tensor", "vector", "scalar"):
            k.e[n_].wait(ev_)
    wst = xrs[0][:, :, 0:128]; wsb = sqt[:, :, 0:128]

    def load_layer_consts(l):
        for g in range(3):
            k.dma(S_, QW[:, 0, g, :], qnw[l, g:g + 1, :].partition_broadcast(128), [], [b_lc])
            k.dma(S_, KW[:, 0, g, :], knw[l, g:g + 1, :].partition_broadcast(128), [], [b_lc])
        k.dma(S_, LNW[:, 0, :], lnw[l:l + 1, :].partition_broadcast(128), [], [b_lc])
        k.dma(S_, LNB[:, 0, :], lnb[l:l + 1, :].partition_broadcast(128), [], [b_lc])
        k.dma("gpsimd", BSP[0:1, 0, :, :], b_sp[l:l + 1, :, :], [], [b_lc], pool="w", npool=NWS)
        k.dma(S_, w8[0:1, 0:8].rearrange("p (g o) -> p g o", o=1), b_sp[l:l + 1, :, 0:1], [], [b_w8], allow_slow_non_contiguous=True)
        k.dma(S_, w8[0:1, 8:16].rearrange("p (g o) -> p g o", o=1), w_sp[l:l + 1, :, 0, 0:1], [], [b_w8], allow_slow_non_contiguous=True)
        k.dve_copy(B00[0:1, 0, :].rearrange("p (g c) -> p g c", g=8), w8[0:1, 0:8].unsqueeze(2).to_broadcast([1, 8, 64]), [b_w8], [b_lc])
        k.dve_copy(W00[0:1, 0, :].rearrange("p (g c) -> p g c", g=8), w8[0:1, 8:16].unsqueeze(2).to_broadcast([1, 8, 64]), [b_w8], [b_lc])
        k.dma(S_, wst, w_sp[l].rearrange("g t s -> t g s"), [], b_xrs[0])
        for g8 in range(8):
            k.dve_tt(wsb[:, g8, :], wst[:, g8, :], tril, ALU.mult, b_xrs[0] + [b_c], [b_sq[g8]])
        k.trs([(psb[:, 0, g8 * 128:(g8 + 1) * 128], wsb[:, g8, :], ident) for g8 in range(8)], b_sq + [b_c], [pbq[0]])
        k.dve_copy(WMT[:, 0].rearrange("p g t -> p (g t)"), psb[:, 0, :], [pbq[0]], [b_lc])

    for gh in range(12):
        k.dve_ts(Eh[:, gh, :], Et[:, gh, 0, :], hm[:, 0:1], None, ALU.mult, None, [b_c], [b_c])
    k.act(csb, cst, AF.Silu, [b_c], [b_c])
    ada_jobs = []
    ada_en = [True]

    def mk_ada(l, t):
        def job():
            wv, bw = wload(w_ada[l, :, t * 256:(t + 1) * 256], 8, 256, ada_ok=False)
            for oc in range(2):
                j = t * 2 + oc
                bk = nxt("pb", 6)
                k.mms([(ps[:, bk, 0:8], wv[:, kc, oc * 128:(oc + 1) * 128], csb[:, kc, :], kc == 0, kc == 7) for kc in range(8)],
                      [bw, b_c], [pb[bk]])
                k.dve_ts(MOD[:, l, j // 8, j % 8, :], ps[:, bk, 0:8], bad[:, l, j:j + 1], None, ALU.add, None, [pb[bk], b_c], [b_mod[l][j // 8]])
            if t in (7, 19):
                a_ = 1 if t == 7 else 4
                nt = n1t if t == 7 else n2t
                for kc in range(8):
                    k.dve_ts(MOD[:, l, a_, kc, :], MOD[:, l, a_, kc, :], 1.0, nt[:, l, kc:kc + 1], ALU.add, ALU.mult, [b_mod[l][a_], b_c], [b_mod[l][a_]])
        return job
    for l in range(2):
        for t in range(24):
            ada_jobs.append((l, mk_ada(l, t)))

    def ada_run(n=1, upto_layer=None):
        while ada_jobs and (n > 0 or (upto_layer is not None and ada_jobs[0][0] <= upto_layer)):
            ada_jobs.pop(0)[1]()
            n -= 1
    ada_run(8)

    REG_P = [(j * 512, 512, "p", j) for j in range(4)]
    REG_S = (T, 4, "s", 4)
    b_hT = [k.bufs("hT%d_" % j, 8) for j in range(5)]

    def modmul(dst, src, l, a, kc, reg, other=None, op1=None, eng="vector", reads=(), writes=()):
        c0, n, kind, ri = reg
        rd = list(reads) + [b_mod[l][a]]
        if kind == "p":
            sc = MOD[:, l, a, kc, 0:1]
            if other is None:
                k.dve_ts(dst, src, sc, None, ALU.mult, None, rd, list(writes), eng=eng)
            else:
                k.dve_stt(dst, src, sc, other, ALU.mult, op1, rd, list(writes), eng=eng)
        else:
            mt = MOD[:, l, a, kc, 1:5]
            if other is None:
                k.dve_tt(dst, src, mt, ALU.mult, rd, list(writes), eng=eng)
            else:
                i = nxt("sm", 6)
                k.dve_tt(sm[i][:, 0:4], src, mt, ALU.mult, rd, [b_sm[i]], eng=eng)
                k.dve_tt(dst, sm[i][:, 0:4], other, op1, [b_sm[i]] + list(reads), list(writes), eng=eng)

    def norm_a(i, nsub, xr, b_xr):
        rst, b_rst = rsts[i % 2], b_rsts[i % 2]
        for kc in range(8):
            k.act(sqt[:, kc, 0:nsub], xr[:, kc, 0:nsub], AF.Square, [b_xr[kc]], [b_sq[kc]])
        bk = nxt("pb", 6)
        k.mms([(ps[:, bk, 0:nsub], ones, sqt[:, kc, 0:nsub], kc == 0, kc == 7) for kc in range(8)], b_sq + [b_c], [pb[bk]])
        k.act(rst[:, 0:nsub], ps[:, bk, 0:nsub], AF.Sqrt, [pb[bk]], [b_rst], scale=1.0 / 1024, bias=EPS)

    def norm_a2(i, nsub):
        rst, b_rst = rsts[i % 2], b_rsts[i % 2]
        k.op("vector", "reciprocal", [b_rst], [b_rst], rst[:, 0:nsub], rst[:, 0:nsub])

    def norm_b(i, l, a, bidx, reg, sub, nsub, xr, b_xr):
        c0, n, kind, ri = reg
        rst, b_rst = rsts[i % 2], b_rsts[i % 2]
        for kc in range(8):
            xv = xr[:, kc, 0:nsub]
            tv = tmpb[:, kc, 0:nsub]
            k.dve_tt(tv, xv, rst[:, 0:nsub], ALU.mult, [b_xr[kc], b_rst], [b_tmpb[kc]])
        for kc in range(8):
            tv = tmpb[:, kc, 0:nsub]
            hv = hT[:, kc, c0 + sub:c0 + sub + nsub]
            if kind == "p":
                k.act(hv, tv, AF.Identity, [b_tmpb[kc], b_mod[l][a], b_mod[l][bidx]], [b_hT[ri][kc]],
                      scale=MOD[:, l, a, kc, 0:1], bias=MOD[:, l, bidx, kc, 0:1])
            else:
                k.dve_tt(tv, tv, MOD[:, l, a, kc, 1:5], ALU.mult, [b_tmpb[kc], b_mod[l][a]], [b_tmpb[kc]])
                k.dve_tt(hv, tv, MOD[:, l, bidx, kc, 1:5], ALU.add, [b_tmpb[kc], b_mod[l][bidx]], [b_hT[ri][kc]])

    def subregs(reg):
        c0, n, kind, ri = reg
        return [(0, 256), (256, 256)] if kind == "p" else [(0, 4)]

    def run_pass(l):
        isY = True
        regs = REG_P + [REG_S]
        if l == 0:
            def xsrc(reg, sub, ns):
                c0, n, kind, ri = reg
                if kind == "s":
                    return xS[:, :, 0:4], []
                return xY[:, :, c0 + sub:c0 + sub + ns], []
        else:
            def xsrc(reg, sub, ns):
                c0, n, kind, ri = reg
                return x1Y[:, :, c0 + sub:c0 + sub + ns], [b_x1Y[ri]]
        xdst, bxd = (x1Y, b_x1Y)
        final = l == 1
        barrier()
        load_layer_consts(l)
        subs = [(reg, sub, ns) for reg in regs for sub, ns in subregs(reg)]

        def xload(i):
            reg, sub, ns = subs[i]
            src, rb = xsrc(reg, sub, ns)
            k.dma(S_, xrs[i % 2][:, :, 0:ns], src, rb, b_xrs[i % 2])
        xload(0)
        xload(1)
        for t in range(len(subs) + 1):
            if t >= 1:
                reg, sub, ns = subs[t - 1]
                norm_b(t - 1, l, 1, 0, reg, sub, ns, xrs[(t - 1) % 2], b_xrs[(t - 1) % 2])
                if t + 1 < len(subs):
                    xload(t + 1)
            if t < len(subs):
                reg, sub, ns = subs[t]
                norm_a(t, ns, xrs[t % 2], b_xrs[t % 2])
                norm_a2(t, ns)
        bh_all = [b for j in range(5) for b in b_hT[j]]
        if stop_after == "N1":
            return True
        def sweep(kv_only):
            for hp in range(2):
                for g in range(3):
                    dil = DILS[g]
                    if kv_only:
                        blocks = {0: [15], 1: [12, 13, 14, 15], 2: list(range(16))}[g]
                    else:
                        blocks = list(range(16))
                    items = []
                    for part in ((1, 2) if kv_only else (0, 1, 2)):
                        col0 = part * 1536 + g * 512 + hp * 256
                        blist = [(b, 128) for b in blocks] + ([(16 + i, 1) for i in range(4)] if not kv_only else [])
                        wref = {}
                        for bi_, (b, M) in enumerate(blist):
                            def s0(part=part, b=b, M=M, col0=col0, first=(bi_ == 0), wref=wref, st={}):
                                if first:
                                    wref["w"] = wload(w_in[l, :, col0:col0 + 256], 8, 256)
                                wv, bw = wref["w"]
                                nw = (QW, KW, None)[part]
                                if M == 128:
                                    s0_, s1_, st_ = block_cols(g, b)
                                    lc = lambda kc: hT[:, kc, s0_:s1_:st_]
                                    rdh = bh_all[0:32]
                                else:
                                    i_s = b - 16
                                    lc = lambda kc: hT[:, kc, T + i_s:T + i_s + 1]
                                    rdh = b_hT[4]
                                bk = nxt("pb", 6)
                                k.mms([(ps[0:M, bk, 0:256], lc(kc), wv[:, kc, :], kc == 0, kc == 7) for kc in range(8)], [bw] + rdh, [pb[bk]])
                                pv = ps[0:M, bk, 0:256]
                                si = nxt("stg", NSTG); st4 = stg[si]; bst = b_stg[si]
                                if part < 2:
                                    mi = nxt("sm", 6); smv = sm[mi]; bsm = b_sm[mi]
                                    for h2 in range(2):
                                        k.act(st4[0:M, 256 + h2 * 128:256 + (h2 + 1) * 128], pv[:, h2 * 128:(h2 + 1) * 128], AF.Square, [pb[bk]], [bst, bsm],
                                              accum_out=smv[0:M, h2:h2 + 1])
                                    k.act(smv[0:M, 0:2], smv[0:M, 0:2], AF.Sqrt, [bsm], [bsm], scale=1.0 / 128, bias=EPS)
                                    k.op("vector", "reciprocal", [bsm], [bsm], smv[0:M, 0:2], smv[0:M, 0:2])
                                    for h2 in range(2):
                                        k.dve_stt(st4[0:M, h2 * 128:(h2 + 1) * 128], pv[:, h2 * 128:(h2 + 1) * 128], smv[0:M, h2:h2 + 1],
                                                  nw[0:M, 0, g, :], ALU.mult, ALU.mult, [pb[bk], bsm, b_lc], [bst])
                                else:
                                    k.act(st4[0:M, 0:256], pv, AF.Copy, [pb[bk]], [bst])
                                if M == 1:
                                    dstt, bd = ((sQb, b_sQ), (sKb, b_sK), (sVb, b_sV))[part]
                                    k.dve_copy(dstt[0:1, i_s, :], st4[0:1, 0:256], [bst], [bd[i_s]])
                                    if part > 0:
                                        k.dma(STQ, kvs[g][l, i_s:i_s + 1, part - 1, hp * 2:hp * 2 + 2, :],
                                              st4[0:1, 0:256].rearrange("p (h d) -> p h d", h=2), [bst], [], pool="st", npool=6, out_final=True)
                                    return
                                if part > 0 and not kv_only:
                                    rows = None
                                    if g == 0 and b == 15:
                                        rows = (0, 128, 1)
                                    elif g == 1 and b >= 12:
                                        rows = (b % 4, 512, 4)
                                    elif g == 2:
                                        rows = (b, 2048, 16)
                                    if rows is not None:
                                        k.dma(STQ, kvp[g][l, rows[0]:rows[1]:rows[2], part - 1, hp * 2:hp * 2 + 2, :],
                                              st4[:, 0:256].rearrange("p (h d) -> p h d", h=2), [bst], [], pool="st", npool=6, out_final=True)
                                if part == 2:
                                    k.dve_copy(Vt[:, b, :], st4[:, 0:256], [bst], [b_V[b]])
                                else:
                                    bi = nxt("sbf", 3)
                                    st["bi"] = bi
                                    k.dve_copy(sbf[bi][:, 0:256], st4[:, 0:256], [bst], [b_sbf[bi]])

                            def s1(part=part, b=b, M=M, st=s0.__defaults__[-1]):
                                if M == 1 or part == 2:
                                    return
                                bi = st["bi"]
                                qi = nxt("pq", 2)
                                k.trs([(psb[:, qi, h2 * 128:(h2 + 1) * 128], sbf[bi][:, h2 * 128:(h2 + 1) * 128], ident) for h2 in range(2)],
                                      [b_sbf[bi], b_c], [pbq[qi]])
                                dT_, bT_ = (QT, b_QT) if part == 0 else (KT, b_KT)
                                k.act(dT_[:, :, b, :], psb[:, qi, 0:256].rearrange("p (h q) -> p h q", h=2), AF.Copy, [pbq[qi]], [bT_[b]])
                            items.append((s0, s1))
                    skew(items, 2)
                    if stop_after == "QKV0":
                        return True
                    if kv_only:
                        hb = {0: [15], 1: [12, 13, 14, 15], 2: list(range(16))}[g]
                        for b in hb:
                            ab, sl = hslot(g, b)
                            c0_ = sl * 512 + hp * 256
                            k.dma(STQ, hs[(l, ab)][0:128, c0_:c0_ + 256].rearrange("p (h q) -> p h q", h=2), KT[:, :, b, :], [b_KT[b]],
                                  [b_hs[(l, ab)][(sl * 2 + hp) * 2]], pool="st", npool=6)
                            k.dma(STQ, hs[(l, ab)][128:256, c0_:c0_ + 256], Vt[:, b, :], [b_V[b]], [b_hs[(l, ab)][(sl * 2 + hp) * 2 + 1]], pool="st", npool=6)
                    if kv_only:
                        continue
                    items = []
                    for b in {0: list(range(1, 16)) + [0], 1: list(range(4, 16)) + [0, 1, 2, 3], 2: list(range(16))}[g]:
                        if g == 0:
                            pvb = b - 1 if b > 0 else None
                            hb_ = 15
                        elif g == 1:
                            pvb = b - 4 if b >= 4 else None
                            hb_ = 12 + b
                        else:
                            pvb = None
                            hb_ = b
                        use_halo = pvb is None
                        hasprev = use_halo or pvb is not None
                        def a0(b=b, pvb=pvb, hb_=hb_, use_halo=use_halo, hasprev=hasprev, st={}):
                            hi = None
                            if use_halo:
                                hi = nxt("h", 3)
                                ab, sl = hslot(g, hb_)
                                c0_ = sl * 512 + hp * 256
                                k.dma(S_, hKt[hi], hr[(l, ab)][0:128, c0_:c0_ + 256].rearrange("p (h q) -> p h q", h=2), [b_hr[(l, ab)]], [b_hK[hi]], pool="hl", npool=3)
                                k.dma(S_, hVt[hi], hr[(l, ab)][128:256, c0_:c0_ + 256], [b_hr[(l, ab)]], [b_hV[hi]], pool="hl2", npool=3)
                            st["hi"] = hi
                            gh0 = g * 4 + hp * 2
                            bk = nxt("pb", 6)
                            its = []; rd = [b_QT[b], b_KT[b]]
                            for h2 in range(2):
                                if hasprev:
                                    if use_halo:
                                        kprev = hKt[hi][:, h2, :]; rd.append(b_hK[hi])
                                    else:
                                        kprev = KT[:, h2, pvb, :]; rd.append(b_KT[pvb])
                                    its.append((ps[:, bk, h2 * 256:h2 * 256 + 128], kprev, QT[:, h2, b, :], True, True))
                                its.append((ps[:, bk, h2 * 256 + 128:h2 * 256 + 256], KT[:, h2, b, :], QT[:, h2, b, :], True, True))
                            k.mms(its, rd, [pb[bk]])
                            pi = nxt("sbf", 3); P_ = sbf[pi]; bP = b_sbf[pi]
                            st["pi"] = pi
                            P4 = P_[:, :].rearrange("p (h a q) -> p h a q", h=2, a=2)
                            S4 = ps[:, bk, :].rearrange("p (h a q) -> p h a q", h=2, a=2)
                            if hasprev:
                                k.act(P_[:, :], ps[:, bk, :], AF.Exp, [pb[bk]], [bP], scale=SCALE)
                            else:
                                k.act(P4[:, :, 1, :], S4[:, :, 1, :], AF.Exp, [pb[bk]], [bP], scale=SCALE)
                            if use_halo:
                                k.dve_tt(P4[:, :, 0, :], P4[:, :, 0, :], Eh[:, gh0:gh0 + 2, :], ALU.mult, [bP, b_c], [bP])
                                k.dve_tt(P4[:, :, 1, :], P4[:, :, 1, :], Et[:, gh0:gh0 + 2, 1, :], ALU.mult, [bP, b_c], [bP])
                            elif hasprev:
                                k.dve_tt(P_[:, :], P_[:, :], Et[:, gh0:gh0 + 2, :, :].rearrange("p h a q -> p (h a q)"), ALU.mult, [bP, b_c], [bP])
                            else:
                                k.dve_tt(P4[:, :, 1, :], P4[:, :, 1, :], Et[:, gh0:gh0 + 2, 1, :], ALU.mult, [bP, b_c], [bP])

                        def a1(b=b, pvb=pvb, use_halo=use_halo, hasprev=hasprev, st=a0.__defaults__[-1]):
                            hi = st["hi"]
                            pi = st["pi"]; P_ = sbf[pi]; bP = b_sbf[pi]
                            s0_, s1_, st_ = block_cols(g, b)
                            bk2 = nxt("pb", 6)
                            its = []; rd = [bP, b_V[b], b_c]
                            for h2 in range(2):
                                Pp = P_[:, h2 * 256:h2 * 256 + 128]; Po = P_[:, h2 * 256 + 128:h2 * 256 + 256]
                                oO = ps[:, bk2, h2 * 256:h2 * 256 + 128]; oD = ps[:, bk2, h2 * 256 + 128:h2 * 256 + 256]
                                if hasprev:
                                    if use_halo:
                                        vprev = hVt[hi][:, h2 * 128:(h2 + 1) * 128]; rd.append(b_hV[hi])
                                    else:
                                        vprev = Vt[:, pvb, h2 * 128:(h2 + 1) * 128]; rd.append(b_V[pvb])
                                    its.append((oO, vprev, Pp, True, False))
                                its.append((oO, Vt[:, b, h2 * 128:(h2 + 1) * 128], Po, not hasprev, True))
                                if hasprev:
                                    its.append((oD, ones, Pp, True, False))
                                its.append((oD, ones, Po, not hasprev, True))
                            k.mms(its, rd, [pb[bk2]])
                            av = acc[:, :, :, s0_:s1_:st_]
                            pv2 = ps[:, bk2, :].rearrange("p (h a q) -> p h a q", h=2, a=2)
                            if g == 0:
                                k.act(av, pv2, AF.Copy, [pb[bk2]], b_accb)
                            else:
                                k.dve_tt(av, pv2, av, ALU.add, [pb[bk2]] + b_accb, b_accb)
                        items.append((a0, a1))
                    skew(items, 2)
                    if stop_after == "ATT0":
                        return True
                    if isY:
                        for i_s in range(4):
                            ci = nxt("cK", 1)
                            k.dma(S_, cK[ci], cch[g][l, i_s, 0:WINS[g]:dil, :, hp * 2:hp * 2 + 2, :], [], [b_cK[ci]], pool="ck", npool=2)
                            bk = nxt("pb", 6)
                            k.mm(ps[:, bk, 0:256], ones[0:1, :], sQb[0:1, i_s, :], True, True, [b_sQ[i_s], b_c], [pb[bk]])
                            si = nxt("stg", NSTG); st4 = stg[si]; bst = b_stg[si]
                            k.dve_tt(st4[:, 0:256].rearrange("p (h d) -> p h d", h=2), cK[ci][:, 0, :, :], ps[:, bk, 0:256].rearrange("p (h d) -> p h d", h=2),
                                     ALU.mult, [b_cK[ci], pb[bk]], [bst])
                            mi = nxt("sm", 6); smv = sm[mi]; bsm = b_sm[mi]
                            k.op("vector", "tensor_reduce", [bst], [bsm], smv[:, 0:2], st4[:, 0:256].rearrange("p (h d) -> p h d", h=2), AX.X, ALU.add)
                            gh0 = g * 4 + hp * 2
                            k.dve_stt(smv[:, 0:2], smv[:, 0:2], SCALE, ALt[:, gh0:gh0 + 2], ALU.mult, ALU.add, [bsm, b_c], [bsm])
                            k.act(psm[:, 0:2], smv[:, 0:2], AF.Exp, [bsm], [b_psm])
                            k.dve_tt(st4[0:1, 256:512], sQb[0:1, i_s, :], sKb[0:1, i_s, :], ALU.mult, [b_sQ[i_s], b_sK[i_s]], [bst])
                            k.op("vector", "tensor_reduce", [bst], [bsm], smv[0:1, 4:6], st4[0:1, 256:512].rearrange("p (h d) -> p h d", h=2), AX.X, ALU.add)
                            k.act(pself[0:1, 0:2], smv[0:1, 4:6], AF.Exp, [bsm], [b_pself], scale=SCALE)
                            k.dve_copy(cVb, cK[ci][:, 1, :, :], [b_cK[ci]], [b_cVb])
                            bk2 = nxt("pb", 6)
                            its = []
                            for h2 in range(2):
                                its.append((ps[:, bk2, h2:h2 + 1], cVb[:, h2, :], psm[:, h2:h2 + 1], True, False))
                                its.append((ps[:, bk2, h2:h2 + 1], sVb[0:1, i_s, h2 * 128:(h2 + 1) * 128], pself[0:1, h2:h2 + 1], False, True))
                            its.append((ps[:, bk2, 2:4], ones, psm[:, 0:2], True, False))
                            its.append((ps[:, bk2, 2:4], ones[0:1, :], pself[0:1, 0:2], False, True))
                            k.mms(its, [b_cVb, b_psm, b_pself, b_sV[i_s], b_c], [pb[bk2]])
                            if g == 0:
                                k.dve_copy(sacc[:, i_s, :], ps[:, bk2, 0:4], [pb[bk2]], [b_sacc[i_s]])
                            else:
                                k.dve_tt(sacc[:, i_s, :], ps[:, bk2, 0:4], sacc[:, i_s, :], ALU.add, [pb[bk2], b_sacc[i_s]], [b_sacc[i_s]])
                if kv_only:
                    continue
                for h2 in range(2):
                    h = hp * 2 + h2
                    for q4 in range(4):
                        cs_ = slice(q4 * 512, (q4 + 1) * 512)
                        k.op("vector", "reciprocal", [b_accb[h2]], [b_accb[h2]], acc[:, h2, 1, cs_], acc[:, h2, 1, cs_])
                        k.dve_tt(yatt[:, h, cs_], acc[:, h2, 0, cs_], acc[:, h2, 1, cs_], ALU.mult, [b_accb[h2]], [b_yatt[h]])
                if isY:
                    for i_s in range(4):
                        k.op("vector", "reciprocal", [b_sacc[i_s]], [b_sacc[i_s]], sacc[:, i_s, 2:4], sacc[:, i_s, 2:4])
                        k.dve_tt(yatt[:, hp * 2:hp * 2 + 2, T + i_s], sacc[:, i_s, 0:2], sacc[:, i_s, 2:4], ALU.mult, [b_sacc[i_s]], [b_yatt[hp * 2], b_yatt[hp * 2 + 1]])

        sweep(True)
        def mk(ab):
            def fn():
                k.cc_allgather(hs[(l, ab)], hr[(l, ab)], b_hs[(l, ab)], [b_hr[(l, ab)]], PAIRS)
            return fn
        pending[:] = [2] + [mk(ab) for ab in NSL]
        sweep(False)
        if stop_after == "att":
            return True
        barrier()
        b_u = [[Buf("u") for _ in range(5)] for _ in range(4)]
        for t2 in range(2):
            wv, bw = wload(w_in[l, :, 4608 + t2 * 256:4608 + (t2 + 1) * 256], 8, 256)
            for o2 in range(2):
                oc = t2 * 2 + o2
                for reg in regs:
                    c0, n, kind, ri = reg
                    bk = nxt("pb", 6)
                    k.mms([(ps[:, bk, 0:n], wv[:, kc, o2 * 128:(o2 + 1) * 128], hT[:, kc, c0:c0 + n], kc == 0, kc == 7) for kc in range(8)],
                          [bw] + b_hT[ri], [pb[bk]])
                    k.act(uT[:, oc, c0:c0 + n], ps[:, bk, 0:n], AF.Gelu, [pb[bk]], [b_u[oc][ri]])
        wvA, bwA = wload(w_in[l, :, 5120:5376], 8, 256)
        wvB, bwB = wload(w_in[l, :, 5376:5632], 8, 256)
        b_ys = k.bufs("ys", 5)
        blist = [(b, 128) for b in range(16)] + ([(16 + i, 1) for i in range(4)] if isY else [])
        items = []
        for b, M in blist:
            def g0(b=b, M=M, st={}):
                if M == 128:
                    lc = lambda kc: hT[:, kc, b * 128:(b + 1) * 128]
                    rdh = b_hT[b // 4]
                else:
                    i_s = b - 16
                    lc = lambda kc: hT[:, kc, T + i_s:T + i_s + 1]
                    rdh = b_hT[4]
                bk = nxt("pb", 6)
                k.mms([(ps[0:M, bk, 0:256], lc(kc), wvA[:, kc, :], kc == 0, kc == 7) for kc in range(8)]
                      + [(ps[0:M, bk, 256:512], lc(kc), wvB[:, kc, :], kc == 0, kc == 7) for kc in range(8)], [bwA, bwB] + rdh, [pb[bk]])
                si = nxt("stg", NSTG); gv = stg[si]; bst = b_stg[si]
                mi = nxt("sm", 6); smv = sm[mi]; bsm = b_sm[mi]
                k.act(gv[0:M, :], ps[0:M, bk, 0:512], AF.Gelu, [pb[bk]], [bst, bsm], accum_out=smv[0:M, 0:1])
                si2 = nxt("stg", NSTG); jk = stg[si2]; bjk = b_stg[si2]
                k.act(jk[0:M, :], gv[0:M, :], AF.Square, [bst], [bjk, bsm], accum_out=smv[0:M, 1:2])
                k.dve_ts(smv[0:M, 0:2], smv[0:M, 0:2], 1.0 / 512, None, ALU.mult, None, [bsm], [bsm])
                k.dve_tt(smv[0:M, 2:3], smv[0:M, 0:1], smv[0:M, 0:1], ALU.mult, [bsm], [bsm])
                k.dve_tt(smv[0:M, 3:4], smv[0:M, 1:2], smv[0:M, 2:3], ALU.subtract, [bsm], [bsm])
                k.act(smv[0:M, 3:4], smv[0:M, 3:4], AF.Sqrt, [bsm], [bsm], scale=1.0, bias=EPS)
                k.op("vector", "reciprocal", [bsm], [bsm], smv[0:M, 3:4], smv[0:M, 3:4])
                k.dve_ts(gv[0:M, :], gv[0:M, :], smv[0:M, 0:1], smv[0:M, 3:4], ALU.subtract, ALU.mult, [bst, bsm], [bst])
                k.dve_tt(gv[0:M, :], gv[0:M, :], LNW[0:M, 0, :], ALU.mult, [bst, b_lc], [bst])
                bi = nxt("sbf", 3); vb = sbf[bi]; bvb = b_sbf[bi]
                st["bi"] = bi
                if M == 128:
                    k.dve_tt(vb[:, :], gv[:, :], LNB[:, 0, :], ALU.add, [bst, b_lc], [bvb])
                else:
                    k.dve_tt(gv[0:1, :], gv[0:1, :], LNB[0:1, 0, :], ALU.add, [bst, b_lc], [bst])
                    k.dma(STQ, sguv[l, i_s:i_s + 1, :], gv[0:1, :], [bst], [], pool="st", npool=6, out_final=True)
                    k.dve_tt(jk[0:1, :], gv[0:1, :], W00[0:1, 0, :], ALU.mult, [bst, b_lc], [bjk])
                    k.dve_tt(vb[0:1, :], jk[0:1, :], B00[0:1, 0, :], ALU.add, [bjk, b_lc], [bvb])

            def g1(b=b, M=M, st=g0.__defaults__[-1]):
                bi = st["bi"]; vb = sbf[bi]; bvb = b_sbf[bi]
                bk2 = nxt("pb", 6)
                if M == 128:
                    its = []
                    for g8 in range(8):
                        c4, gg = g8 // 2, g8 % 2
                        o_ = ps[64 * gg:64 * gg + 64, bk2, c4 * 128:(c4 + 1) * 128]
                        its.append((o_, vb[:, g8 * 64:(g8 + 1) * 64], WMT[:, 0, g8, :], True, False))
                        its.append((o_, ones[0:1, 0:64], BSP[0:1, 0, g8, :], False, True))
                    k.mms(its, [bvb, b_c, b_lc], [pb[bk2]])
                    k.dve_tt(ysgu[:, :, b * 128:(b + 1) * 128], ps[:, bk2, :].rearrange("p (c t) -> p c t", c=4), uT[:, :, b * 128:(b + 1) * 128], ALU.mult,
                             [pb[bk2]] + [b_u[oc][b // 4] for oc in range(4)], [b_ys[b // 4]])
                else:
                    i_s = b - 16
                    k.mms([(ps[:, bk2, c4:c4 + 1], vb[0:1, c4 * 128:(c4 + 1) * 128], ones[0:1, 0:1], True, True) for c4 in range(4)], [bvb, b_c], [pb[bk2]])
                    k.dve_tt(ysgu[:, :, T + i_s], ps[:, bk2, 0:4], uT[:, :, T + i_s], ALU.mult, [pb[bk2]] + [b_u[oc][4] for oc in range(4)], [b_ys[4]])
            items.append((g0, g1))
        skew(items, 2)
        if stop_after == "sgu":
            return True
        ada_run(0, upto_layer=l)
        ada_en[0] = False
        barrier()
        b_mg = [[Buf("mg") for _ in range(5)] for _ in range(8)]
        for t4 in range(4):
            wga, bga = wload(w_in[l, :, 5632 + t4 * 256:5632 + (t4 + 1) * 256], 8, 256)
            wgb, bgb = wload(w_in[l, :, 6656 + t4 * 256:6656 + (t4 + 1) * 256], 8, 256)
            wpa, bpa = wload(w_pa[l, :, t4 * 256:(t4 + 1) * 256], 4, 256)
            wps, bps = wload(w_ps[l, :, t4 * 256:(t4 + 1) * 256], 4, 256)
            for oc in range(2):
                c = t4 * 2 + oc
                for reg in regs:
                    c0, n, kind, ri = reg
                    b1, b2, b3, b4 = nxt("pb", 6), nxt("pb", 6), nxt("pb", 6), nxt("pb", 6)
                    osl = slice(oc * 128, (oc + 1) * 128)
                    k.mms([(ps[:, b1, 0:n], wga[:, kc, osl], hT[:, kc, c0:c0 + n], kc == 0, kc == 7) for kc in range(8)], [bga] + b_hT[ri], [pb[b1]])
                    k.mms([(ps[:, b2, 0:n], wgb[:, kc, osl], hT[:, kc, c0:c0 + n], kc == 0, kc == 7) for kc in range(8)], [bgb] + b_hT[ri], [pb[b2]])
                    k.mms([(ps[:, b3, 0:n], wpa[:, kc, osl], yatt[:, kc, c0:c0 + n], kc == 0, kc == 3) for kc in range(4)], [bpa] + b_yatt, [pb[b3]])
                    k.mms([(ps[:, b4, 0:n], wps[:, kc, osl], ysgu[:, kc, c0:c0 + n], kc == 0, kc == 3) for kc in range(4)], [bps, b_ys[ri]], [pb[b4]])
                    s1 = nxt("stg", NSTG); s2 = nxt("stg", NSTG)
                    k.act(stg[s1][:, 0:n], ps[:, b1, 0:n], AF.Sigmoid, [pb[b1]], [b_stg[s1]])
                    k.act(stg[s2][:, 0:n], ps[:, b2, 0:n], AF.Sigmoid, [pb[b2]], [b_stg[s2]])
                    k.dve_tt(stg[s1][:, 0:n], stg[s1][:, 0:n], ps[:, b3, 0:n], ALU.mult, [b_stg[s1], pb[b3]], [b_stg[s1]])
                    k.dve_tt(stg[s2][:, 0:n], stg[s2][:, 0:n], ps[:, b4, 0:n], ALU.mult, [b_stg[s2], pb[b4]], [b_stg[s2]])
                    k.dve_tt(mg[:, c, c0:c0 + n], stg[s1][:, 0:n], stg[s2][:, 0:n], ALU.add, [b_stg[s1], b_stg[s2]], [b_mg[c][ri]])
        if stop_after == "merge":
            return True
        wo = [wload(w_o[l, :, t4 * 256:(t4 + 1) * 256], 8, 256) for t4 in range(4)]
        xload(0)
        xload(1)
        for t in range(len(subs) + 1):
            if t >= 1:
                reg, sub, ns = subs[t - 1]
                norm_b(t - 1, l, 4, 3, reg, sub, ns, xrs[(t - 1) % 2], b_xrs[(t - 1) % 2])
                if t + 1 < len(subs):
                    xload(t + 1)
            if t < len(subs):
                reg, sub, ns = subs[t]
                c0, n, kind, ri = reg
                xr, b_xr = xrs[t % 2], b_xrs[t % 2]
                for c in range(8):
                    wv_, bw_ = wo[c // 2]
                    bk = nxt("pb", 6)
                    k.mms([(ps[:, bk, 0:ns], wv_[:, kc, (c % 2) * 128:(c % 2 + 1) * 128], mg[:, kc, c0 + sub:c0 + sub + ns], kc == 0, kc == 7) for kc in range(8)],
                          [bw_] + [b_mg[kc][ri] for kc in range(8)], [pb[bk]])
                    modmul(xr[:, c, 0:ns], ps[:, bk, 0:ns], l, 2, c, reg, other=xr[:, c, 0:ns], op1=ALU.add, reads=[pb[bk], b_xr[c]], writes=[b_xr[c]])
                k.dma(STQ, xmD[:, :, c0 + sub:c0 + sub + ns], xr[:, :, 0:ns], b_xr, [b_xmD[ri]], pool="st", npool=6)
                norm_a(t, ns, xr, b_xr)
                norm_a2(t, ns)
        if stop_after == "wout":
            return True
        ada_en[0] = True
        barrier()
        for grp in ([regs[0:2], regs[2:]]):
            hoff = grp[0][0]
            b_hd = [[Buf("hd") for _ in range(5)] for _ in range(22)]
            for f2 in range(11):
                wa, ba_ = wload(w_fi[l, :, f2 * 256:(f2 + 1) * 256], 8, 256)
                wb_, bb_ = wload(w_fi[l, :, 2816 + f2 * 256:2816 + (f2 + 1) * 256], 8, 256)
                for oc in range(2):
                    f = f2 * 2 + oc
                    for reg in grp:
                        c0, n, kind, ri = reg
                        b1, b2 = nxt("pb", 6), nxt("pb", 6)
                        osl = slice(oc * 128, (oc + 1) * 128)
                        k.mms([(ps[:, b1, 0:n], wa[:, kc, osl], hT[:, kc, c0:c0 + n], kc == 0, kc == 7) for kc in range(8)], [ba_] + b_hT[ri], [pb[b1]])
                        k.mms([(ps[:, b2, 0:n], wb_[:, kc, osl], hT[:, kc, c0:c0 + n], kc == 0, kc == 7) for kc in range(8)], [bb_] + b_hT[ri], [pb[b2]])
                        s1 = nxt("stg", NSTG)
                        k.act(stg[s1][:, 0:n], ps[:, b1, 0:n], AF.Silu, [pb[b1]], [b_stg[s1]])
                        k.dve_tt(hid[:, f, c0 - hoff:c0 - hoff + n], stg[s1][:, 0:n], ps[:, b2, 0:n], ALU.mult, [b_stg[s1], pb[b2]], [b_hd[f][ri]])
            cr = [(c, reg) for c in range(8) for reg in grp]

            def xmload(j):
                c, reg = cr[j]
                c0, n, kind, ri = reg
                k.dma(S_, xmc[j % 2][:, 0:n], xmD[:, c, c0:c0 + n], [b_xmD[ri]], [b_xmc[j % 2]], pool="xm", npool=2)
            xmload(0)
            wfo = None
            for j, (c, reg) in enumerate(cr):
                c0, n, kind, ri = reg
                if reg is grp[0]:
                    wfo = [wload(w_fo[l, hf * 1408:(hf + 1) * 1408, c * 128:(c + 1) * 128], 11, 128) for hf in range(2)]
                if j + 1 < len(cr):
                    xmload(j + 1)
                xi = j % 2
                bk = nxt("pb", 6)
                k.mms([(ps[:, bk, 0:n], wfo[kc // 11][0][:, kc % 11, :], hid[:, kc, c0 - hoff:c0 - hoff + n], kc == 0, kc == 21) for kc in range(22)],
                      [wfo[0][1], wfo[1][1]] + [b_hd[kc][ri] for kc in range(22)], [pb[bk]])
                modmul(xmc[xi][:, 0:n], ps[:, bk, 0:n], l, 5, c, reg, other=xmc[xi][:, 0:n], op1=ALU.add, reads=[pb[bk], b_xmc[xi]], writes=[b_xmc[xi]])
                if final:
                    dst = ysT[:, c, 0:4] if kind == "s" else yT[:, c, c0:c0 + n]
                    k.dma(STQ, dst, xmc[xi][:, 0:n], [b_xmc[xi]], [], pool="st", npool=6, out_final=True)
                else:
                    k.dma(STQ, xdst[:, c, c0:c0 + n], xmc[xi][:, 0:n], [b_xmc[xi]], [bxd[ri]], pool="st", npool=6)

    b_accb = k.bufs("acc", 2)
    b_QT = k.bufs("QT", 16); b_KT = k.bufs("KT", 16); b_V = k.bufs("V", 16)
    b_yatt = k.bufs("yatt", 4)
    for l_ in range(2):
        ada_run(0, upto_layer=l_ - 1)
        if l_ == 1:
            ada_run(8)
        if run_pass(l_) or stop_after == "pass%d" % l_:
            break
    k.finish()
    return k


def _fm(a):
    n = a.shape[0]
    return np.ascontiguousarray(a.T.reshape(8, 128, n).transpose(1, 0, 2))


def _consts():
    hh = np.arange(1, 13, dtype=np.float32)
    slopes = np.power(np.float32(2.0), -8.0 * hh / 12).astype(np.float32).reshape(3, 4)
    kk = np.arange(128)[:, None].astype(np.float64)
    qq = np.arange(128)[None, :].astype(np.float64)
    E = np.zeros((128, 12, 2, 128), np.float32)
    AL = np.zeros((128, 12), np.float32)
    for g in range(3):
        for h in range(4):
            s = float(slopes[g, h]) * DILS[g]
            E[:, g * 4 + h, 0, :] = np.where(kk >= qq, np.exp(-s * (128 + qq - kk)), 0.0)
            E[:, g * 4 + h, 1, :] = np.where(kk <= qq, np.exp(-s * (qq - kk)), 0.0)
            AL[:, g * 4 + h] = -s * (128 - np.arange(128))
    tril = (np.arange(128)[None, :] <= np.arange(128)[:, None]).astype(np.float32)
    return dict(ident=np.eye(128).astype(ml_dtypes.bfloat16), ones=np.ones((128, 128), ml_dtypes.bfloat16),
                E=E.astype(ml_dtypes.bfloat16), AL=AL, tril=tril)


_CACHE = {}


def kernel(x_prompt, x_sample, cache_kv_w128, cache_kv_w512, cache_kv_w2048, c_prompt, c_sample,
           w_ada, b_ada, norm1_w, w_in, q_norm_w, k_norm_w, sgu_ln_w, sgu_ln_b, w_spatial, b_spatial,
           w_proj_att, w_proj_sgu, w_out, norm2_w, w_ffn_in, w_ffn_out):
    f = lambda a: np.ascontiguousarray(np.asarray(a, dtype=np.float32))
    x_prompt, x_sample = f(x_prompt), f(x_sample)
    caches = [f(cache_kv_w128), f(cache_kv_w512), f(cache_kv_w2048)]
    c_prompt, c_sample = f(c_prompt), f(c_sample)
    if "k" not in _CACHE:
        _CACHE["k"] = build()
    kk = _CACHE["k"]
    shared = dict(w_ada=f(w_ada), w_in=f(w_in), q_norm_w=f(q_norm_w), k_norm_w=f(k_norm_w), sgu_ln_w=f(sgu_ln_w), sgu_ln_b=f(sgu_ln_b),
                  w_spatial=f(w_spatial), b_spatial=f(b_spatial), w_proj_att=f(w_proj_att), w_proj_sgu=f(w_proj_sgu), w_out=f(w_out),
                  w_ffn_in=f(w_ffn_in), w_ffn_out=f(w_ffn_out))
    shared["b_adaT"] = np.ascontiguousarray(f(b_ada).reshape(2, 48, 128).transpose(2, 0, 1))
    shared["norm1T"] = np.ascontiguousarray(f(norm1_w).reshape(2, 8, 128).transpose(2, 0, 1))
    shared["norm2T"] = np.ascontiguousarray(f(norm2_w).reshape(2, 8, 128).transpose(2, 0, 1))
    shared.update(_consts())
    in_maps = []
    for c in range(8):
        b, r = c // 2, c % 2
        m = dict(shared)
        m["xY"] = _fm(x_prompt[b, r * T:(r + 1) * T])
        m["xS"] = _fm(x_sample[4 * c:4 * c + 4, 0, :])
        cc = np.zeros((8, 1024), np.float32)
        cc[0] = c_prompt[b]
        cc[1:5] = c_sample[4 * c:4 * c + 4]
        m["cT"] = _fm(cc)
        m["hmask"] = np.full((128, 1), float(r), np.float32)
        for g in range(3):
            m["cache%d" % g] = np.ascontiguousarray(caches[g][:, 4 * c:4 * c + 4])
        in_maps.append(m)
    res = run_bass_kernel_spmd(kk.nc, in_maps, core_ids=list(range(8)))
    R_ = res.results
    y_prompt = np.zeros((4, 4096, 1024), np.float32)
    y_sample = np.zeros((32, 1, 1024), np.float32)
    kvp = [np.zeros((2, 4, WINS[g], 2, 4, 128), np.float32) for g in range(3)]
    kvs = [np.zeros((2, 32, 1, 2, 4, 128), np.float32) for g in range(3)]
    sguv = np.zeros((2, 32, 1, 512), np.float32)
    for c in range(8):
        b, r = c // 2, c % 2
        o = R_[c]
        y_prompt[b, r * T:(r + 1) * T] = np.asarray(o["yT"]).transpose(1, 0, 2).reshape(1024, T).T
        y_sample[4 * c:4 * c + 4, 0] = np.asarray(o["ysT"]).transpose(1, 0, 2).reshape(1024, 4).T
        for g in range(3):
            if r == 1:
                kvp[g][:, b] = np.asarray(o["kvp%d" % g])
            kvs[g][:, 4 * c:4 * c + 4, 0] = np.asarray(o["kvs%d" % g])
        sguv[:, 4 * c:4 * c + 4, 0] = np.asarray(o["sguv"])
    return (y_prompt, y_sample, kvp[0], kvp[1], kvp[2], kvs[0], kvs[1], kvs[2], sguv)
```
